# Optimizing a Trainium2 kernel written in Bass

```python
import jax, jax.numpy as jnp
from jax import lax
import numpy as np

D_MODEL = 1024
BATCH = 8
SEQ = 2048
DEPTH = 2

GRID_W = 64
WIN_H = 8
WIN_W = 16
NA_HEADS = 8
NA_HEAD_DIM = 64
NA_WIDTH = NA_HEADS * NA_HEAD_DIM
RET_HEADS = 4
RET_HEAD_DIM = 128
RET_WIDTH = RET_HEADS * RET_HEAD_DIM
RET_CHUNK = 128
RET_THETA_BASE = 10000.0
LRU_WIDTH = 512
LRU_BLOCKS = 8
LRU_BLOCK_DIM = LRU_WIDTH // LRU_BLOCKS
LRU_C = 8.0
CONV_W = 4
CONV_PAD_LEFT = 2
MIN_RAD = 0.9
MAX_RAD = 0.999
N_BRANCH = 3
BRANCH_WIDTH = 512
D_FF = 4 * D_MODEL
PLE_DIM = 256
RMS_EPS = 1e-6
OFF_RET = 3 * NA_WIDTH
OFF_LRU = OFF_RET + 4 * RET_WIDTH
OFF_GATE = OFF_LRU + 2 * LRU_WIDTH
IN_COLS = OFF_GATE + N_BRANCH * D_MODEL

kernel_name = "hybrid_natten_retnet_rglru_encoder"


def rmsnorm(x, g):
    xf = x.astype(jnp.float32)
    y = xf * lax.rsqrt(jnp.mean(xf * xf, axis=-1, keepdims=True) + RMS_EPS)
    return (y * g.astype(jnp.float32)).astype(x.dtype)


def neighborhood_attention(q, k, v, rpb):
    b, s, h, dh = q.shape
    rows = s // GRID_W
    kh = min(WIN_H, rows)
    qg = q.reshape(b, rows, GRID_W, h, dh) * (dh ** -0.5)
    kg = k.reshape(b, rows, GRID_W, h, dh)
    vg = v.reshape(b, rows, GRID_W, h, dh)
    col = jnp.arange(GRID_W)
    col_start = jnp.clip(col - WIN_W // 2, 0, GRID_W - WIN_W)
    col_idx = col_start[:, None] + jnp.arange(WIN_W)[None, :]
    col_bias_idx = col_idx - col[:, None] + (WIN_W - 1)

    def row_block(r):
        r_start = jnp.clip(r - kh // 2, 0, rows - kh)
        k_rows = lax.dynamic_slice_in_dim(kg, r_start, kh, axis=1)
        v_rows = lax.dynamic_slice_in_dim(vg, r_start, kh, axis=1)
        k_win = k_rows[:, :, col_idx]
        v_win = v_rows[:, :, col_idx]
        q_row = lax.dynamic_index_in_dim(qg, r, axis=1, keepdims=False)
        row_bias_idx = r_start + jnp.arange(kh) - r + (WIN_H - 1)
        bias = rpb[:, row_bias_idx[None, :, None], col_bias_idx[:, None, :]]
        scores = jnp.einsum('bchd,bicjhd->bhcij', q_row, k_win).astype(jnp.float32)
        scores = scores + bias.astype(jnp.float32)
        probs = jax.nn.softmax(scores.reshape(b, h, GRID_W, kh * WIN_W), axis=-1)
        probs = probs.reshape(b, h, GRID_W, kh, WIN_W).astype(v.dtype)
        return jnp.einsum('bhcij,bicjhd->bchd', probs, v_win)

    out = lax.map(row_block, jnp.arange(rows))
    return out.transpose(1, 0, 2, 3, 4).reshape(b, s, h * dh)


def rotate_pairs(t, cos, sin):
    t1 = t[..., 0::2]
    t2 = t[..., 1::2]
    return jnp.stack([t1 * cos - t2 * sin, t1 * sin + t2 * cos], axis=-1).reshape(t.shape)


def chunkwise_retention(q, k, v, log_gamma, include_diag):
    b, s, h, dk = q.shape
    dv = v.shape[-1]
    n_chunks = s // RET_CHUNK
    idx = jnp.arange(RET_CHUNK, dtype=jnp.float32)
    diff = idx[:, None] - idx[None, :]
    valid = (diff >= 0) if include_diag else (diff > 0)
    intra = jnp.where(valid[None], jnp.exp(log_gamma[:, None, None] * jnp.maximum(diff, 0.0)[None]), 0.0)
    q_dec = jnp.exp(log_gamma[:, None] * (idx + 1.0)[None])[:, :, None]
    k_dec = jnp.exp(log_gamma[:, None] * (RET_CHUNK - 1.0 - idx)[None])[:, :, None]
    c_dec = jnp.exp(log_gamma * RET_CHUNK)[:, None, None]

    def to_chunks(t):
        return t.reshape(b, n_chunks, RET_CHUNK, h, t.shape[-1]).transpose(1, 0, 3, 2, 4)

    def step(state, inp):
        qi, ki, vi = inp
        scores = jnp.einsum('bhid,bhjd->bhij', qi, ki) * intra
        out = jnp.einsum('bhij,bhjv->bhiv', scores, vi) + jnp.einsum('bhid,bhdv->bhiv', qi * q_dec, state)
        state = state * c_dec + jnp.einsum('bhjd,bhjv->bhdv', ki * k_dec, vi)
        return state, out

    state0 = jnp.zeros((b, h, dk, dv), jnp.float32)
    _, out = lax.scan(step, state0, (to_chunks(q), to_chunks(k), to_chunks(v)))
    return out.transpose(1, 0, 3, 2, 4).reshape(b, s, h, dv)


def retention_branch(q, k, v, g, gn):
    b, s, _ = q.shape
    f = jnp.float32
    q = q.astype(f).reshape(b, s, RET_HEADS, RET_HEAD_DIM)
    k = k.astype(f).reshape(b, s, RET_HEADS, RET_HEAD_DIM)
    v = v.astype(f).reshape(b, s, RET_HEADS, RET_HEAD_DIM)
    pos = jnp.arange(s, dtype=f)
    theta = 1.0 / (RET_THETA_BASE ** jnp.linspace(0.0, 1.0, RET_HEAD_DIM // 2, dtype=f))
    ang = pos[:, None] * theta[None, :]
    cos = jnp.cos(ang)[None, :, None, :]
    sin = jnp.sin(ang)[None, :, None, :]
    q = rotate_pairs(q, cos, sin) * (RET_HEAD_DIM ** -0.5)
    k = rotate_pairs(k, cos, sin)
    hidx = jnp.arange(RET_HEADS, dtype=f)
    log_gamma_fwd = jnp.log1p(-jnp.exp2(-5.0 - hidx))
    log_gamma_bwd = jnp.log1p(-jnp.exp2(-5.5 - hidx))
    y_fwd = chunkwise_retention(q, k, v, log_gamma_fwd, True)
    y_bwd = jnp.flip(chunkwise_retention(jnp.flip(q, 1), jnp.flip(k, 1), jnp.flip(v, 1), log_gamma_bwd, False), 1)
    y = y_fwd + y_bwd
    y = y * lax.rsqrt(jnp.mean(y * y, axis=-1, keepdims=True) + RMS_EPS)
    y = y.reshape(b, s, RET_WIDTH) * gn.astype(f)
    return jax.nn.silu(g.astype(f)) * y


def block_diag_linear(t, w, bias):
    tb = t.reshape(t.shape[0], t.shape[1], LRU_BLOCKS, LRU_BLOCK_DIM)
    return jnp.einsum('bsni,nij->bsnj', tb, w).reshape(t.shape) + bias


def linear_recurrence_combine(c1, c2):
    a1, b1 = c1
    a2, b2 = c2
    return a1 * a2, a2 * b1 + b2


def rg_lru(xs, w_a, b_a, w_x, b_x, lam, reverse):
    f = jnp.float32
    s = xs.shape[1]
    r = jax.nn.sigmoid(block_diag_linear(xs, w_a.astype(f), b_a.astype(f)))
    i = jax.nn.sigmoid(block_diag_linear(xs, w_x.astype(f), b_x.astype(f)))
    log_a = -LRU_C * r * jax.nn.softplus(-lam.astype(f))
    a = jnp.exp(log_a)
    mult = jnp.sqrt(-jnp.expm1(2.0 * log_a))
    first = s - 1 if reverse else 0
    mult = jnp.where((jnp.arange(s) == first)[None, :, None], 1.0, mult)
    _, hs = lax.associative_scan(linear_recurrence_combine, (a, mult * i * xs), axis=1, reverse=reverse)
    return hs


def rglru_branch(xc, gc, conv_w, conv_b, wa, ba, wx, bx, lam):
    f = jnp.float32
    xf = xc.astype(f)
    xf = lax.conv_general_dilated(
        xf, conv_w.astype(f)[:, None, :], window_strides=(1,),
        padding=[(CONV_PAD_LEFT, CONV_W - 1 - CONV_PAD_LEFT)],
        dimension_numbers=('NWC', 'WIO', 'NWC'), feature_group_count=LRU_WIDTH) + conv_b.astype(f)
    y = rg_lru(xf, wa[0], ba[0], wx[0], bx[0], lam[0], False) + rg_lru(xf, wa[1], ba[1], wx[1], bx[1], lam[1], True)
    return jax.nn.gelu(gc.astype(f)) * y


def hybrid_layer(x, p_i, g_mix, w_in, b_gate, na_rpb, ret_gn, conv_w, conv_b,
                 lru_wa, lru_ba, lru_wx, lru_bx, lru_lambda, w_branch, w_out,
                 g_mlp, w_up, w_down, g_ple, w_ple_gate, w_ple):
    b, s, _ = x.shape
    h = rmsnorm(x, g_mix)
    proj = jnp.einsum('bsd,dc->bsc', h, w_in)
    qa, ka, va = jnp.split(proj[..., :OFF_RET], 3, axis=-1)
    qr, kr, vr, gr = jnp.split(proj[..., OFF_RET:OFF_LRU], 4, axis=-1)
    xc, gc = jnp.split(proj[..., OFF_LRU:OFF_GATE], 2, axis=-1)
    gate_logits = proj[..., OFF_GATE:].reshape(b, s, N_BRANCH, D_MODEL)

    def heads(t):
        return t.reshape(b, s, NA_HEADS, NA_HEAD_DIM)

    y_na = neighborhood_attention(heads(qa), heads(ka), heads(va), na_rpb)
    y_ret = retention_branch(qr, kr, vr, gr, ret_gn)
    y_lru = rglru_branch(xc, gc, conv_w, conv_b, lru_wa, lru_ba, lru_wx, lru_bx, lru_lambda)
    ys = jnp.stack([y_na.astype(h.dtype), y_ret.astype(h.dtype), y_lru.astype(h.dtype)], axis=2)
    branch = jnp.einsum('bsnw,nwd->bsnd', ys, w_branch).astype(jnp.float32)
    gates = jax.nn.sigmoid((gate_logits + b_gate).astype(jnp.float32))
    merged = jnp.sum(gates * branch, axis=2).astype(x.dtype)
    x = x + merged @ w_out

    h = rmsnorm(x, g_mlp)
    x = x + jnp.square(jax.nn.relu(h @ w_up)) @ w_down

    h = rmsnorm(x, g_ple)
    x = x + jax.nn.sigmoid(h @ w_ple_gate) * (p_i @ w_ple)
    return x


def setup_inputs(seed: int = 0) -> dict:
    key = jax.random.key(seed)
    ks = jax.random.split(key, 24)
    f = jnp.float32

    def nrm(k, shape, scale):
        return jax.random.normal(k, shape, f) * scale

    u = jax.random.uniform(ks[13], (DEPTH, 2, LRU_WIDTH), f, minval=MIN_RAD, maxval=MAX_RAD)
    sig = u ** (1.0 / LRU_C)
    lam = jnp.log(sig) - jnp.log1p(-sig)
    return {
        'x': nrm(ks[0], (BATCH, SEQ, D_MODEL), 1.0),
        'p': nrm(ks[1], (DEPTH, BATCH, SEQ, PLE_DIM), 1.0),
        'g_mix': 1.0 + nrm(ks[2], (DEPTH, D_MODEL), 0.02),
        'w_in': nrm(ks[3], (DEPTH, D_MODEL, IN_COLS), D_MODEL ** -0.5),
        'b_gate': nrm(ks[4], (DEPTH, N_BRANCH, D_MODEL), 0.02),
        'na_rpb': nrm(ks[5], (DEPTH, NA_HEADS, 2 * WIN_H - 1, 2 * WIN_W - 1), 0.02),
        'ret_gn': 1.0 + nrm(ks[6], (DEPTH, RET_WIDTH), 0.02),
        'conv_w': nrm(ks[7], (DEPTH, CONV_W, LRU_WIDTH), CONV_W ** -0.5),
        'conv_b': nrm(ks[8], (DEPTH, LRU_WIDTH), 0.02),
        'lru_wa': nrm(ks[9], (DEPTH, 2, LRU_BLOCKS, LRU_BLOCK_DIM, LRU_BLOCK_DIM), LRU_BLOCK_DIM ** -0.5),
        'lru_ba': nrm(ks[10], (DEPTH, 2, LRU_WIDTH), 0.02),
        'lru_wx': nrm(ks[11], (DEPTH, 2, LRU_BLOCKS, LRU_BLOCK_DIM, LRU_BLOCK_DIM), LRU_BLOCK_DIM ** -0.5),
        'lru_bx': nrm(ks[12], (DEPTH, 2, LRU_WIDTH), 0.02),
        'lru_lambda': lam,
        'w_branch': nrm(ks[14], (DEPTH, N_BRANCH, BRANCH_WIDTH, D_MODEL), BRANCH_WIDTH ** -0.5),
        'w_out': nrm(ks[15], (DEPTH, D_MODEL, D_MODEL), D_MODEL ** -0.5),
        'g_mlp': 1.0 + nrm(ks[16], (DEPTH, D_MODEL), 0.02),
        'w_up': nrm(ks[17], (DEPTH, D_MODEL, D_FF), D_MODEL ** -0.5),
        'w_down': nrm(ks[18], (DEPTH, D_FF, D_MODEL), D_FF ** -0.5),
        'g_ple': 1.0 + nrm(ks[19], (DEPTH, D_MODEL), 0.02),
        'w_ple_gate': nrm(ks[20], (DEPTH, D_MODEL, D_MODEL), D_MODEL ** -0.5),
        'w_ple': nrm(ks[21], (DEPTH, PLE_DIM, D_MODEL), PLE_DIM ** -0.5),
        'g_final': 1.0 + nrm(ks[22], (D_MODEL,), 0.02),
    }


def reference(x, p, g_mix, w_in, b_gate, na_rpb, ret_gn, conv_w, conv_b,
              lru_wa, lru_ba, lru_wx, lru_bx, lru_lambda, w_branch, w_out,
              g_mlp, w_up, w_down, g_ple, w_ple_gate, w_ple, g_final):
    for i in range(DEPTH):
        x = hybrid_layer(x, p[i], g_mix[i], w_in[i], b_gate[i], na_rpb[i], ret_gn[i],
                         conv_w[i], conv_b[i], lru_wa[i], lru_ba[i], lru_wx[i], lru_bx[i],
                         lru_lambda[i], w_branch[i], w_out[i], g_mlp[i], w_up[i], w_down[i],
                         g_ple[i], w_ple_gate[i], w_ple[i])
    return rmsnorm(x, g_final)
```

```python
import contextlib
import numpy as np
import concourse.bass as bass
import concourse.mybir as mybir
from concourse.bass_utils import run_bass_kernel_spmd

F32 = mybir.dt.float32
BF16 = mybir.dt.bfloat16
AF = mybir.ActivationFunctionType
ALU = mybir.AluOpType

D = 1024
S = 2048
NT = 16
DEPTH = 2
EPS = 1e-6
NEG = -30000.0
WTILE = 4096
NSLOT = 3

OFF_RET = 1536
OFF_LRU = 3584
OFF_GATE = 4608
NCOLS_SMALL = 68


def _kmajor(W):
    K, N = W.shape
    return np.ascontiguousarray(W.reshape(K // 128, 128, N).transpose(1, 0, 2).reshape(128, -1))


def _ret_perm():
    idx = []
    for h in range(4):
        idx += [h * 128 + 2 * m for m in range(64)]
        idx += [h * 128 + 2 * m + 1 for m in range(64)]
    return np.array(idx)


def _layer_tiles(l, inp):
    w_in = inp['w_in'][l]
    tiles = []
    for ps in range(2):
        q = w_in[:, ps * 256:(ps + 1) * 256]
        k = w_in[:, 512 + ps * 256:512 + (ps + 1) * 256]
        tiles.append(_kmajor(np.concatenate([q, k], 1)))
        tiles.append(_kmajor(w_in[:, 1024 + ps * 256:1024 + (ps + 1) * 256]))
    perm = _ret_perm()
    qr = w_in[:, OFF_RET:OFF_RET + 512]
    kr = w_in[:, OFF_RET + 512:OFF_RET + 1024]
    vr = w_in[:, OFF_RET + 1024:OFF_RET + 1536]
    gr = w_in[:, OFF_RET + 1536:OFF_RET + 2048]
    tiles += [_kmajor(kr[:, perm]), _kmajor(vr), _kmajor(qr[:, perm]), _kmajor(gr)]
    xc = w_in[:, OFF_LRU:OFF_LRU + 512]
    gc = w_in[:, OFF_LRU + 512:OFF_LRU + 1024]
    for ch in range(4):
        a = _kmajor(np.concatenate([xc[:, ch * 128:(ch + 1) * 128], gc[:, ch * 128:(ch + 1) * 128]], 1))
        mats = np.zeros((128, 4, 128), np.float32)
        for mi, (nm, d) in enumerate([('lru_wa', 0), ('lru_wx', 0), ('lru_wa', 1), ('lru_wx', 1)]):
            w = inp[nm][l][d]
            mats[0:64, mi, 0:64] = w[2 * ch]
            mats[64:128, mi, 64:128] = w[2 * ch + 1]
        tiles.append(np.concatenate([a, mats.reshape(128, 512)], 1))
    for n in range(3):
        tiles.append(_kmajor(inp['w_branch'][l][n]))
        for half in range(2):
            c0 = OFF_GATE + n * 1024 + half * 512
            tiles.append(_kmajor(w_in[:, c0:c0 + 512]))
    for half in range(2):
        tiles.append(_kmajor(inp['w_out'][l][:, half * 512:(half + 1) * 512]))
    for s in range(4):
        for j in range(2):
            tiles.append(_kmajor(inp['w_up'][l][:, (2 * s + j) * 512:(2 * s + j + 1) * 512]))
        for j in range(2):
            tiles.append(_kmajor(inp['w_down'][l][(2 * s + j) * 512:(2 * s + j + 1) * 512, :]))
    tiles.append(_kmajor(inp['w_ple'][l]))
    for half in range(2):
        tiles.append(_kmajor(inp['w_ple_gate'][l][:, half * 512:(half + 1) * 512]))
    return tiles


def _tile_sizes():
    sz = []
    for ps in range(2):
        sz += [4096, 2048]
    sz += [4096] * 4
    sz += [2560] * 4
    for n in range(3):
        sz += [4096, 4096, 4096]
    sz += [4096, 4096]
    for s in range(4):
        sz += [4096] * 4
    sz += [2048, 4096, 4096]
    return sz


def _na_bias_table(rpb):
    ext = np.concatenate([rpb.reshape(8, 15 * 31), np.full((8, 1), NEG, np.float32)], 1)
    p = np.arange(128)
    b = p // 64
    kc = p % 64
    d = np.arange(14)
    c = np.arange(64)
    cs = np.clip(c - 8, 0, 48)
    row = d[None, :, None] + b[:, None, None]
    col = kc[:, None, None] - c[None, None, :] + 15
    valid = (kc[:, None, None] >= cs[None, None, :]) & (kc[:, None, None] < cs[None, None, :] + 16)
    flat = np.where(valid, row * 31 + np.clip(col, 0, 30), 15 * 31)
    out = ext[:, flat]
    return np.ascontiguousarray(out.transpose(1, 0, 2, 3)).astype(np.float32)


def _consts():
    f8 = np.float64
    c = {}
    c['ident'] = np.eye(128, dtype=np.float32)
    pos = np.arange(S, dtype=np.float32)
    theta = (1.0 / (np.float32(10000.0) ** np.linspace(0.0, 1.0, 64, dtype=np.float32))).astype(np.float32)
    ang = (pos[:, None] * theta[None, :]).astype(np.float32)
    cos = np.cos(ang.astype(f8)).astype(np.float32).reshape(16, 128, 64).transpose(1, 0, 2)
    sin = np.sin(ang.astype(f8)).astype(np.float32).reshape(16, 128, 64).transpose(1, 0, 2)
    c['cos'] = np.ascontiguousarray(cos)
    c['sin'] = np.ascontiguousarray(sin)
    hidx = np.arange(4, dtype=f8)
    lgf = np.log1p(-np.exp2(-5.0 - hidx))
    lgb = np.log1p(-np.exp2(-5.5 - hidx))
    i = np.arange(128, dtype=f8)
    scale = 128.0 ** -0.5
    diff = i[None, :] - i[:, None]
    DT = np.zeros((128, 4, 128), f8)
    for h in range(4):
        DT[:, h, :] = np.where(diff >= 0, np.exp(lgf[h] * np.maximum(diff, 0)),
                               np.exp(lgb[h] * np.maximum(-diff, 0))) * scale
    c['DT'] = DT.astype(np.float32)
    QF = np.zeros((128, 4, 128), f8)
    QB = np.zeros((128, 4, 128), f8)
    for h in range(4):
        QF[:, h, :] = (np.exp(lgf[h] * (i + 1.0)) * scale)[None, :]
        QB[:, h, :] = (np.exp(lgb[h] * (128.0 - i)) * scale)[None, :]
    c['QF'] = QF.astype(np.float32)
    c['QB'] = QB.astype(np.float32)
    KD = np.zeros((128, 2, 4), f8)
    for h in range(4):
        KD[:, 0, h] = np.exp(lgf[h] * (127.0 - i))
        KD[:, 1, h] = np.exp(lgb[h] * i)
    c['KD'] = KD.astype(np.float32)
    cdf = [float(np.exp(lgf[h] * 128.0)) for h in range(4)]
    cdb = [float(np.exp(lgb[h] * 128.0)) for h in range(4)]
    return c, cdf, cdb


_CONSTS, _CDF, _CDB = _consts()


def _small_cols(l, inp):
    out = np.zeros((128, NCOLS_SMALL), np.float32)
    out[:, 0:24] = inp['b_gate'][l].reshape(3, 8, 128).transpose(2, 0, 1).reshape(128, 24)
    out[:, 24:40] = inp['conv_w'][l].reshape(4, 4, 128).transpose(2, 1, 0).reshape(128, 16)
    out[:, 40:44] = inp['conv_b'][l].reshape(4, 128).T
    out[:, 44:52] = inp['lru_ba'][l].reshape(2, 4, 128).transpose(2, 0, 1).reshape(128, 8)
    out[:, 52:60] = inp['lru_bx'][l].reshape(2, 4, 128).transpose(2, 0, 1).reshape(128, 8)
    out[:, 60:68] = inp['lru_lambda'][l].reshape(2, 4, 128).transpose(2, 0, 1).reshape(128, 8)
    return out


class Res:
    __slots__ = ('w', 'r', 'name')

    def __init__(self, name=''):
        self.w = None
        self.r = {}
        self.name = name


class Trk:
    def __init__(self, nc, stack):
        self.nc = nc
        self.stack = stack
        self.eng = {'pe': nc.tensor, 'act': nc.scalar, 'dve': nc.vector, 'pool': nc.gpsimd, 'sp': nc.sync}
        self.sem = {}
        self.cnt = {}
        for e in ['pe', 'act', 'dve', 'pool']:
            self.sem[e] = stack.enter_context(nc.semaphore('s_' + e))
            self.cnt[e] = 0
        self.dsem = {}
        self.dcnt = {}
        self.waited = {e: {} for e in self.eng}

    def _dsem(self, key):
        if key not in self.dsem:
            self.dsem[key] = self.stack.enter_context(self.nc.semaphore('d_%d' % len(self.dsem)))
            self.dcnt[key] = 0
        return self.dsem[key]

    def _wait(self, e, deps, raw=()):
        need = {}
        for d, is_raw in [(d, False) for d in deps] + [(d, True) for d in raw]:
            if d is None:
                continue
            kind, key, v = d
            if kind == 'e' and key == e and e == 'pe':
                continue
            k = (kind, key)
            if self.waited[e].get(k, 0) >= v:
                continue
            if need.get(k, 0) < v:
                need[k] = v
        for (kind, key), v in need.items():
            sem = self.sem[key] if kind == 'e' else self.dsem[key]
            self.eng[e].wait_ge(sem, v)
            self.waited[e][(kind, key)] = v

    def _deps(self, reads, writes):
        deps = []
        for w in writes:
            deps.append(w.w)
            for (kind, key), v in w.r.items():
                deps.append((kind, key, v))
        return deps

    def _commit(self, tok, reads, writes):
        k = (tok[0], tok[1])
        for r in reads:
            r.r[k] = tok[2]
        for w in writes:
            w.w = tok
            w.r = {}

    def op(self, e, fn, reads=(), writes=()):
        self._wait(e, self._deps(reads, writes), raw=[r.w for r in reads])
        inst = fn()
        self.cnt[e] += 1
        inst.then_inc(self.sem[e], 1)
        self._commit(('e', e, self.cnt[e]), reads, writes)

    def dma(self, q, key, out, in_, reads=(), writes=()):
        self._wait(q, self._deps(reads, writes), raw=[r.w for r in reads])
        sem = self._dsem(key)
        inst = self.eng[q].dma_start(out=out, in_=in_)
        self.dcnt[key] += 16
        inst.then_inc(sem, 16)
        self._commit(('d', key, self.dcnt[key]), reads, writes)

    def barrier(self, engines=('pe', 'act', 'dve', 'pool', 'sp')):
        for e in engines:
            deps = [('e', f, self.cnt[f]) for f in self.cnt if f != e and self.cnt[f] > 0]
            deps += [('d', k, v) for k, v in self.dcnt.items() if v > 0]
            self._wait(e, deps)

    def wait_all(self, e):
        deps = [('e', f, self.cnt[f]) for f in self.cnt if f != e and self.cnt[f] > 0]
        deps += [('d', k, v) for k, v in self.dcnt.items() if v > 0]
        self._wait(e, deps)


def build(depth=DEPTH, stop=None, dbg=False):
    nc = bass.Bass("TRN2", target_bir_lowering=False)
    sizes = _tile_sizes()
    TOT = sum(sizes)
    offs = np.concatenate([[0], np.cumsum(sizes)]).astype(int)
    NTILE = len(sizes)

    x_in = nc.dram_tensor("x", [S, D], F32, kind="ExternalInput").ap()
    pT_in = nc.dram_tensor("pT", [DEPTH, 256, S], F32, kind="ExternalInput").ap()
    ws_in = nc.dram_tensor("ws", [DEPTH, 128, TOT], F32, kind="ExternalInput").ap()
    cols_in = nc.dram_tensor("cols", [128, DEPTH * NCOLS_SMALL], F32, kind="ExternalInput").ap()
    gvec_in = nc.dram_tensor("gvec", [DEPTH * 3 + 1, D], F32, kind="ExternalInput").ap()
    gn_in = nc.dram_tensor("gn", [DEPTH, 512], F32, kind="ExternalInput").ap()
    bt_in = nc.dram_tensor("bt", [DEPTH, 128, 8 * 14 * 64], F32, kind="ExternalInput").ap()
    ident_in = nc.dram_tensor("ident", [128, 128], F32, kind="ExternalInput").ap()
    cos_in = nc.dram_tensor("cos", [128, 16 * 64], F32, kind="ExternalInput").ap()
    sin_in = nc.dram_tensor("sin", [128, 16 * 64], F32, kind="ExternalInput").ap()
    dt_in = nc.dram_tensor("DT", [128, 512], F32, kind="ExternalInput").ap()
    qf_in = nc.dram_tensor("QF", [128, 512], F32, kind="ExternalInput").ap()
    qb_in = nc.dram_tensor("QB", [128, 512], F32, kind="ExternalInput").ap()
    kd_in = nc.dram_tensor("KD", [128, 8], F32, kind="ExternalInput").ap()
    y_out = nc.dram_tensor("y", [S, D], F32, kind="ExternalOutput").ap()
    xd = nc.dram_tensor("xd", [S, D], F32).ap()
    dbg_out = {}
    if dbg:
        for nm in ['hT', 'ynaT', 'yretT', 'ylruT']:
            dbg_out[nm] = nc.dram_tensor("dbg_" + nm, [128, 8 * S if nm == 'hT' else 4 * S], BF16,
                                         kind="ExternalOutput").ap()
        for nm in ['xE', 'xF', 'xG']:
            dbg_out[nm] = nc.dram_tensor("dbg_" + nm, [S, D], F32, kind="ExternalOutput").ap()
        dbg_out['qT'] = nc.dram_tensor("dbg_qT", [128, 2 * S], BF16, kind="ExternalOutput").ap()
        dbg_out['kT'] = nc.dram_tensor("dbg_kT", [128, 2 * S], BF16, kind="ExternalOutput").ap()
        dbg_out['Ve'] = nc.dram_tensor("dbg_Ve", [128, 16 * 4 * 65], BF16, kind="ExternalOutput").ap()
        dbg_out['Vo'] = nc.dram_tensor("dbg_Vo", [128, 16 * 4 * 65], BF16, kind="ExternalOutput").ap()
        dbg_out['bt'] = nc.dram_tensor("dbg_bt", [128, 8 * 14 * 64], BF16, kind="ExternalOutput").ap()
        for nm in ['krot', 'vtm', 'Sb']:
            dbg_out[nm] = nc.dram_tensor("dbg_" + nm, [128, 16 * 512], BF16, kind="ExternalOutput").ap()
        for nm in ['qTc', 'kTc', 'pTr', 'qfT', 'qbT', 'yrt']:
            dbg_out[nm] = nc.dram_tensor("dbg_" + nm, [128, 512], BF16, kind="ExternalOutput").ap()
        for nm in ['sge', 'o1']:
            dbg_out[nm] = nc.dram_tensor("dbg_" + nm, [128, 512], F32, kind="ExternalOutput").ap()

    stack = contextlib.ExitStack()
    with stack:
        T = Trk(nc, stack)

        sb_state = {'off': (int(nc.sbuf_base) + 63) // 64 * 64, 'n': 0}
        sb_addr = {}

        def sb(name, shape, dt):
            nbytes = int(np.prod(shape[1:])) * (2 if dt == BF16 else 4)
            nbytes = (nbytes + 63) // 64 * 64
            addr = sb_state['off']
            sb_state['off'] += nbytes
            assert sb_state['off'] <= int(nc.sbuf_top), (name, sb_state['off'], int(nc.sbuf_top))
            h = nc.alloc_sbuf_tensor_at(name, shape, dt, offset=addr)
            sb_addr[name] = addr
            return h

        ARENA_BYTES = 86 * 1024
        arena = sb("arena", [128, ARENA_BYTES // 4], F32)
        ARENA_ADDR = sb_addr["arena"]
        hT = sb("hT", [128, 8, S], BF16)
        ysT = [sb("ysT%d" % i, [128, 4, S], BF16) for i in range(3)]
        YS_ADDR = sb_addr["ysT0"]
        wbuf = [sb("wbuf%d" % i, [128, WTILE], BF16) for i in range(NSLOT)]
        ident = sb("ident", [128, 128], BF16)
        colp = sb("colp", [128, DEPTH * NCOLS_SMALL], F32)
        gbc = sb("gbc", [128, D], F32)
        ssq = sb("ssq", [128, 16], F32)
        rstd = sb("rstd", [128, 16], F32)
        hn = [sb("hn%d" % i, [128, D], BF16) for i in range(2)]
        junk = sb("junk", [128, D], BF16)
        kdt = sb("kdt", [128, 8], F32)
        sqf = [sb("sqf%d" % i, [128, 512], F32) for i in range(2)]
        lru_c = sb("lru_c", [128, DEPTH * 8], F32)
        lru_2c = sb("lru_2c", [128, DEPTH * 8], F32)

        class Carver:
            def __init__(self):
                self.off = 0
                self.n = 0

            def reset(self):
                self.off = 0

            def get(self, name, shape, dt):
                nbytes = int(np.prod(shape[1:])) * (2 if dt == BF16 else 4)
                nbytes = (nbytes + 63) // 64 * 64
                assert self.off + nbytes <= ARENA_BYTES, (name, self.off, nbytes)
                self.n += 1
                h = nc.alloc_sbuf_tensor_at("%s_%d" % (name, self.n), shape, dt, offset=ARENA_ADDR + self.off)
                self.off += nbytes
                return h

        CV = Carver()

        X = nc.alloc_sbuf_tensor_at("Xres", [128, NT, D], F32, offset=ARENA_ADDR)
        R_X = [Res('X%d' % t) for t in range(NT)]
        R_hT = [Res('hT%d' % t) for t in range(NT)]
        R_ys = [[Res('ys%d_%d' % (i, t)) for t in range(NT)] for i in range(3)]
        R_xd = [Res('xd%d' % t) for t in range(NT)]
        R_misc = Res('misc')
        R_junk = Res('junk')
        R_gG = Res('gbc')
        R_ssG = Res('ssq')
        R_hnG = [Res('hn0'), Res('hn1')]

        pbank = [stack.enter_context(nc.psum_tensor("pb%d" % i, [128, 512], F32)) for i in range(6)]
        tbank = [stack.enter_context(nc.psum_tensor("tb%d" % i, [128, 1024], BF16)) for i in range(2)]
        R_pb = [Res('pb%d' % i) for i in range(6)]
        R_tb = [Res('tb%d' % i) for i in range(2)]
        pctr = [0, 0]

        def ps_next():
            i = pctr[0] % 6
            pctr[0] += 1
            return pbank[i], R_pb[i]

        def tb_next():
            i = pctr[1] % 2
            pctr[1] += 1
            return tbank[i], R_tb[i]

        R_w = [Res('w%d' % i) for i in range(NSLOT)]
        wstate = {'issued': 0, 'taken': 0}
        total_tiles = depth * NTILE

        wdone = set()

        def w_issue_ready():
            while wstate['issued'] < total_tiles and wstate['issued'] < wstate['released'] + NSLOT:
                g = wstate['issued']
                l, i = divmod(g, NTILE)
                slot = g % NSLOT
                n = sizes[i]
                bsz = 2048 if n % 2048 == 0 else 512
                T.dma('pool', ('w', slot), wbuf[slot][:, 0:n].rearrange("p (a b) -> p a b", b=bsz),
                      ws_in[l, :, int(offs[i]):int(offs[i]) + n].rearrange("p (a b) -> p a b", b=bsz),
                      writes=[R_w[slot]])
                wstate['issued'] += 1

        def w_next(expect_size):
            g = wstate['taken']
            l, i = divmod(g, NTILE)
            assert sizes[i] == expect_size, (g, i, sizes[i], expect_size)
            assert g < wstate['issued'], "weight tile not issued (too many held)"
            slot = g % NSLOT
            wstate['taken'] += 1
            wheld.append(g)
            return wbuf[slot], R_w[slot]

        def w_done(k=1):
            for _ in range(k):
                g = wheld.pop(0)
                wdone.add(g)
            while wstate['released'] in wdone:
                wdone.discard(wstate['released'])
                wstate['released'] += 1
            w_issue_ready()

        def w_done_last():
            g = wheld.pop()
            wdone.add(g)
            while wstate['released'] in wdone:
                wdone.discard(wstate['released'])
                wstate['released'] += 1
            w_issue_ready()

        wheld = []
        wstate['released'] = 0

        T.dma('pool', 'c_id', ident[:], ident_in, writes=[R_misc])
        T.dma('sp', 'c_col', colp[:], cols_in, writes=[R_misc])
        T.dma('sp', 'c_kd', kdt[:], kd_in, writes=[R_misc])
        w_issue_ready()
        for t in range(NT):
            T.dma('sp', ('x', t % 4), X[:, t, :], x_in[t * 128:(t + 1) * 128, :], writes=[R_X[t]])

        R_lc = Res('lruc')
        for l in range(depth):
            src = colp[:, l * NCOLS_SMALL + 60:l * NCOLS_SMALL + 68]
            dst = lru_c[:, l * 8:(l + 1) * 8]
            T.op('act', lambda: nc.scalar.activation(out=dst, in_=src, func=AF.Exp, scale=-1.0),
                 reads=[R_misc], writes=[R_lc])
            T.op('act', lambda: nc.scalar.activation(out=dst, in_=dst, func=AF.Ln, bias=1.0),
                 reads=[R_lc], writes=[R_lc])
            T.op('dve', lambda: nc.vector.tensor_scalar(out=lru_2c[:, l * 8:(l + 1) * 8], in0=dst, scalar1=-16.0,
                                                         scalar2=None, op0=ALU.mult), reads=[R_lc], writes=[R_lc])
            T.op('dve', lambda: nc.vector.tensor_scalar(out=dst, in0=dst, scalar1=-8.0, scalar2=None, op0=ALU.mult),
                 reads=[R_lc], writes=[R_lc])

        def rmsnorm_to_hT(grow, spill):
            R_g = R_gG
            T.dma('sp', 'c_g', gbc[:], gvec_in[grow:grow + 1, :].partition_broadcast(128), writes=[R_g],
                  reads=[])
            R_ss = R_ssG
            for t in range(NT):
                T.op('act', lambda: nc.scalar.activation(out=junk[:], in_=X[:, t, :], func=AF.Square,
                                                         accum_out=ssq[:, t:t + 1]),
                     reads=[R_X[t]], writes=[R_ss, R_junk])
                if spill:
                    T.dma('sp', ('xs', t % 4), xd[t * 128:(t + 1) * 128, :], X[:, t, :], reads=[R_X[t]],
                          writes=[R_xd[t]])
            T.op('dve', lambda: nc.vector.tensor_scalar(out=rstd[:], in0=ssq[:], scalar1=1.0 / D, scalar2=EPS,
                                                         op0=ALU.mult, op1=ALU.add), reads=[R_ss], writes=[R_ss])
            T.op('act', lambda: nc.scalar.activation(out=rstd[:], in_=rstd[:], func=AF.Sqrt), reads=[R_ss],
                 writes=[R_ss])
            T.op('dve', lambda: nc.vector.reciprocal(out=rstd[:], in_=rstd[:]), reads=[R_ss], writes=[R_ss])
            R_hn = R_hnG
            for t in range(NT):
                b = t % 2
                T.op('dve', lambda: nc.vector.scalar_tensor_tensor(out=hn[b][:], in0=X[:, t, :],
                                                                    scalar=rstd[:, t:t + 1], in1=gbc[:],
                                                                    op0=ALU.mult, op1=ALU.mult),
                     reads=[R_X[t], R_ss, R_g], writes=[R_hn[b]])
                tb, rtb = tb_next()
                for kc in range(8):
                    T.op('pe', lambda: nc.tensor.transpose(out=tb[:, kc * 128:(kc + 1) * 128],
                                                           in_=hn[b][:, kc * 128:(kc + 1) * 128], identity=ident[:]),
                         reads=[R_hn[b], R_misc], writes=[rtb])
                if t % 2 == 0:
                    T.op('act', lambda: nc.scalar.copy(out=hT[:, :, t * 128:(t + 1) * 128],
                                                       in_=tb[:, :].rearrange("p (k n) -> p k n", k=8)),
                         reads=[rtb], writes=[R_hT[t]])
                else:
                    T.op('dve', lambda: nc.vector.tensor_copy(out=hT[:, :, t * 128:(t + 1) * 128],
                                                               in_=tb[:, :].rearrange("p (k n) -> p k n", k=8)),
                         reads=[rtb], writes=[R_hT[t]])

        def transpose_tm_to_fm(src_tile, R_src, ncols, dstT, kc0, t, R_dst):
            nk = ncols // 128
            tb, rtb = tb_next()
            for j in range(nk):
                T.op('pe', lambda: nc.tensor.transpose(out=tb[:, j * 128:(j + 1) * 128],
                                                       in_=src_tile[:, j * 128:(j + 1) * 128], identity=ident[:]),
                     reads=[R_src, R_misc], writes=[rtb])
            T.op('act', lambda: nc.scalar.copy(out=dstT[:, kc0:kc0 + nk, t * 128:(t + 1) * 128],
                                               in_=tb[:, 0:nk * 128].rearrange("p (k n) -> p k n", k=nk)),
                 reads=[rtb], writes=[R_dst])

        def dump(name, src_ap, reads):
            if dbg and name in dbg_out:
                T.barrier()
                T.dma('sp', 'dbg', dbg_out[name], src_ap, reads=reads)
                T.barrier()

        for l in range(depth):
            cb = l * NCOLS_SMALL

            rmsnorm_to_hT(l * 3 + 0, spill=True)
            T.barrier(engines=('act', 'dve', 'pool', 'sp'))
            if dbg and l == 0:
                dump('hT', hT[:].rearrange("p k n -> p (k n)"), R_hT)
            if stop == 'A':
                break

            CV.reset()
            btab = CV.get("btab", [128, 8, 14, 64], BF16)
            R_bt = Res('bt')
            T.dma('pool', 'c_bt', btab[:].rearrange("p a b c -> p a (b c)"),
                  bt_in[l].rearrange("p (a b) -> p a b", a=8), writes=[R_bt])
            qT = CV.get("qT", [128, 2, S], BF16)
            kT = CV.get("kT", [128, 2, S], BF16)
            Ve = CV.get("Ve", [128, 16, 4, 65], BF16)
            Vo = CV.get("Vo", [128, 16, 4, 65], BF16)
            NSC = 4
            scb = [CV.get("sc%d" % i, [128, 512], F32) for i in range(NSC)]
            ptb = [CV.get("pt%d" % i, [128, 512], BF16) for i in range(NSC)]
            ytile = [CV.get("yt%d" % i, [128, 256], BF16) for i in range(2)]
            rden = [CV.get("rd%d" % i, [128, 4], F32) for i in range(2)]
            R_sc = [Res() for _ in range(NSC)]
            R_pt = [Res() for _ in range(NSC)]
            R_yt = [Res(), Res()]
            R_rd = [Res(), Res()]
            R_q = [Res() for _ in range(4)]
            R_k = [Res() for _ in range(4)]
            R_ve = [Res() for _ in range(NT)]
            R_vo = [Res() for _ in range(NT)]
            R_ones = Res('ones')
            for Vt_ in (Ve, Vo):
                T.op('act', lambda: nc.scalar.activation(out=Vt_[:, :, :, 64],
                                                         in_=ident[:, 0:64].rearrange("p (a b) -> p a b", a=16),
                                                         func=AF.Copy, scale=0.0, bias=1.0),
                     reads=[R_misc], writes=[R_ones])
            sctr = 0
            for ps in range(2):
                wqk, rwqk = w_next(4096)
                wqk3 = wqk[:, :].rearrange("p (k n) -> p k n", k=8)
                for which, dst, Rd in ((0, qT, R_q), (1, kT, R_k)):
                    for hp in range(2):
                        for tg in range(4):
                            pb, rpb = ps_next()
                            for kc in range(8):
                                c0 = which * 256 + hp * 128
                                T.op('pe', lambda: nc.tensor.matmul(pb[:, :], lhsT=wqk3[:, kc, c0:c0 + 128],
                                                                    rhs=hT[:, kc, tg * 512:(tg + 1) * 512],
                                                                    start=(kc == 0), stop=(kc == 7)),
                                     reads=[rwqk] + R_hT[tg * 4:tg * 4 + 4], writes=[rpb])
                            sc_ = 0.125 if which == 0 else 1.0
                            T.op('act', lambda: nc.scalar.activation(out=dst[:, hp, tg * 512:(tg + 1) * 512],
                                                                     in_=pb[:, :], func=AF.Copy, scale=sc_),
                                 reads=[rpb], writes=[Rd[tg]])
                w_done()
                wv, rwv = w_next(2048)
                wv3 = wv[:, 0:2048].rearrange("p (k n) -> p k n", k=8)
                for par, Vt, Rv, ntl in ((0, Ve, R_ve, 16), (1, Vo, R_vo, 15)):
                    for t in range(ntl):
                        tok0 = t * 128 + par * 64
                        pb, rpb = ps_next()
                        rr = R_hT[t:t + 2] if par else R_hT[t:t + 1]
                        for kc in range(8):
                            T.op('pe', lambda: nc.tensor.matmul(pb[:, 0:256], lhsT=hT[:, kc, tok0:tok0 + 128],
                                                                rhs=wv3[:, kc, :], start=(kc == 0), stop=(kc == 7)),
                                 reads=[rwv] + rr, writes=[rpb])
                        T.op('act', lambda: nc.scalar.copy(out=Vt[:, t, :, 0:64],
                                                           in_=pb[:, 0:256].rearrange("p (h d) -> p h d", h=4)),
                             reads=[rpb, R_ones], writes=[Rv[t]])
                w_done()
                steps = [(t, a, hh) for t in range(NT) for a in range(2) for hh in range(2)]
                front = {}

                def na_front(sidx):
                    t, a, hh = steps[sidx]
                    r = 2 * t + a
                    rs = min(max(r - 4, 0), 24)
                    d0 = rs - r + 7
                    pb, rpb = ps_next()
                    for hp in range(2):
                        for i in range(4):
                            k0 = (rs + 2 * i) * 64
                            T.op('pe', lambda: nc.tensor.matmul(
                                pb[:, (hp * 4 + i) * 64:(hp * 4 + i + 1) * 64],
                                lhsT=kT[hh * 64:(hh + 1) * 64, hp, k0:k0 + 128],
                                rhs=qT[hh * 64:(hh + 1) * 64, hp, r * 64:(r + 1) * 64],
                                start=True, stop=True),
                                reads=[R_k[k0 // 512], R_k[(k0 + 127) // 512], R_q[r // 8]], writes=[rpb])
                    si = sidx % NSC
                    hg = ps * 4 + hh
                    T.op('dve', lambda: nc.vector.tensor_tensor(
                        out=scb[si][:, :].rearrange("p (h i c) -> p h i c", h=2, i=4),
                        in0=pb[:, :].rearrange("p (h i c) -> p h i c", h=2, i=4),
                        in1=btab[:, hg:hg + 3:2, d0:d0 + 7:2, :], op=ALU.add),
                        reads=[rpb, R_bt], writes=[R_sc[si]])
                    T.op('act', lambda: nc.scalar.activation(out=ptb[si][:, :], in_=scb[si][:, :], func=AF.Exp),
                         reads=[R_sc[si]], writes=[R_pt[si]])

                obs = {}

                def na_back(sidx):
                    t, a, hh = steps[sidx]
                    r = 2 * t + a
                    rs = min(max(r - 4, 0), 24)
                    si = sidx % NSC
                    if (a, hh) == (0, 0):
                        obs[t] = ps_next()
                    ob, rob = obs[t]
                    for hp in range(2):
                        hl = hp * 2 + hh
                        for i in range(4):
                            kr = rs + 2 * i
                            if rs % 2 == 0:
                                Vt, Rv, tl = Ve, R_ve, kr // 2
                            else:
                                Vt, Rv, tl = Vo, R_vo, (kr - 1) // 2
                            T.op('pe', lambda: nc.tensor.matmul(
                                ob[a * 64:(a + 1) * 64, hl * 65:(hl + 1) * 65],
                                lhsT=ptb[si][:, (hp * 4 + i) * 64:(hp * 4 + i + 1) * 64],
                                rhs=Vt[:, tl, hl, :], start=(i == 0), stop=(i == 3)),
                                reads=[R_pt[si], Rv[tl]], writes=[rob])
                    if (a, hh) == (1, 1):
                        yb = t % 2
                        ob3 = ob[:, 0:260].rearrange("p (h d) -> p h d", h=4)
                        T.op('dve', lambda: nc.vector.reciprocal(out=rden[yb][:, :], in_=ob3[:, :, 64]),
                             reads=[rob], writes=[R_rd[yb]])
                        T.op('dve', lambda: nc.vector.tensor_tensor(
                            out=ytile[yb][:, :].rearrange("p (h d) -> p h d", h=4), in0=ob3[:, :, 0:64],
                            in1=rden[yb][:, :].unsqueeze(2).to_broadcast([128, 4, 64]), op=ALU.mult),
                            reads=[rob, R_rd[yb]], writes=[R_yt[yb]])
                        pend_tr.append(t)

                pend_tr = []

                def na_flush():
                    while pend_tr:
                        t_ = pend_tr.pop(0)
                        transpose_tm_to_fm(ytile[t_ % 2], R_yt[t_ % 2], 256, ysT[0], ps * 2, t_, R_ys[0][t_])

                GRP = 2
                ngrp = len(steps) // GRP
                for k_ in range(GRP):
                    na_front(k_)
                for g_ in range(ngrp):
                    if g_ + 1 < ngrp:
                        for k_ in range(GRP):
                            na_front((g_ + 1) * GRP + k_)
                    if steps[g_ * GRP][1:] == (1, 0):
                        na_flush()
                    for k_ in range(GRP):
                        na_back(g_ * GRP + k_)
                na_flush()
            T.barrier(engines=('act', 'dve', 'pool', 'sp'))
            if dbg and l == 0:
                dump('ynaT', ysT[0][:].rearrange("p k n -> p (k n)"), R_ys[0])
                dump('qT', qT[:].rearrange("p k n -> p (k n)"), [])
                dump('kT', kT[:].rearrange("p k n -> p (k n)"), [])
                dump('Ve', Ve[:].rearrange("p a b c -> p (a b c)"), [])
                dump('Vo', Vo[:].rearrange("p a b c -> p (a b c)"), [])
                dump('bt', btab[:].rearrange("p a b c -> p (a b c)"), [])
            if stop == 'B':
                break

            CV.reset()
            cosT = CV.get("cos", [128, 16, 64], F32)
            sinT = CV.get("sin", [128, 16, 64], F32)
            DTt = CV.get("DT", [128, 4, 128], F32)
            QFt = CV.get("QF", [128, 4, 128], BF16)
            QBt = CV.get("QB", [128, 4, 128], BF16)
            gnb = gbc
            R_ct = Res('ctab')
            T.dma('sp', 'c_cos', cosT[:].rearrange("p a b -> p (a b)"), cos_in, writes=[R_ct])
            T.dma('sp', 'c_sin', sinT[:].rearrange("p a b -> p (a b)"), sin_in, writes=[R_ct])
            T.dma('sp', 'c_dt', DTt[:].rearrange("p a b -> p (a b)"), dt_in, writes=[R_ct])
            T.dma('pool', 'c_qf', QFt[:].rearrange("p a b -> p (a b)"), qf_in, writes=[R_ct])
            T.dma('pool', 'c_qb', QBt[:].rearrange("p a b -> p (a b)"), qb_in, writes=[R_ct])
            T.dma('sp', 'c_gn', gnb[:, 0:512], gn_in[l:l + 1, :].partition_broadcast(128), writes=[R_ct])
            krot = CV.get("krot", [128, NT, 512], BF16)
            vtm = CV.get("vtm", [128, NT, 512], BF16)
            Sb = CV.get("Sb", [128, NT, 512], BF16)
            Sf32 = CV.get("Sf32", [128, 512], F32)
            Sfb = [CV.get("Sfb%d" % i, [128, 512], BF16) for i in range(2)]
            rt1 = sqf[0]
            rt2 = sqf[1]
            kd = [CV.get("kd%d" % i, [128, 512], BF16) for i in range(2)]
            qrot = [CV.get("qrot%d" % i, [128, 512], BF16) for i in range(2)]
            qTc = [CV.get("qTc%d" % i, [128, 4, 128], BF16) for i in range(2)]
            kTc = [CV.get("kTc%d" % i, [128, 4, 128], BF16) for i in range(2)]
            qfT = [CV.get("qfT%d" % i, [128, 4, 128], BF16) for i in range(2)]
            qbT = [CV.get("qbT%d" % i, [128, 4, 128], BF16) for i in range(2)]
            pTr = [CV.get("pTr%d" % i, [128, 4, 128], BF16) for i in range(2)]
            sge = [CV.get("sge%d" % i, [128, 512], F32) for i in range(2)]
            o1 = [nc.alloc_sbuf_tensor_at("o1_%d_%d" % (l, i), [128, 512], F32, offset=sb_addr["hn%d" % i])
                  for i in range(2)]
            yrt = [CV.get("yrt%d" % i, [128, 512], BF16) for i in range(2)]
            gss = [ssq[:, 0:4], ssq[:, 4:8]]
            R_kr = [Res() for _ in range(NT)]
            R_v = [Res() for _ in range(NT)]
            R_Sb = [Res() for _ in range(NT)]
            R_S32 = Res()
            R_Sfb = [Res(), Res()]
            R_rt = Res()
            R_kd = [Res(), Res()]
            R_qrot = [Res(), Res()]
            R_qTc = [Res(), Res()]
            R_kTc = [Res(), Res()]
            R_qfT = [Res(), Res()]
            R_qbT = [Res(), Res()]
            R_pTr = [Res(), Res()]
            R_sge = [Res(), Res()]
            R_o1 = [Res(), Res()]
            R_yrt = [Res(), Res()]
            R_gss = [Res(), Res()]

            def rotary(pb, rpb, t, dst, R_dst):
                p4 = pb[:, :].rearrange("p (h two m) -> p h two m", h=4, two=2)
                d4 = dst.rearrange("p (h two m) -> p h two m", h=4, two=2)
                a4 = rt1[:, :].rearrange("p (h two m) -> p h two m", h=4, two=2)
                b4 = rt2[:, :].rearrange("p (h two m) -> p h two m", h=4, two=2)
                cb_ = cosT[:, t, :].unsqueeze(1).unsqueeze(1).to_broadcast([128, 4, 2, 64])
                sb_ = sinT[:, t, :].unsqueeze(1).unsqueeze(1).to_broadcast([128, 4, 2, 64])
                T.op('dve', lambda: nc.vector.tensor_tensor(out=a4, in0=p4, in1=cb_, op=ALU.mult),
                     reads=[rpb, R_ct], writes=[R_rt])
                T.op('dve', lambda: nc.vector.tensor_tensor(out=b4, in0=p4[:, :, ::-1, :], in1=sb_, op=ALU.mult),
                     reads=[rpb, R_ct], writes=[R_rt])
                T.op('dve', lambda: nc.vector.tensor_tensor(out=d4[:, :, 0, :], in0=a4[:, :, 0, :], in1=b4[:, :, 0, :],
                                                             op=ALU.subtract), reads=[R_rt], writes=[R_dst])
                T.op('dve', lambda: nc.vector.tensor_tensor(out=d4[:, :, 1, :], in0=a4[:, :, 1, :], in1=b4[:, :, 1, :],
                                                             op=ALU.add), reads=[R_rt], writes=[R_dst])

            def proj_tm(w3, rw, t):
                pb, rpb = ps_next()
                for kc in range(8):
                    T.op('pe', lambda: nc.tensor.matmul(pb[:, :], lhsT=hT[:, kc, t * 128:(t + 1) * 128],
                                                        rhs=w3[:, kc, :], start=(kc == 0), stop=(kc == 7)),
                         reads=[rw, R_hT[t]], writes=[rpb])
                return pb, rpb

            wk, rwk = w_next(4096)
            wk3 = wk[:, :].rearrange("p (k n) -> p k n", k=8)
            wv_, rwv_ = w_next(4096)
            wv3_ = wv_[:, :].rearrange("p (k n) -> p k n", k=8)
            for t in range(NT):
                pb, rpb = proj_tm(wk3, rwk, t)
                rotary(pb, rpb, t, krot[:, t, :], R_kr[t])
                pb, rpb = proj_tm(wv3_, rwv_, t)
                T.op('act', lambda: nc.scalar.copy(out=vtm[:, t, :], in_=pb[:, :]), reads=[rpb], writes=[R_v[t]])
            w_done(2)

            def state_u(c, dirn):
                b = c % 2
                T.op('dve', lambda: nc.vector.tensor_tensor(
                    out=kd[b][:, :].rearrange("p (h d) -> p h d", h=4),
                    in0=krot[:, c, :].rearrange("p (h d) -> p h d", h=4),
                    in1=kdt[:, dirn * 4:dirn * 4 + 4].unsqueeze(2).to_broadcast([128, 4, 128]), op=ALU.mult),
                    reads=[R_kr[c], R_misc], writes=[R_kd[b]])
                pb, rpb = ps_next()
                for h in range(4):
                    T.op('pe', lambda: nc.tensor.matmul(pb[:, h * 128:(h + 1) * 128], lhsT=kd[b][:, h * 128:(h + 1) * 128],
                                                        rhs=vtm[:, c, h * 128:(h + 1) * 128], start=True, stop=True),
                         reads=[R_kd[b], R_v[c]], writes=[rpb])
                return pb, rpb

            def state_acc(pb, rpb, S32, R_S, cdec, first):
                if first:
                    T.op('dve', lambda: nc.vector.tensor_copy(out=S32[:, :], in_=pb[:, :]), reads=[rpb], writes=[R_S])
                else:
                    for h in range(4):
                        T.op('dve', lambda: nc.vector.scalar_tensor_tensor(
                            out=S32[:, h * 128:(h + 1) * 128], in0=S32[:, h * 128:(h + 1) * 128], scalar=cdec[h],
                            in1=pb[:, h * 128:(h + 1) * 128], op0=ALU.mult, op1=ALU.add),
                            reads=[rpb, R_S], writes=[R_S])

            u_next = state_u(NT - 1, 1)
            for c in range(NT - 1, 0, -1):
                u_cur = u_next
                if c - 1 >= 1:
                    u_next = state_u(c - 1, 1)
                state_acc(u_cur[0], u_cur[1], Sf32, R_S32, _CDB, first=(c == NT - 1))
                T.op('act', lambda: nc.scalar.copy(out=Sb[:, c - 1, :], in_=Sf32[:, :]), reads=[R_S32],
                     writes=[R_Sb[c - 1]])
            wq, rwq = w_next(4096)
            wq3 = wq[:, :].rearrange("p (k n) -> p k n", k=8)
            wg_, rwg_ = w_next(4096)
            wg3_ = wg_[:, :].rearrange("p (k n) -> p k n", k=8)
            def ret_front(c):
                b = c % 2
                pb, rpb = proj_tm(wq3, rwq, c)
                rotary(pb, rpb, c, qrot[b][:, :], R_qrot[b])
                for src, Rs, dstc, Rdc in ((qrot[b][:, :], R_qrot[b], qTc[b], R_qTc[b]),
                                           (krot[:, c, :], R_kr[c], kTc[b], R_kTc[b])):
                    tb, rtb = tb_next()
                    for h in range(4):
                        T.op('pe', lambda: nc.tensor.transpose(out=tb[:, h * 128:(h + 1) * 128],
                                                               in_=src[:, h * 128:(h + 1) * 128], identity=ident[:]),
                             reads=[Rs, R_misc], writes=[rtb])
                    T.op('act', lambda: nc.scalar.copy(out=dstc[:].rearrange("p h n -> p (h n)"), in_=tb[:, 0:512]),
                         reads=[rtb], writes=[Rdc])
                if c > 0:
                    T.op('dve', lambda: nc.vector.tensor_tensor(out=qfT[b][:], in0=qTc[b][:], in1=QFt[:], op=ALU.mult),
                         reads=[R_qTc[b], R_ct], writes=[R_qfT[b]])
                if c < NT - 1:
                    T.op('dve', lambda: nc.vector.tensor_tensor(out=qbT[b][:], in0=qTc[b][:], in1=QBt[:], op=ALU.mult),
                         reads=[R_qTc[b], R_ct], writes=[R_qbT[b]])
                pg, rpg = proj_tm(wg3_, rwg_, c)
                T.op('act', lambda: nc.scalar.activation(out=sge[b][:, :], in_=pg[:, :], func=AF.Silu), reads=[rpg],
                     writes=[R_sge[b]])
                T.op('dve', lambda: nc.vector.tensor_tensor(out=sge[b][:, :], in0=sge[b][:, :], in1=gnb[:, 0:512],
                                                             op=ALU.mult), reads=[R_sge[b], R_ct], writes=[R_sge[b]])
                pscore, rps = ps_next()
                for h in range(4):
                    T.op('pe', lambda: nc.tensor.matmul(pscore[:, h * 128:(h + 1) * 128], lhsT=kTc[b][:, h, :],
                                                        rhs=qTc[b][:, h, :], start=True, stop=True),
                         reads=[R_kTc[b], R_qTc[b]], writes=[rps])
                T.op('dve', lambda: nc.vector.tensor_tensor(out=pTr[b][:].rearrange("p h n -> p (h n)"),
                                                             in0=pscore[:, :],
                                                             in1=DTt[:].rearrange("p h n -> p (h n)"), op=ALU.mult),
                     reads=[rps, R_ct], writes=[R_pTr[b]])
            def ret_mid(c):
                b = c % 2
                po, rpo = ps_next()
                sfb_prev = Sfb[(c + 1) % 2]
                R_sfb_prev = R_Sfb[(c + 1) % 2]
                for h in range(4):
                    last_is = 0 if (c == 0 and c == NT - 1) else None
                    steps = [('intra', None)]
                    if c > 0:
                        steps.append(('f', None))
                    if c < NT - 1:
                        steps.append(('b', None))
                    for si_, (kind, _) in enumerate(steps):
                        st_, sp_ = (si_ == 0), (si_ == len(steps) - 1)
                        if kind == 'intra':
                            T.op('pe', lambda: nc.tensor.matmul(po[:, h * 128:(h + 1) * 128], lhsT=pTr[b][:, h, :],
                                                                rhs=vtm[:, c, h * 128:(h + 1) * 128], start=st_,
                                                                stop=sp_),
                                 reads=[R_pTr[b], R_v[c]], writes=[rpo])
                        elif kind == 'f':
                            T.op('pe', lambda: nc.tensor.matmul(po[:, h * 128:(h + 1) * 128], lhsT=qfT[b][:, h, :],
                                                                rhs=sfb_prev[:, h * 128:(h + 1) * 128], start=st_,
                                                                stop=sp_),
                                 reads=[R_qfT[b], R_sfb_prev], writes=[rpo])
                        else:
                            T.op('pe', lambda: nc.tensor.matmul(po[:, h * 128:(h + 1) * 128], lhsT=qbT[b][:, h, :],
                                                                rhs=Sb[:, c, h * 128:(h + 1) * 128], start=st_,
                                                                stop=sp_),
                                 reads=[R_qbT[b], R_Sb[c]], writes=[rpo])
                if c < NT - 1:
                    u_ = state_u(c, 0)
                    state_acc(u_[0], u_[1], Sf32, R_S32, _CDF, first=(c == 0))
                    T.op('act', lambda: nc.scalar.copy(out=Sfb[b][:, :], in_=Sf32[:, :]), reads=[R_S32],
                         writes=[R_Sfb[b]])
                T.op('act', lambda: nc.scalar.activation(out=o1[b][:, :], in_=po[:, :], func=AF.Square),
                     reads=[rpo], writes=[R_o1[b]])
                T.op('dve', lambda: nc.vector.tensor_reduce(out=gss[b][:, :],
                                                             in_=o1[b][:, :].rearrange("p (h d) -> p h d", h=4),
                                                             axis=mybir.AxisListType.X, op=ALU.add),
                     reads=[R_o1[b]], writes=[R_gss[b]])
                T.op('dve', lambda: nc.vector.tensor_scalar(out=gss[b][:, :], in0=gss[b][:, :], scalar1=1.0 / 128,
                                                             scalar2=EPS, op0=ALU.mult, op1=ALU.add),
                     reads=[R_gss[b]], writes=[R_gss[b]])
                T.op('act', lambda: nc.scalar.activation(out=gss[b][:, :], in_=gss[b][:, :], func=AF.Sqrt),
                     reads=[R_gss[b]], writes=[R_gss[b]])
                T.op('dve', lambda: nc.vector.reciprocal(out=gss[b][:, :], in_=gss[b][:, :]), reads=[R_gss[b]],
                     writes=[R_gss[b]])
                T.op('dve', lambda: nc.vector.tensor_tensor(
                    out=o1[b][:, :].rearrange("p (h d) -> p h d", h=4),
                    in0=po[:, :].rearrange("p (h d) -> p h d", h=4),
                    in1=gss[b][:, :].unsqueeze(2).to_broadcast([128, 4, 128]), op=ALU.mult),
                    reads=[rpo, R_gss[b]], writes=[R_o1[b]])
                T.op('dve', lambda: nc.vector.tensor_tensor(out=yrt[b][:, :], in0=o1[b][:, :], in1=sge[b][:, :],
                                                             op=ALU.mult), reads=[R_o1[b], R_sge[b]],
                     writes=[R_yrt[b]])

            def ret_back(c):
                b = c % 2
                transpose_tm_to_fm(yrt[b], R_yrt[b], 512, ysT[1], 0, c, R_ys[1][c])

            ret_front(0)
            for c in range(NT):
                if c + 1 < NT:
                    ret_front(c + 1)
                ret_mid(c)
                if c >= 1:
                    ret_back(c - 1)
            ret_back(NT - 1)
            w_done(2)
            T.barrier(engines=('act', 'dve', 'pool', 'sp'))
            if dbg and l == 0:
                dump('yretT', ysT[1][:].rearrange("p k n -> p (k n)"), R_ys[1])
                dump('krot', krot[:].rearrange("p a b -> p (a b)"), [])
                dump('vtm', vtm[:].rearrange("p a b -> p (a b)"), [])
                dump('Sb', Sb[:].rearrange("p a b -> p (a b)"), [])
                for nm_, t_ in (('qTc', qTc[0]), ('kTc', kTc[0]), ('pTr', pTr[0]), ('qfT', qfT[0]), ('qbT', qbT[0])):
                    dump(nm_, t_[:].rearrange("p a b -> p (a b)"), [])
                dump('yrt', yrt[0][:, :], [])
                dump('sge', sge[0][:, :], [])
                dump('o1', o1[0][:, :], [])
            if stop == 'C':
                break

            CV.reset()
            xcp2 = [CV.get("xcp%d" % i, [128, S + 4], F32) for i in range(2)]
            xf = CV.get("xf", [128, S], F32)
            xfb = CV.get("xfb", [128, S], BF16)
            gg2 = [CV.get("gg%d" % i, [128, S], BF16) for i in range(2)]
            abuf = [CV.get("abuf%d" % i, [128, S], F32) for i in range(2)]
            bbuf = [CV.get("bbuf%d" % i, [128, S], F32) for i in range(2)]
            mbuf = [CV.get("mbuf%d" % i, [128, S], F32) for i in range(2)]
            R_xf, R_xfb = Res(), Res()
            R_xcp2 = [Res(), Res()]
            R_gg2 = [Res(), Res()]
            R_a = [Res(), Res()]
            R_b = [Res(), Res()]
            R_m = [Res(), Res()]
            R_pad = Res()
            for xcp_ in xcp2:
                T.op('act', lambda: nc.scalar.activation(out=xcp_[:, 0:2], in_=ident[:, 0:2], func=AF.Copy,
                                                         scale=0.0), reads=[R_misc], writes=[R_pad])
                T.op('act', lambda: nc.scalar.activation(out=xcp_[:, S + 2:S + 4], in_=ident[:, 0:2], func=AF.Copy,
                                                         scale=0.0), reads=[R_misc], writes=[R_pad])
            lru_w = {}

            def lru_front(ch):
                xcp, gg, R_xcp, R_gg = xcp2[ch % 2], gg2[ch % 2], R_xcp2[ch % 2], R_gg2[ch % 2]
                wl, rwl = w_next(2560)
                lru_w[ch] = (wl, rwl)
                wl3 = wl[:, 0:2048].rearrange("p (k n) -> p k n", k=8)
                for which in range(2):
                    for tg in range(4):
                        pb, rpb = ps_next()
                        for kc in range(8):
                            T.op('pe', lambda: nc.tensor.matmul(pb[:, :], lhsT=wl3[:, kc, which * 128:(which + 1) * 128],
                                                                rhs=hT[:, kc, tg * 512:(tg + 1) * 512],
                                                                start=(kc == 0), stop=(kc == 7)),
                                 reads=[rwl] + R_hT[tg * 4:tg * 4 + 4], writes=[rpb])
                        if which == 0:
                            T.op('act', lambda: nc.scalar.copy(out=xcp[:, 2 + tg * 512:2 + (tg + 1) * 512], in_=pb[:, :]),
                                 reads=[rpb, R_pad], writes=[R_xcp])
                        else:
                            T.op('act', lambda: nc.scalar.activation(out=gg[:, tg * 512:(tg + 1) * 512], in_=pb[:, :],
                                                                     func=AF.Gelu_apprx_tanh),
                                 reads=[rpb], writes=[R_gg])

            def lru_main(ch):
                xcp, gg, R_xcp, R_gg = xcp2[ch % 2], gg2[ch % 2], R_xcp2[ch % 2], R_gg2[ch % 2]
                wl, rwl = lru_w[ch]
                wm = wl[:, 2048:2560].rearrange("p (m n) -> p m n", m=4)
                cw = lambda k_: colp[:, cb + 24 + ch * 4 + k_:cb + 24 + ch * 4 + k_ + 1]
                T.op('dve', lambda: nc.vector.tensor_scalar(out=xf[:, :], in0=xcp[:, 0:S], scalar1=cw(0),
                                                             scalar2=colp[:, cb + 40 + ch:cb + 41 + ch], op0=ALU.mult,
                                                             op1=ALU.add), reads=[R_xcp, R_misc], writes=[R_xf])
                for k_ in range(1, 4):
                    T.op('dve', lambda: nc.vector.scalar_tensor_tensor(out=xf[:, :], in0=xcp[:, k_:k_ + S],
                                                                        scalar=cw(k_), in1=xf[:, :], op0=ALU.mult,
                                                                        op1=ALU.add), reads=[R_xcp, R_xf, R_misc],
                         writes=[R_xf])
                T.op('act', lambda: nc.scalar.copy(out=xfb[:, :], in_=xf[:, :]), reads=[R_xf], writes=[R_xfb])
                def lru_dir(dirn):
                    ci = l * 8 + dirn * 4 + ch
                    ba_ = colp[:, cb + 44 + dirn * 4 + ch:cb + 45 + dirn * 4 + ch]
                    bx_ = colp[:, cb + 52 + dirn * 4 + ch:cb + 53 + dirn * 4 + ch]
                    ab, bb, mb = abuf[dirn], bbuf[dirn], mbuf[dirn]
                    Ra, Rb, Rm = R_a[dirn], R_b[dirn], R_m[dirn]
                    for tg in range(4):
                        for mi, bias_, dstb, Rd in ((dirn * 2, ba_, ab, Ra), (dirn * 2 + 1, bx_, bb, Rb)):
                            pb, rpb = ps_next()
                            T.op('pe', lambda: nc.tensor.matmul(pb[:, :], lhsT=wm[:, mi, :],
                                                                rhs=xfb[:, tg * 512:(tg + 1) * 512], start=True,
                                                                stop=True), reads=[rwl, R_xfb], writes=[rpb])
                            T.op('act', lambda: nc.scalar.activation(out=dstb[:, tg * 512:(tg + 1) * 512], in_=pb[:, :],
                                                                     func=AF.Sigmoid, bias=bias_),
                                 reads=[rpb, R_misc], writes=[Rd])
                        yield
                    T.op('act', lambda: nc.scalar.activation(out=mb[:, :], in_=ab[:, :], func=AF.Exp,
                                                             scale=lru_2c[:, ci:ci + 1]), reads=[Ra, R_lc],
                         writes=[Rm])
                    T.op('act', lambda: nc.scalar.activation(out=ab[:, :], in_=ab[:, :], func=AF.Exp,
                                                             scale=lru_c[:, ci:ci + 1]), reads=[Ra, R_lc],
                         writes=[Ra])
                    T.op('pool', lambda: nc.gpsimd.tensor_tensor(out=bb[:, :], in0=bb[:, :], in1=xf[:, :],
                                                                  op=ALU.mult), reads=[Rb, R_xf], writes=[Rb])
                    yield
                    T.op('act', lambda: nc.scalar.activation(out=mb[:, :], in_=mb[:, :], func=AF.Sqrt, scale=-1.0,
                                                             bias=1.0), reads=[Rm], writes=[Rm])
                    first = S - 1 if dirn else 0
                    T.op('dve', lambda: nc.vector.tensor_scalar(out=mb[:, first:first + 1],
                                                                 in0=mb[:, first:first + 1], scalar1=0.0, scalar2=1.0,
                                                                 op0=ALU.mult, op1=ALU.add), reads=[Rm], writes=[Rm])
                    yield
                    T.op('dve', lambda: nc.vector.tensor_tensor(out=bb[:, :], in0=bb[:, :], in1=mb[:, :],
                                                                 op=ALU.mult), reads=[Rb, Rm], writes=[Rb])
                    yield
                    if dirn == 0:
                        T.op('dve', lambda: nc.vector.tensor_tensor_scan(out=bb[:, :], data0=ab[:, :],
                                                                          data1=bb[:, :], initial=0.0, op0=ALU.mult,
                                                                          op1=ALU.add), reads=[Ra, Rb], writes=[Rb])
                    else:
                        T.op('dve', lambda: nc.vector.tensor_tensor_scan(out=bb[:, ::-1], data0=ab[:, ::-1],
                                                                          data1=bb[:, ::-1], initial=0.0,
                                                                          op0=ALU.mult, op1=ALU.add),
                             reads=[Ra, Rb], writes=[Rb])
                    yield

                gens = [lru_dir(0), lru_dir(1)]
                alive = [True, True]
                rounds = 0
                while any(alive):
                    for gi_ in range(2):
                        if alive[gi_]:
                            try:
                                next(gens[gi_])
                            except StopIteration:
                                alive[gi_] = False
                    rounds += 1
                    if rounds == 4 and ch + 1 < 4:
                        lru_front(ch + 1)
                T.op('pool', lambda: nc.gpsimd.tensor_tensor(out=bbuf[0][:, :], in0=bbuf[0][:, :], in1=bbuf[1][:, :],
                                                              op=ALU.add), reads=[R_b[0], R_b[1]], writes=[R_b[0]])
                T.op('pool', lambda: nc.gpsimd.tensor_tensor(out=ysT[2][:, ch, :], in0=bbuf[0][:, :], in1=gg[:, :],
                                                              op=ALU.mult), reads=[R_b[0], R_gg], writes=R_ys[2])

            lru_front(0)
            for ch in range(4):
                lru_main(ch)
                w_done()
            T.barrier(engines=('act', 'dve', 'pool', 'sp'))
            if dbg and l == 0:
                dump('ylruT', ysT[2][:].rearrange("p k n -> p (k n)"), R_ys[2])
            if stop == 'D':
                break

            CV.reset()
            macc = CV.get("macc", [128, 8, S], F32)
            gtmp = [CV.get("gtmp%d" % i, [128, 512], F32) for i in range(2)]
            ptmp = [CV.get("ptmp%d" % i, [128, 512], F32) for i in range(2)]
            R_macc = [[Res() for _ in range(4)] for _ in range(8)]
            R_gt = [Res(), Res()]
            R_ptm = [Res(), Res()]
            mT = nc.alloc_sbuf_tensor_at("mT_%d" % l, [128, 8, S], BF16, offset=YS_ADDR)
            R_mT = [Res() for _ in range(4)]
            gctr = 0
            for n in range(3):
                wb, rwb = w_next(4096)
                wb3 = wb[:, :].rearrange("p (k n) -> p k n", k=4)
                for half in range(2):
                    wgt, rwgt = w_next(4096)
                    wgt3 = wgt[:, :].rearrange("p (k n) -> p k n", k=8)
                    for j in range(4):
                        cc = half * 4 + j
                        bcol = colp[:, cb + n * 8 + cc:cb + n * 8 + cc + 1]
                        for tg in range(4):
                            pl, rpl = ps_next()
                            for kc in range(8):
                                T.op('pe', lambda: nc.tensor.matmul(pl[:, :], lhsT=wgt3[:, kc, j * 128:(j + 1) * 128],
                                                                    rhs=hT[:, kc, tg * 512:(tg + 1) * 512],
                                                                    start=(kc == 0), stop=(kc == 7)),
                                     reads=[rwgt] + R_hT[tg * 4:tg * 4 + 4], writes=[rpl])
                            pbr, rpbr = ps_next()
                            for kc in range(4):
                                T.op('pe', lambda: nc.tensor.matmul(pbr[:, :], lhsT=wb3[:, kc, cc * 128:(cc + 1) * 128],
                                                                    rhs=ysT[n][:, kc, tg * 512:(tg + 1) * 512],
                                                                    start=(kc == 0), stop=(kc == 3)),
                                     reads=[rwb] + R_ys[n][tg * 4:tg * 4 + 4], writes=[rpbr])
                            gi = gctr % 2
                            gctr += 1
                            T.op('act', lambda: nc.scalar.activation(out=gtmp[gi][:, :], in_=pl[:, :], func=AF.Sigmoid,
                                                                     bias=bcol), reads=[rpl, R_misc],
                                 writes=[R_gt[gi]])
                            msl = macc[:, cc, tg * 512:(tg + 1) * 512]
                            if n == 0:
                                T.op('dve', lambda: nc.vector.tensor_tensor(out=msl, in0=pbr[:, :], in1=gtmp[gi][:, :],
                                                                             op=ALU.mult), reads=[rpbr, R_gt[gi]],
                                     writes=[R_macc[cc][tg]])
                            else:
                                T.op('dve', lambda: nc.vector.tensor_tensor(out=ptmp[gi][:, :], in0=pbr[:, :],
                                                                             in1=gtmp[gi][:, :], op=ALU.mult),
                                     reads=[rpbr, R_gt[gi]], writes=[R_ptm[gi]])
                                if n == 1:
                                    T.op('dve', lambda: nc.vector.tensor_tensor(out=msl, in0=msl, in1=ptmp[gi][:, :],
                                                                                 op=ALU.add),
                                         reads=[R_ptm[gi], R_macc[cc][tg]], writes=[R_macc[cc][tg]])
                                else:
                                    T.op('dve', lambda: nc.vector.tensor_tensor(
                                        out=mT[:, cc, tg * 512:(tg + 1) * 512], in0=msl, in1=ptmp[gi][:, :], op=ALU.add),
                                        reads=[R_ptm[gi], R_macc[cc][tg]] + R_ys[0][tg * 4:tg * 4 + 4] +
                                        R_ys[1][tg * 4:tg * 4 + 4], writes=[R_mT[tg]])
                    w_done_last()
                w_done()
            T.barrier(engines=('act', 'dve', 'pool', 'sp'))
            for t in range(NT):
                T.dma('sp', ('x', t % 4), X[:, t, :], xd[t * 128:(t + 1) * 128, :], reads=[R_xd[t]], writes=[R_X[t]])
            for half in range(2):
                wo, rwo = w_next(4096)
                wo3 = wo[:, :].rearrange("p (k n) -> p k n", k=8)
                for t in range(NT):
                    pb, rpb = ps_next()
                    for kc in range(8):
                        T.op('pe', lambda: nc.tensor.matmul(pb[:, :], lhsT=mT[:, kc, t * 128:(t + 1) * 128],
                                                            rhs=wo3[:, kc, :], start=(kc == 0), stop=(kc == 7)),
                             reads=[rwo, R_mT[t // 4]], writes=[rpb])
                    xs = X[:, t, half * 512:(half + 1) * 512]
                    T.op('dve', lambda: nc.vector.tensor_tensor(out=xs, in0=xs, in1=pb[:, :], op=ALU.add),
                         reads=[rpb, R_X[t]], writes=[R_X[t]])
                w_done()
            if dbg and l == 0:
                T.barrier()
                for t in range(NT):
                    T.dma('sp', 'dbg', dbg_out['xE'][t * 128:(t + 1) * 128, :], X[:, t, :], reads=[R_X[t]])
                T.barrier()
            if stop == 'E':
                break

            rmsnorm_to_hT(l * 3 + 1, spill=False)
            actT = nc.alloc_sbuf_tensor_at("actT_%d" % l, [128, 8, S], BF16,
                                           offset=YS_ADDR)
            R_sq = [Res(), Res()]
            R_act = [[Res() for _ in range(4)] for _ in range(8)]
            sctr2 = 0
            for s_ in range(4):
                for j2 in range(2):
                    wu, rwu = w_next(4096)
                    wu3 = wu[:, :].rearrange("p (k n) -> p k n", k=8)
                    for j in range(4):
                        fc = j2 * 4 + j
                        for tg in range(4):
                            pb, rpb = ps_next()
                            for kc in range(8):
                                T.op('pe', lambda: nc.tensor.matmul(pb[:, :], lhsT=wu3[:, kc, j * 128:(j + 1) * 128],
                                                                    rhs=hT[:, kc, tg * 512:(tg + 1) * 512],
                                                                    start=(kc == 0), stop=(kc == 7)),
                                     reads=[rwu] + R_hT[tg * 4:tg * 4 + 4], writes=[rpb])
                            si = sctr2 % 2
                            sctr2 += 1
                            T.op('act', lambda: nc.scalar.activation(out=sqf[si][:, :], in_=pb[:, :], func=AF.Square),
                                 reads=[rpb], writes=[R_sq[si]])
                            T.op('dve', lambda: nc.vector.scalar_tensor_tensor(
                                out=actT[:, fc, tg * 512:(tg + 1) * 512], in0=pb[:, :], scalar=0.0, in1=sqf[si][:, :],
                                op0=ALU.is_gt, op1=ALU.mult), reads=[rpb, R_sq[si]], writes=[R_act[fc][tg]] + R_mT)
                    w_done()
                wd = []
                for j2 in range(2):
                    w_, rw_ = w_next(4096)
                    wd.append((w_[:, :].rearrange("p (k n) -> p k n", k=4), rw_))
                for t in range(NT):
                    for half in range(2):
                        pb, rpb = ps_next()
                        for fc in range(8):
                            w3_, rw_ = wd[fc // 4]
                            T.op('pe', lambda: nc.tensor.matmul(pb[:, :], lhsT=actT[:, fc, t * 128:(t + 1) * 128],
                                                                rhs=w3_[:, fc % 4, half * 512:(half + 1) * 512],
                                                                start=(fc == 0), stop=(fc == 7)),
                                 reads=[rw_, R_act[fc][t // 4]], writes=[rpb])
                        xs = X[:, t, half * 512:(half + 1) * 512]
                        T.op('dve', lambda: nc.vector.tensor_tensor(out=xs, in0=xs, in1=pb[:, :], op=ALU.add),
                             reads=[rpb, R_X[t]], writes=[R_X[t]])
                w_done(2)
            if dbg and l == 0:
                T.barrier()
                for t in range(NT):
                    T.dma('sp', 'dbg', dbg_out['xF'][t * 128:(t + 1) * 128, :], X[:, t, :], reads=[R_X[t]])
                T.barrier()
            if stop == 'F':
                break

            rmsnorm_to_hT(l * 3 + 2, spill=False)
            ppT = nc.alloc_sbuf_tensor_at("ppT_%d" % l, [128, 2, S], BF16,
                                          offset=YS_ADDR)
            R_pp = Res()
            for kc in range(2):
                T.dma('pool', 'c_pp', ppT[:, kc, :], pT_in[l, kc * 128:(kc + 1) * 128, :],
                      writes=[R_pp] + R_act[0] + R_act[1])
            wp, rwp = w_next(2048)
            wp3 = wp[:, 0:2048].rearrange("p (k n) -> p k n", k=2)
            gctr = 0
            for half in range(2):
                wgp, rwgp = w_next(4096)
                wgp3 = wgp[:, :].rearrange("p (k n) -> p k n", k=8)
                for t in range(NT):
                    pl, rpl = ps_next()
                    for kc in range(8):
                        T.op('pe', lambda: nc.tensor.matmul(pl[:, :], lhsT=hT[:, kc, t * 128:(t + 1) * 128],
                                                            rhs=wgp3[:, kc, :], start=(kc == 0), stop=(kc == 7)),
                             reads=[rwgp, R_hT[t]], writes=[rpl])
                    pe_, rpe_ = ps_next()
                    for kc in range(2):
                        T.op('pe', lambda: nc.tensor.matmul(pe_[:, :], lhsT=ppT[:, kc, t * 128:(t + 1) * 128],
                                                            rhs=wp3[:, kc, half * 512:(half + 1) * 512],
                                                            start=(kc == 0), stop=(kc == 1)),
                             reads=[rwp, R_pp], writes=[rpe_])
                    gi = gctr % 2
                    gctr += 1
                    T.op('act', lambda: nc.scalar.activation(out=sqf[gi][:, :], in_=pl[:, :], func=AF.Sigmoid),
                         reads=[rpl], writes=[R_sq[gi]])
                    T.op('dve', lambda: nc.vector.tensor_tensor(out=sqf[gi][:, :], in0=pe_[:, :], in1=sqf[gi][:, :],
                                                                 op=ALU.mult), reads=[rpe_, R_sq[gi]],
                         writes=[R_sq[gi]])
                    xs = X[:, t, half * 512:(half + 1) * 512]
                    T.op('dve', lambda: nc.vector.tensor_tensor(out=xs, in0=xs, in1=sqf[gi][:, :], op=ALU.add),
                         reads=[R_sq[gi], R_X[t]], writes=[R_X[t]])
                w_done_last()
            w_done()
            if dbg and l == 0:
                T.barrier()
                for t in range(NT):
                    T.dma('sp', 'dbg', dbg_out['xG'][t * 128:(t + 1) * 128, :], X[:, t, :], reads=[R_X[t]])
                T.barrier()
            if stop == 'G':
                break

        if stop is None:
            R_g = R_gG
            T.dma('sp', 'c_g', gbc[:], gvec_in[DEPTH * 3:DEPTH * 3 + 1, :].partition_broadcast(128), writes=[R_g])
            R_ss = R_ssG
            for t in range(NT):
                T.op('act', lambda: nc.scalar.activation(out=junk[:], in_=X[:, t, :], func=AF.Square,
                                                         accum_out=ssq[:, t:t + 1]), reads=[R_X[t]], writes=[R_ss, R_junk])
            T.op('dve', lambda: nc.vector.tensor_scalar(out=rstd[:], in0=ssq[:], scalar1=1.0 / D, scalar2=EPS,
                                                         op0=ALU.mult, op1=ALU.add), reads=[R_ss], writes=[R_ss])
            T.op('act', lambda: nc.scalar.activation(out=rstd[:], in_=rstd[:], func=AF.Sqrt), reads=[R_ss],
                 writes=[R_ss])
            T.op('dve', lambda: nc.vector.reciprocal(out=rstd[:], in_=rstd[:]), reads=[R_ss], writes=[R_ss])
            for t in range(NT):
                T.op('dve', lambda: nc.vector.scalar_tensor_tensor(out=X[:, t, :], in0=X[:, t, :],
                                                                    scalar=rstd[:, t:t + 1], in1=gbc[:], op0=ALU.mult,
                                                                    op1=ALU.mult), reads=[R_X[t], R_ss, R_g],
                     writes=[R_X[t]])
                T.dma('sp', ('yo', t % 4), y_out[t * 128:(t + 1) * 128, :], X[:, t, :], reads=[R_X[t]])
        T.barrier()
    return nc


def _find_addr(nc, t):
    raise RuntimeError("cannot determine arena address; attrs=%s" % [a for a in dir(t) if not a.startswith('__')])


def _addr_of(nc, t, arena_addr, arena):
    for attr in ('addr', 'offset', 'address', 'base_addr', 'start_addr'):
        if hasattr(t, attr):
            v = getattr(t, attr)
            v = v() if callable(v) else v
            if isinstance(v, int):
                return v
    raise RuntimeError("no addr attr: %s" % [a for a in dir(t) if not a.startswith('__')])


def _prep_inputs(inp):
    inp = {k: np.asarray(v, dtype=np.float32) for k, v in inp.items()}
    ws = np.stack([np.concatenate(_layer_tiles(l, inp), axis=1) for l in range(DEPTH)], 0)
    cols = np.concatenate([_small_cols(l, inp) for l in range(DEPTH)], 1)
    gv = []
    for l in range(DEPTH):
        gv += [inp['g_mix'][l], inp['g_mlp'][l], inp['g_ple'][l]]
    gv.append(inp['g_final'])
    gvec = np.stack(gv, 0)
    bt = np.stack([_na_bias_table(inp['na_rpb'][l]).reshape(128, -1) for l in range(DEPTH)], 0)
    shared = {
        'ws': np.ascontiguousarray(ws), 'cols': np.ascontiguousarray(cols), 'gvec': np.ascontiguousarray(gvec),
        'gn': np.ascontiguousarray(inp['ret_gn']), 'bt': np.ascontiguousarray(bt),
        'ident': _CONSTS['ident'], 'cos': _CONSTS['cos'].reshape(128, -1), 'sin': _CONSTS['sin'].reshape(128, -1),
        'DT': _CONSTS['DT'].reshape(128, -1), 'QF': _CONSTS['QF'].reshape(128, -1),
        'QB': _CONSTS['QB'].reshape(128, -1), 'KD': _CONSTS['KD'].reshape(128, -1),
    }
    in_maps = []
    for b in range(8):
        m = dict(shared)
        m['x'] = np.ascontiguousarray(inp['x'][b])
        m['pT'] = np.ascontiguousarray(inp['p'][:, b].transpose(0, 2, 1))
        in_maps.append(m)
    return in_maps


def kernel(**inputs):
    in_maps = _prep_inputs(inputs)
    nc = build()
    res = run_bass_kernel_spmd(nc, in_maps, core_ids=list(range(8)))
    return np.stack([np.asarray(r['y'], dtype=np.float32) for r in res.results], 0)
```

```python
import contextlib
import numpy as np
import concourse.bass as bass
import concourse.mybir as mybir
from concourse.bass_utils import run_bass_kernel_spmd

F32 = mybir.dt.float32
BF16 = mybir.dt.bfloat16
AF = mybir.ActivationFunctionType
ALU = mybir.AluOpType

D = 1024
S = 2048
NT = 16
DEPTH = 2
EPS = 1e-6
NEG = -30000.0
WTILE = 4096
NSLOT = 3

OFF_RET = 1536
OFF_LRU = 3584
OFF_GATE = 4608
NCOLS_SMALL = 68


def _kmajor(W):
    K, N = W.shape
    return np.ascontiguousarray(W.reshape(K // 128, 128, N).transpose(1, 0, 2).reshape(128, -1))


def _ret_perm():
    idx = []
    for h in range(4):
        idx += [h * 128 + 2 * m for m in range(64)]
        idx += [h * 128 + 2 * m + 1 for m in range(64)]
    return np.array(idx)


def _layer_tiles(l, inp):
    w_in = inp['w_in'][l]
    tiles = []
    for ps in range(2):
        q = w_in[:, ps * 256:(ps + 1) * 256]
        k = w_in[:, 512 + ps * 256:512 + (ps + 1) * 256]
        tiles.append(_kmajor(np.concatenate([q, k], 1)))
        tiles.append(_kmajor(w_in[:, 1024 + ps * 256:1024 + (ps + 1) * 256]))
    perm = _ret_perm()
    qr = w_in[:, OFF_RET:OFF_RET + 512]
    kr = w_in[:, OFF_RET + 512:OFF_RET + 1024]
    vr = w_in[:, OFF_RET + 1024:OFF_RET + 1536]
    gr = w_in[:, OFF_RET + 1536:OFF_RET + 2048]
    tiles += [_kmajor(kr[:, perm]), _kmajor(vr), _kmajor(qr[:, perm]), _kmajor(gr)]
    xc = w_in[:, OFF_LRU:OFF_LRU + 512]
    gc = w_in[:, OFF_LRU + 512:OFF_LRU + 1024]
    for ch in range(4):
        a = _kmajor(np.concatenate([xc[:, ch * 128:(ch + 1) * 128], gc[:, ch * 128:(ch + 1) * 128]], 1))
        mats = np.zeros((128, 4, 128), np.float32)
        for mi, (nm, d) in enumerate([('lru_wa', 0), ('lru_wx', 0), ('lru_wa', 1), ('lru_wx', 1)]):
            w = inp[nm][l][d]
            mats[0:64, mi, 0:64] = w[2 * ch]
            mats[64:128, mi, 64:128] = w[2 * ch + 1]
        tiles.append(np.concatenate([a, mats.reshape(128, 512)], 1))
    for n in range(3):
        tiles.append(_kmajor(inp['w_branch'][l][n]))
        for half in range(2):
            c0 = OFF_GATE + n * 1024 + half * 512
            tiles.append(_kmajor(w_in[:, c0:c0 + 512]))
    for half in range(2):
        tiles.append(_kmajor(inp['w_out'][l][:, half * 512:(half + 1) * 512]))
    for s in range(4):
        for j in range(2):
            tiles.append(_kmajor(inp['w_up'][l][:, (2 * s + j) * 512:(2 * s + j + 1) * 512]))
        for j in range(2):
            tiles.append(_kmajor(inp['w_down'][l][(2 * s + j) * 512:(2 * s + j + 1) * 512, :]))
    tiles.append(_kmajor(inp['w_ple'][l]))
    for half in range(2):
        tiles.append(_kmajor(inp['w_ple_gate'][l][:, half * 512:(half + 1) * 512]))
    return tiles


def _tile_sizes():
    sz = []
    for ps in range(2):
        sz += [4096, 2048]
    sz += [4096] * 4
    sz += [2560] * 4
    for n in range(3):
        sz += [4096, 4096, 4096]
    sz += [4096, 4096]
    for s in range(4):
        sz += [4096] * 4
    sz += [2048, 4096, 4096]
    return sz


def _na_bias_table(rpb):
    ext = np.concatenate([rpb.reshape(8, 15 * 31), np.full((8, 1), NEG, np.float32)], 1)
    p = np.arange(128)
    b = p // 64
    kc = p % 64
    d = np.arange(14)
    c = np.arange(64)
    cs = np.clip(c - 8, 0, 48)
    row = d[None, :, None] + b[:, None, None]
    col = kc[:, None, None] - c[None, None, :] + 15
    valid = (kc[:, None, None] >= cs[None, None, :]) & (kc[:, None, None] < cs[None, None, :] + 16)
    flat = np.where(valid, row * 31 + np.clip(col, 0, 30), 15 * 31)
    out = ext[:, flat]
    return np.ascontiguousarray(out.transpose(1, 0, 2, 3)).astype(np.float32)


def _consts():
    f8 = np.float64
    c = {}
    c['ident'] = np.eye(128, dtype=np.float32)
    pos = np.arange(S, dtype=np.float32)
    theta = (1.0 / (np.float32(10000.0) ** np.linspace(0.0, 1.0, 64, dtype=np.float32))).astype(np.float32)
    ang = (pos[:, None] * theta[None, :]).astype(np.float32)
    cos = np.cos(ang.astype(f8)).astype(np.float32).reshape(16, 128, 64).transpose(1, 0, 2)
    sin = np.sin(ang.astype(f8)).astype(np.float32).reshape(16, 128, 64).transpose(1, 0, 2)
    c['cos'] = np.ascontiguousarray(cos)
    c['sin'] = np.ascontiguousarray(sin)
    hidx = np.arange(4, dtype=f8)
    lgf = np.log1p(-np.exp2(-5.0 - hidx))
    lgb = np.log1p(-np.exp2(-5.5 - hidx))
    i = np.arange(128, dtype=f8)
    scale = 128.0 ** -0.5
    diff = i[None, :] - i[:, None]
    DT = np.zeros((128, 4, 128), f8)
    for h in range(4):
        DT[:, h, :] = np.where(diff >= 0, np.exp(lgf[h] * np.maximum(diff, 0)),
                               np.exp(lgb[h] * np.maximum(-diff, 0))) * scale
    c['DT'] = DT.astype(np.float32)
    QF = np.zeros((128, 4, 128), f8)
    QB = np.zeros((128, 4, 128), f8)
    for h in range(4):
        QF[:, h, :] = (np.exp(lgf[h] * (i + 1.0)) * scale)[None, :]
        QB[:, h, :] = (np.exp(lgb[h] * (128.0 - i)) * scale)[None, :]
    c['QF'] = QF.astype(np.float32)
    c['QB'] = QB.astype(np.float32)
    KD = np.zeros((128, 2, 4), f8)
    for h in range(4):
        KD[:, 0, h] = np.exp(lgf[h] * (127.0 - i))
        KD[:, 1, h] = np.exp(lgb[h] * i)
    c['KD'] = KD.astype(np.float32)
    cdf = [float(np.exp(lgf[h] * 128.0)) for h in range(4)]
    cdb = [float(np.exp(lgb[h] * 128.0)) for h in range(4)]
    return c, cdf, cdb


_CONSTS, _CDF, _CDB = _consts()


def _small_cols(l, inp):
    out = np.zeros((128, NCOLS_SMALL), np.float32)
    out[:, 0:24] = inp['b_gate'][l].reshape(3, 8, 128).transpose(2, 0, 1).reshape(128, 24)
    out[:, 24:40] = inp['conv_w'][l].reshape(4, 4, 128).transpose(2, 1, 0).reshape(128, 16)
    out[:, 40:44] = inp['conv_b'][l].reshape(4, 128).T
    out[:, 44:52] = inp['lru_ba'][l].reshape(2, 4, 128).transpose(2, 0, 1).reshape(128, 8)
    out[:, 52:60] = inp['lru_bx'][l].reshape(2, 4, 128).transpose(2, 0, 1).reshape(128, 8)
    out[:, 60:68] = inp['lru_lambda'][l].reshape(2, 4, 128).transpose(2, 0, 1).reshape(128, 8)
    return out


class Res:
    __slots__ = ('w', 'r', 'name')

    def __init__(self, name=''):
        self.w = None
        self.r = {}
        self.name = name


class Trk:
    def __init__(self, nc, stack):
        self.nc = nc
        self.stack = stack
        self.eng = {'pe': nc.tensor, 'act': nc.scalar, 'dve': nc.vector, 'pool': nc.gpsimd, 'sp': nc.sync}
        self.sem = {}
        self.cnt = {}
        for e in ['pe', 'act', 'dve', 'pool']:
            self.sem[e] = stack.enter_context(nc.semaphore('s_' + e))
            self.cnt[e] = 0
        self.dsem = {}
        self.dcnt = {}
        self.waited = {e: {} for e in self.eng}

    def _dsem(self, key):
        if key not in self.dsem:
            self.dsem[key] = self.stack.enter_context(self.nc.semaphore('d_%d' % len(self.dsem)))
            self.dcnt[key] = 0
        return self.dsem[key]

    def _wait(self, e, deps, raw=()):
        need = {}
        for d, is_raw in [(d, False) for d in deps] + [(d, True) for d in raw]:
            if d is None:
                continue
            kind, key, v = d
            if kind == 'e' and key == e and e == 'pe':
                continue
            k = (kind, key)
            if self.waited[e].get(k, 0) >= v:
                continue
            if need.get(k, 0) < v:
                need[k] = v
        for (kind, key), v in need.items():
            sem = self.sem[key] if kind == 'e' else self.dsem[key]
            self.eng[e].wait_ge(sem, v)
            self.waited[e][(kind, key)] = v

    def _deps(self, reads, writes):
        deps = []
        for w in writes:
            deps.append(w.w)
            for (kind, key), v in w.r.items():
                deps.append((kind, key, v))
        return deps

    def _commit(self, tok, reads, writes):
        k = (tok[0], tok[1])
        for r in reads:
            r.r[k] = tok[2]
        for w in writes:
            w.w = tok
            w.r = {}

    def op(self, e, fn, reads=(), writes=()):
        self._wait(e, self._deps(reads, writes), raw=[r.w for r in reads])
        inst = fn()
        self.cnt[e] += 1
        inst.then_inc(self.sem[e], 1)
        self._commit(('e', e, self.cnt[e]), reads, writes)

    def dma(self, q, key, out, in_, reads=(), writes=()):
        self._wait(q, self._deps(reads, writes), raw=[r.w for r in reads])
        sem = self._dsem(key)
        inst = self.eng[q].dma_start(out=out, in_=in_)
        self.dcnt[key] += 16
        inst.then_inc(sem, 16)
        self._commit(('d', key, self.dcnt[key]), reads, writes)

    def barrier(self, engines=('pe', 'act', 'dve', 'pool', 'sp')):
        for e in engines:
            deps = [('e', f, self.cnt[f]) for f in self.cnt if f != e and self.cnt[f] > 0]
            deps += [('d', k, v) for k, v in self.dcnt.items() if v > 0]
            self._wait(e, deps)

    def wait_all(self, e):
        deps = [('e', f, self.cnt[f]) for f in self.cnt if f != e and self.cnt[f] > 0]
        deps += [('d', k, v) for k, v in self.dcnt.items() if v > 0]
        self._wait(e, deps)


def build(depth=DEPTH, stop=None, dbg=False):
    nc = bass.Bass("TRN2", target_bir_lowering=False)
    sizes = _tile_sizes()
    TOT = sum(sizes)
    offs = np.concatenate([[0], np.cumsum(sizes)]).astype(int)
    NTILE = len(sizes)

    x_in = nc.dram_tensor("x", [S, D], F32, kind="ExternalInput").ap()
    pT_in = nc.dram_tensor("pT", [DEPTH, 256, S], F32, kind="ExternalInput").ap()
    ws_in = nc.dram_tensor("ws", [DEPTH, 128, TOT], F32, kind="ExternalInput").ap()
    cols_in = nc.dram_tensor("cols", [128, DEPTH * NCOLS_SMALL], F32, kind="ExternalInput").ap()
    gvec_in = nc.dram_tensor("gvec", [DEPTH * 3 + 1, D], F32, kind="ExternalInput").ap()
    gn_in = nc.dram_tensor("gn", [DEPTH, 512], F32, kind="ExternalInput").ap()
    bt_in = nc.dram_tensor("bt", [DEPTH, 128, 8 * 14 * 64], F32, kind="ExternalInput").ap()
    ident_in = nc.dram_tensor("ident", [128, 128], F32, kind="ExternalInput").ap()
    cos_in = nc.dram_tensor("cos", [128, 16 * 64], F32, kind="ExternalInput").ap()
    sin_in = nc.dram_tensor("sin", [128, 16 * 64], F32, kind="ExternalInput").ap()
    dt_in = nc.dram_tensor("DT", [128, 512], F32, kind="ExternalInput").ap()
    qf_in = nc.dram_tensor("QF", [128, 512], F32, kind="ExternalInput").ap()
    qb_in = nc.dram_tensor("QB", [128, 512], F32, kind="ExternalInput").ap()
    kd_in = nc.dram_tensor("KD", [128, 8], F32, kind="ExternalInput").ap()
    y_out = nc.dram_tensor("y", [S, D], F32, kind="ExternalOutput").ap()
    xd = nc.dram_tensor("xd", [S, D], F32).ap()
    dbg_out = {}
    if dbg:
        for nm in ['hT', 'ynaT', 'yretT', 'ylruT']:
            dbg_out[nm] = nc.dram_tensor("dbg_" + nm, [128, 8 * S if nm == 'hT' else 4 * S], BF16,
                                         kind="ExternalOutput").ap()
        for nm in ['xE', 'xF', 'xG']:
            dbg_out[nm] = nc.dram_tensor("dbg_" + nm, [S, D], F32, kind="ExternalOutput").ap()
        dbg_out['qT'] = nc.dram_tensor("dbg_qT", [128, 2 * S], BF16, kind="ExternalOutput").ap()
        dbg_out['kT'] = nc.dram_tensor("dbg_kT", [128, 2 * S], BF16, kind="ExternalOutput").ap()
        dbg_out['Ve'] = nc.dram_tensor("dbg_Ve", [128, 16 * 4 * 65], BF16, kind="ExternalOutput").ap()
        dbg_out['Vo'] = nc.dram_tensor("dbg_Vo", [128, 16 * 4 * 65], BF16, kind="ExternalOutput").ap()
        dbg_out['bt'] = nc.dram_tensor("dbg_bt", [128, 8 * 14 * 64], BF16, kind="ExternalOutput").ap()
        for nm in ['krot', 'vtm', 'Sb']:
            dbg_out[nm] = nc.dram_tensor("dbg_" + nm, [128, 16 * 512], BF16, kind="ExternalOutput").ap()
        for nm in ['qTc', 'kTc', 'pTr', 'qfT', 'qbT', 'yrt']:
            dbg_out[nm] = nc.dram_tensor("dbg_" + nm, [128, 512], BF16, kind="ExternalOutput").ap()
        for nm in ['sge', 'o1']:
            dbg_out[nm] = nc.dram_tensor("dbg_" + nm, [128, 512], F32, kind="ExternalOutput").ap()

    stack = contextlib.ExitStack()
    with stack:
        T = Trk(nc, stack)

        sb_state = {'off': (int(nc.sbuf_base) + 63) // 64 * 64, 'n': 0}
        sb_addr = {}

        def sb(name, shape, dt):
            nbytes = int(np.prod(shape[1:])) * (2 if dt == BF16 else 4)
            nbytes = (nbytes + 63) // 64 * 64
            addr = sb_state['off']
            sb_state['off'] += nbytes
            assert sb_state['off'] <= int(nc.sbuf_top), (name, sb_state['off'], int(nc.sbuf_top))
            h = nc.alloc_sbuf_tensor_at(name, shape, dt, offset=addr)
            sb_addr[name] = addr
            return h

        ARENA_BYTES = 86 * 1024
        arena = sb("arena", [128, ARENA_BYTES // 4], F32)
        ARENA_ADDR = sb_addr["arena"]
        hT = sb("hT", [128, 8, S], BF16)
        ysT = [sb("ysT%d" % i, [128, 4, S], BF16) for i in range(3)]
        YS_ADDR = sb_addr["ysT0"]
        wbuf = [sb("wbuf%d" % i, [128, WTILE], BF16) for i in range(NSLOT)]
        ident = sb("ident", [128, 128], BF16)
        colp = sb("colp", [128, DEPTH * NCOLS_SMALL], F32)
        gbc = sb("gbc", [128, D], F32)
        ssq = sb("ssq", [128, 16], F32)
        rstd = sb("rstd", [128, 16], F32)
        hn = [sb("hn%d" % i, [128, D], BF16) for i in range(2)]
        junk = sb("junk", [128, D], BF16)
        kdt = sb("kdt", [128, 8], F32)
        sqf = [sb("sqf%d" % i, [128, 512], F32) for i in range(2)]
        lru_c = sb("lru_c", [128, DEPTH * 8], F32)
        mhalf = sb("mhalf", [128, 4], F32)
        lru_2c = sb("lru_2c", [128, DEPTH * 8], F32)

        class Carver:
            def __init__(self):
                self.off = 0
                self.n = 0

            def reset(self):
                self.off = 0

            def get(self, name, shape, dt):
                nbytes = int(np.prod(shape[1:])) * (2 if dt == BF16 else 4)
                nbytes = (nbytes + 63) // 64 * 64
                assert self.off + nbytes <= ARENA_BYTES, (name, self.off, nbytes)
                self.n += 1
                h = nc.alloc_sbuf_tensor_at("%s_%d" % (name, self.n), shape, dt, offset=ARENA_ADDR + self.off)
                self.off += nbytes
                return h

        CV = Carver()

        X = nc.alloc_sbuf_tensor_at("Xres", [128, NT, D], F32, offset=ARENA_ADDR)
        R_X = [Res('X%d' % t) for t in range(NT)]
        R_hT = [Res('hT%d' % t) for t in range(NT)]
        R_ys = [[Res('ys%d_%d' % (i, t)) for t in range(NT)] for i in range(3)]
        R_xd = [Res('xd%d' % t) for t in range(NT)]
        R_misc = Res('misc')
        R_junk = Res('junk')
        R_gG = Res('gbc')
        R_ssG = Res('ssq')
        R_hnG = [Res('hn0'), Res('hn1')]

        pbank = [stack.enter_context(nc.psum_tensor("pb%d" % i, [128, 512], F32)) for i in range(6)]
        tbank = [stack.enter_context(nc.psum_tensor("tb%d" % i, [128, 1024], BF16)) for i in range(2)]
        R_pb = [Res('pb%d' % i) for i in range(6)]
        R_tb = [Res('tb%d' % i) for i in range(2)]
        pctr = [0, 0]

        def ps_next():
            i = pctr[0] % 6
            pctr[0] += 1
            return pbank[i], R_pb[i]

        def tb_next():
            i = pctr[1] % 2
            pctr[1] += 1
            return tbank[i], R_tb[i]

        R_w = [Res('w%d' % i) for i in range(NSLOT)]
        wstate = {'issued': 0, 'taken': 0}
        total_tiles = depth * NTILE

        wdone = set()

        def w_issue_ready():
            while wstate['issued'] < total_tiles and wstate['issued'] < wstate['released'] + NSLOT:
                g = wstate['issued']
                l, i = divmod(g, NTILE)
                slot = g % NSLOT
                n = sizes[i]
                bsz = 2048 if n % 2048 == 0 else 512
                T.dma('pool', ('w', slot), wbuf[slot][:, 0:n].rearrange("p (a b) -> p a b", b=bsz),
                      ws_in[l, :, int(offs[i]):int(offs[i]) + n].rearrange("p (a b) -> p a b", b=bsz),
                      writes=[R_w[slot]])
                wstate['issued'] += 1

        def w_next(expect_size):
            g = wstate['taken']
            l, i = divmod(g, NTILE)
            assert sizes[i] == expect_size, (g, i, sizes[i], expect_size)
            assert g < wstate['issued'], "weight tile not issued (too many held)"
            slot = g % NSLOT
            wstate['taken'] += 1
            wheld.append(g)
            return wbuf[slot], R_w[slot]

        def w_done(k=1):
            for _ in range(k):
                g = wheld.pop(0)
                wdone.add(g)
            while wstate['released'] in wdone:
                wdone.discard(wstate['released'])
                wstate['released'] += 1
            w_issue_ready()

        def w_done_last():
            g = wheld.pop()
            wdone.add(g)
            while wstate['released'] in wdone:
                wdone.discard(wstate['released'])
                wstate['released'] += 1
            w_issue_ready()

        wheld = []
        wstate['released'] = 0

        T.dma('pool', 'c_setup', ident[:], ident_in, writes=[R_misc])
        T.dma('sp', 'c_setup', colp[:], cols_in, writes=[R_misc])
        T.dma('sp', 'c_setup', kdt[:], kd_in, writes=[R_misc])
        w_issue_ready()
        for t in range(NT):
            T.dma('sp', ('x', t % 4), X[:, t, :], x_in[t * 128:(t + 1) * 128, :], writes=[R_X[t]])

        R_lc = Res('lruc')
        R_mh = Res('mhalf')
        T.op('act', lambda: nc.scalar.activation(out=mhalf[:, :], in_=ident[:, 0:4], func=AF.Copy, scale=0.0, bias=-0.5),
             reads=[R_misc], writes=[R_mh])
        for l in range(depth):
            src = colp[:, l * NCOLS_SMALL + 60:l * NCOLS_SMALL + 68]
            dst = lru_c[:, l * 8:(l + 1) * 8]
            T.op('act', lambda: nc.scalar.activation(out=dst, in_=src, func=AF.Exp, scale=-1.0),
                 reads=[R_misc], writes=[R_lc])
            T.op('act', lambda: nc.scalar.activation(out=dst, in_=dst, func=AF.Ln, bias=1.0),
                 reads=[R_lc], writes=[R_lc])
            T.op('dve', lambda: nc.vector.tensor_scalar(out=lru_2c[:, l * 8:(l + 1) * 8], in0=dst, scalar1=-16.0,
                                                         scalar2=None, op0=ALU.mult), reads=[R_lc], writes=[R_lc])
            T.op('dve', lambda: nc.vector.tensor_scalar(out=dst, in0=dst, scalar1=-8.0, scalar2=None, op0=ALU.mult),
                 reads=[R_lc], writes=[R_lc])

        def rmsnorm_to_hT(grow, spill):
            R_g = R_gG
            T.dma('sp', 'c_g', gbc[:], gvec_in[grow:grow + 1, :].partition_broadcast(128), writes=[R_g],
                  reads=[])
            R_ss = R_ssG
            for t in range(NT):
                T.op('act', lambda: nc.scalar.activation(out=junk[:], in_=X[:, t, :], func=AF.Square,
                                                         accum_out=ssq[:, t:t + 1]),
                     reads=[R_X[t]], writes=[R_ss, R_junk])
                if spill:
                    T.dma('sp', ('xs', t % 4), xd[t * 128:(t + 1) * 128, :], X[:, t, :], reads=[R_X[t]],
                          writes=[R_xd[t]])
            T.op('dve', lambda: nc.vector.tensor_scalar(out=rstd[:], in0=ssq[:], scalar1=1.0 / D, scalar2=EPS,
                                                         op0=ALU.mult, op1=ALU.add), reads=[R_ss], writes=[R_ss])
            T.op('act', lambda: nc.scalar.activation(out=rstd[:], in_=rstd[:], func=AF.Sqrt), reads=[R_ss],
                 writes=[R_ss])
            T.op('dve', lambda: nc.vector.reciprocal(out=rstd[:], in_=rstd[:]), reads=[R_ss], writes=[R_ss])
            R_hn = R_hnG
            for t in range(NT):
                b = t % 2
                T.op('dve', lambda: nc.vector.scalar_tensor_tensor(out=hn[b][:], in0=X[:, t, :],
                                                                    scalar=rstd[:, t:t + 1], in1=gbc[:],
                                                                    op0=ALU.mult, op1=ALU.mult),
                     reads=[R_X[t], R_ss, R_g], writes=[R_hn[b]])
                tb, rtb = tb_next()
                for kc in range(8):
                    T.op('pe', lambda: nc.tensor.transpose(out=tb[:, kc * 128:(kc + 1) * 128],
                                                           in_=hn[b][:, kc * 128:(kc + 1) * 128], identity=ident[:]),
                         reads=[R_hn[b], R_misc], writes=[rtb])
                if t % 2 == 0:
                    T.op('act', lambda: nc.scalar.copy(out=hT[:, :, t * 128:(t + 1) * 128],
                                                       in_=tb[:, :].rearrange("p (k n) -> p k n", k=8)),
                         reads=[rtb], writes=[R_hT[t]])
                else:
                    T.op('dve', lambda: nc.vector.tensor_copy(out=hT[:, :, t * 128:(t + 1) * 128],
                                                               in_=tb[:, :].rearrange("p (k n) -> p k n", k=8)),
                         reads=[rtb], writes=[R_hT[t]])

        def transpose_tm_to_fm(src_tile, R_src, ncols, dstT, kc0, t, R_dst):
            nk = ncols // 128
            tb, rtb = tb_next()
            for j in range(nk):
                T.op('pe', lambda: nc.tensor.transpose(out=tb[:, j * 128:(j + 1) * 128],
                                                       in_=src_tile[:, j * 128:(j + 1) * 128], identity=ident[:]),
                     reads=[R_src, R_misc], writes=[rtb])
            T.op('act', lambda: nc.scalar.copy(out=dstT[:, kc0:kc0 + nk, t * 128:(t + 1) * 128],
                                               in_=tb[:, 0:nk * 128].rearrange("p (k n) -> p k n", k=nk)),
                 reads=[rtb], writes=[R_dst])

        def dump(name, src_ap, reads):
            if dbg and name in dbg_out:
                T.barrier()
                T.dma('sp', 'dbg', dbg_out[name], src_ap, reads=reads)
                T.barrier()

        for l in range(depth):
            cb = l * NCOLS_SMALL

            rmsnorm_to_hT(l * 3 + 0, spill=True)
            T.barrier(engines=('act', 'dve', 'pool', 'sp'))
            if dbg and l == 0:
                dump('hT', hT[:].rearrange("p k n -> p (k n)"), R_hT)
            if stop == 'A':
                break

            CV.reset()
            btab = CV.get("btab", [128, 8, 14, 64], BF16)
            R_bt = Res('bt')
            T.dma('pool', 'c_bt', btab[:].rearrange("p a b c -> p a (b c)"),
                  bt_in[l].rearrange("p (a b) -> p a b", a=8), writes=[R_bt])
            qT = CV.get("qT", [128, 2, S], BF16)
            kT = CV.get("kT", [128, 2, S], BF16)
            Ve = CV.get("Ve", [128, 16, 4, 65], BF16)
            Vo = CV.get("Vo", [128, 16, 4, 65], BF16)
            NSC = 4
            scb = [CV.get("sc%d" % i, [128, 512], F32) for i in range(NSC)]
            ptb = [CV.get("pt%d" % i, [128, 512], BF16) for i in range(NSC)]
            ytile = [CV.get("yt%d" % i, [128, 256], BF16) for i in range(2)]
            rden = [CV.get("rd%d" % i, [128, 4], F32) for i in range(2)]
            R_sc = [Res() for _ in range(NSC)]
            R_pt = [Res() for _ in range(NSC)]
            R_yt = [Res(), Res()]
            R_rd = [Res(), Res()]
            R_q = [Res() for _ in range(4)]
            R_k = [Res() for _ in range(4)]
            R_ve = [Res() for _ in range(NT)]
            R_vo = [Res() for _ in range(NT)]
            R_ones = Res('ones')
            for Vt_ in (Ve, Vo):
                T.op('act', lambda: nc.scalar.activation(out=Vt_[:, :, :, 64],
                                                         in_=ident[:, 0:64].rearrange("p (a b) -> p a b", a=16),
                                                         func=AF.Copy, scale=0.0, bias=1.0),
                     reads=[R_misc], writes=[R_ones])
            sctr = 0
            for ps in range(2):
                wqk, rwqk = w_next(4096)
                wqk3 = wqk[:, :].rearrange("p (k n) -> p k n", k=8)
                for which, dst, Rd in ((0, qT, R_q), (1, kT, R_k)):
                    for hp in range(2):
                        for tg in range(4):
                            pb, rpb = ps_next()
                            for kc in range(8):
                                c0 = which * 256 + hp * 128
                                T.op('pe', lambda: nc.tensor.matmul(pb[:, :], lhsT=wqk3[:, kc, c0:c0 + 128],
                                                                    rhs=hT[:, kc, tg * 512:(tg + 1) * 512],
                                                                    start=(kc == 0), stop=(kc == 7)),
                                     reads=[rwqk] + R_hT[tg * 4:tg * 4 + 4], writes=[rpb])
                            sc_ = 0.125 if which == 0 else 1.0
                            T.op('act', lambda: nc.scalar.activation(out=dst[:, hp, tg * 512:(tg + 1) * 512],
                                                                     in_=pb[:, :], func=AF.Copy, scale=sc_),
                                 reads=[rpb], writes=[Rd[tg]])
                w_done()
                wv, rwv = w_next(2048)
                wv3 = wv[:, 0:2048].rearrange("p (k n) -> p k n", k=8)
                for par, Vt, Rv, ntl in ((0, Ve, R_ve, 16), (1, Vo, R_vo, 15)):
                    for t in range(ntl):
                        tok0 = t * 128 + par * 64
                        pb, rpb = ps_next()
                        rr = R_hT[t:t + 2] if par else R_hT[t:t + 1]
                        for kc in range(8):
                            T.op('pe', lambda: nc.tensor.matmul(pb[:, 0:256], lhsT=hT[:, kc, tok0:tok0 + 128],
                                                                rhs=wv3[:, kc, :], start=(kc == 0), stop=(kc == 7)),
                                 reads=[rwv] + rr, writes=[rpb])
                        T.op('act', lambda: nc.scalar.copy(out=Vt[:, t, :, 0:64],
                                                           in_=pb[:, 0:256].rearrange("p (h d) -> p h d", h=4)),
                             reads=[rpb, R_ones], writes=[Rv[t]])
                w_done()
                steps = [(t, a, hh) for t in range(NT) for a in range(2) for hh in range(2)]
                front = {}

                def na_front(sidx):
                    t, a, hh = steps[sidx]
                    r = 2 * t + a
                    rs = min(max(r - 4, 0), 24)
                    d0 = rs - r + 7
                    pb, rpb = ps_next()
                    for hp in range(2):
                        for i in range(4):
                            k0 = (rs + 2 * i) * 64
                            T.op('pe', lambda: nc.tensor.matmul(
                                pb[:, (hp * 4 + i) * 64:(hp * 4 + i + 1) * 64],
                                lhsT=kT[hh * 64:(hh + 1) * 64, hp, k0:k0 + 128],
                                rhs=qT[hh * 64:(hh + 1) * 64, hp, r * 64:(r + 1) * 64],
                                start=True, stop=True),
                                reads=[R_k[k0 // 512], R_k[(k0 + 127) // 512], R_q[r // 8]], writes=[rpb])
                    si = sidx % NSC
                    hg = ps * 4 + hh
                    T.op('dve', lambda: nc.vector.tensor_tensor(
                        out=scb[si][:, :].rearrange("p (h i c) -> p h i c", h=2, i=4),
                        in0=pb[:, :].rearrange("p (h i c) -> p h i c", h=2, i=4),
                        in1=btab[:, hg:hg + 3:2, d0:d0 + 7:2, :], op=ALU.add),
                        reads=[rpb, R_bt], writes=[R_sc[si]])
                    T.op('act', lambda: nc.scalar.activation(out=ptb[si][:, :], in_=scb[si][:, :], func=AF.Exp),
                         reads=[R_sc[si]], writes=[R_pt[si]])

                obs = {}

                def na_back(sidx):
                    t, a, hh = steps[sidx]
                    r = 2 * t + a
                    rs = min(max(r - 4, 0), 24)
                    si = sidx % NSC
                    if (a, hh) == (0, 0):
                        obs[t] = ps_next()
                    ob, rob = obs[t]
                    for hp in range(2):
                        hl = hp * 2 + hh
                        for i in range(4):
                            kr = rs + 2 * i
                            if rs % 2 == 0:
                                Vt, Rv, tl = Ve, R_ve, kr // 2
                            else:
                                Vt, Rv, tl = Vo, R_vo, (kr - 1) // 2
                            T.op('pe', lambda: nc.tensor.matmul(
                                ob[a * 64:(a + 1) * 64, hl * 65:(hl + 1) * 65],
                                lhsT=ptb[si][:, (hp * 4 + i) * 64:(hp * 4 + i + 1) * 64],
                                rhs=Vt[:, tl, hl, :], start=(i == 0), stop=(i == 3)),
                                reads=[R_pt[si], Rv[tl]], writes=[rob])
                    if (a, hh) == (1, 1):
                        yb = t % 2
                        ob3 = ob[:, 0:260].rearrange("p (h d) -> p h d", h=4)
                        T.op('dve', lambda: nc.vector.reciprocal(out=rden[yb][:, :], in_=ob3[:, :, 64]),
                             reads=[rob], writes=[R_rd[yb]])
                        T.op('dve', lambda: nc.vector.tensor_tensor(
                            out=ytile[yb][:, :].rearrange("p (h d) -> p h d", h=4), in0=ob3[:, :, 0:64],
                            in1=rden[yb][:, :].unsqueeze(2).to_broadcast([128, 4, 64]), op=ALU.mult),
                            reads=[rob, R_rd[yb]], writes=[R_yt[yb]])
                        pend_tr.append(t)

                pend_tr = []

                def na_flush():
                    while pend_tr:
                        t_ = pend_tr.pop(0)
                        transpose_tm_to_fm(ytile[t_ % 2], R_yt[t_ % 2], 256, ysT[0], ps * 2, t_, R_ys[0][t_])

                GRP = 2
                ngrp = len(steps) // GRP
                for k_ in range(GRP):
                    na_front(k_)
                for g_ in range(ngrp):
                    if g_ + 1 < ngrp:
                        for k_ in range(GRP):
                            na_front((g_ + 1) * GRP + k_)
                    if steps[g_ * GRP][1:] == (1, 0):
                        na_flush()
                    for k_ in range(GRP):
                        na_back(g_ * GRP + k_)
                na_flush()
            T.barrier(engines=('act', 'dve', 'pool', 'sp'))
            if dbg and l == 0:
                dump('ynaT', ysT[0][:].rearrange("p k n -> p (k n)"), R_ys[0])
                dump('qT', qT[:].rearrange("p k n -> p (k n)"), [])
                dump('kT', kT[:].rearrange("p k n -> p (k n)"), [])
                dump('Ve', Ve[:].rearrange("p a b c -> p (a b c)"), [])
                dump('Vo', Vo[:].rearrange("p a b c -> p (a b c)"), [])
                dump('bt', btab[:].rearrange("p a b c -> p (a b c)"), [])
            if stop == 'B':
                break

            CV.reset()
            cosT = CV.get("cos", [128, 16, 64], F32)
            sinT = CV.get("sin", [128, 16, 64], F32)
            DTt = CV.get("DT", [128, 4, 128], F32)
            QFt = CV.get("QF", [128, 4, 128], BF16)
            QBt = CV.get("QB", [128, 4, 128], BF16)
            gnb = gbc
            R_ct = Res('ctab')
            T.dma('sp', 'c_cos', cosT[:].rearrange("p a b -> p (a b)"), cos_in, writes=[R_ct])
            T.dma('sp', 'c_sin', sinT[:].rearrange("p a b -> p (a b)"), sin_in, writes=[R_ct])
            T.dma('sp', 'c_dt', DTt[:].rearrange("p a b -> p (a b)"), dt_in, writes=[R_ct])
            T.dma('pool', 'c_qf', QFt[:].rearrange("p a b -> p (a b)"), qf_in, writes=[R_ct])
            T.dma('pool', 'c_qb', QBt[:].rearrange("p a b -> p (a b)"), qb_in, writes=[R_ct])
            T.dma('sp', 'c_gn', gnb[:, 0:512], gn_in[l:l + 1, :].partition_broadcast(128), writes=[R_ct])
            krot = CV.get("krot", [128, NT, 512], BF16)
            vtm = CV.get("vtm", [128, NT, 512], BF16)
            Sb = CV.get("Sb", [128, NT, 512], BF16)
            Sf32 = CV.get("Sf32", [128, 512], F32)
            Sfb = [CV.get("Sfb%d" % i, [128, 512], BF16) for i in range(2)]
            rt1 = sqf[0]
            rt2 = sqf[1]
            kd = [CV.get("kd%d" % i, [128, 512], BF16) for i in range(2)]
            qrot = [CV.get("qrot%d" % i, [128, 512], BF16) for i in range(2)]
            qTc = [CV.get("qTc%d" % i, [128, 4, 128], BF16) for i in range(2)]
            kTc = [CV.get("kTc%d" % i, [128, 4, 128], BF16) for i in range(2)]
            qfT = [CV.get("qfT%d" % i, [128, 4, 128], BF16) for i in range(2)]
            qbT = [CV.get("qbT%d" % i, [128, 4, 128], BF16) for i in range(2)]
            pTr = [CV.get("pTr%d" % i, [128, 4, 128], BF16) for i in range(2)]
            sge = [CV.get("sge%d" % i, [128, 512], F32) for i in range(2)]
            o1 = [nc.alloc_sbuf_tensor_at("o1_%d_%d" % (l, i), [128, 512], F32, offset=sb_addr["hn%d" % i])
                  for i in range(2)]
            yrt = [CV.get("yrt%d" % i, [128, 512], BF16) for i in range(2)]
            gss = [ssq[:, 0:4], ssq[:, 4:8]]
            R_kr = [Res() for _ in range(NT)]
            R_v = [Res() for _ in range(NT)]
            R_Sb = [Res() for _ in range(NT)]
            R_S32 = Res()
            R_Sfb = [Res(), Res()]
            R_rt = Res()
            R_kd = [Res(), Res()]
            R_qrot = [Res(), Res()]
            R_qTc = [Res(), Res()]
            R_kTc = [Res(), Res()]
            R_qfT = [Res(), Res()]
            R_qbT = [Res(), Res()]
            R_pTr = [Res(), Res()]
            R_sge = [Res(), Res()]
            R_o1 = [Res(), Res()]
            R_yrt = [Res(), Res()]
            R_gss = [Res(), Res()]

            def rotary(pb, rpb, t, dst, R_dst):
                p4 = pb[:, :].rearrange("p (h two m) -> p h two m", h=4, two=2)
                d4 = dst.rearrange("p (h two m) -> p h two m", h=4, two=2)
                a4 = rt1[:, :].rearrange("p (h two m) -> p h two m", h=4, two=2)
                b4 = rt2[:, :].rearrange("p (h two m) -> p h two m", h=4, two=2)
                cb_ = cosT[:, t, :].unsqueeze(1).unsqueeze(1).to_broadcast([128, 4, 2, 64])
                sb_ = sinT[:, t, :].unsqueeze(1).unsqueeze(1).to_broadcast([128, 4, 2, 64])
                T.op('dve', lambda: nc.vector.tensor_tensor(out=a4, in0=p4, in1=cb_, op=ALU.mult),
                     reads=[rpb, R_ct], writes=[R_rt])
                T.op('dve', lambda: nc.vector.tensor_tensor(out=b4, in0=p4[:, :, ::-1, :], in1=sb_, op=ALU.mult),
                     reads=[rpb, R_ct], writes=[R_rt])
                T.op('dve', lambda: nc.vector.tensor_tensor(out=d4[:, :, 0, :], in0=a4[:, :, 0, :], in1=b4[:, :, 0, :],
                                                             op=ALU.subtract), reads=[R_rt], writes=[R_dst])
                T.op('dve', lambda: nc.vector.tensor_tensor(out=d4[:, :, 1, :], in0=a4[:, :, 1, :], in1=b4[:, :, 1, :],
                                                             op=ALU.add), reads=[R_rt], writes=[R_dst])

            def proj_tm(w3, rw, t):
                pb, rpb = ps_next()
                for kc in range(8):
                    T.op('pe', lambda: nc.tensor.matmul(pb[:, :], lhsT=hT[:, kc, t * 128:(t + 1) * 128],
                                                        rhs=w3[:, kc, :], start=(kc == 0), stop=(kc == 7)),
                         reads=[rw, R_hT[t]], writes=[rpb])
                return pb, rpb

            wk, rwk = w_next(4096)
            wk3 = wk[:, :].rearrange("p (k n) -> p k n", k=8)
            wv_, rwv_ = w_next(4096)
            wv3_ = wv_[:, :].rearrange("p (k n) -> p k n", k=8)
            for t in range(NT):
                pb, rpb = proj_tm(wk3, rwk, t)
                rotary(pb, rpb, t, krot[:, t, :], R_kr[t])
                pb, rpb = proj_tm(wv3_, rwv_, t)
                T.op('act', lambda: nc.scalar.copy(out=vtm[:, t, :], in_=pb[:, :]), reads=[rpb], writes=[R_v[t]])
            w_done(2)

            def state_u(c, dirn):
                b = c % 2
                T.op('dve', lambda: nc.vector.tensor_tensor(
                    out=kd[b][:, :].rearrange("p (h d) -> p h d", h=4),
                    in0=krot[:, c, :].rearrange("p (h d) -> p h d", h=4),
                    in1=kdt[:, dirn * 4:dirn * 4 + 4].unsqueeze(2).to_broadcast([128, 4, 128]), op=ALU.mult),
                    reads=[R_kr[c], R_misc], writes=[R_kd[b]])
                pb, rpb = ps_next()
                for h in range(4):
                    T.op('pe', lambda: nc.tensor.matmul(pb[:, h * 128:(h + 1) * 128], lhsT=kd[b][:, h * 128:(h + 1) * 128],
                                                        rhs=vtm[:, c, h * 128:(h + 1) * 128], start=True, stop=True),
                         reads=[R_kd[b], R_v[c]], writes=[rpb])
                return pb, rpb

            def state_acc(pb, rpb, S32, R_S, cdec, first):
                if first:
                    T.op('dve', lambda: nc.vector.tensor_copy(out=S32[:, :], in_=pb[:, :]), reads=[rpb], writes=[R_S])
                else:
                    for h in range(4):
                        T.op('dve', lambda: nc.vector.scalar_tensor_tensor(
                            out=S32[:, h * 128:(h + 1) * 128], in0=S32[:, h * 128:(h + 1) * 128], scalar=cdec[h],
                            in1=pb[:, h * 128:(h + 1) * 128], op0=ALU.mult, op1=ALU.add),
                            reads=[rpb, R_S], writes=[R_S])

            u_next = state_u(NT - 1, 1)
            for c in range(NT - 1, 0, -1):
                u_cur = u_next
                if c - 1 >= 1:
                    u_next = state_u(c - 1, 1)
                state_acc(u_cur[0], u_cur[1], Sf32, R_S32, _CDB, first=(c == NT - 1))
                T.op('act', lambda: nc.scalar.copy(out=Sb[:, c - 1, :], in_=Sf32[:, :]), reads=[R_S32],
                     writes=[R_Sb[c - 1]])
            wq, rwq = w_next(4096)
            wq3 = wq[:, :].rearrange("p (k n) -> p k n", k=8)
            wg_, rwg_ = w_next(4096)
            wg3_ = wg_[:, :].rearrange("p (k n) -> p k n", k=8)
            def ret_front(c):
                b = c % 2
                pb, rpb = proj_tm(wq3, rwq, c)
                rotary(pb, rpb, c, qrot[b][:, :], R_qrot[b])
                for src, Rs, dstc, Rdc in ((qrot[b][:, :], R_qrot[b], qTc[b], R_qTc[b]),
                                           (krot[:, c, :], R_kr[c], kTc[b], R_kTc[b])):
                    tb, rtb = tb_next()
                    for h in range(4):
                        T.op('pe', lambda: nc.tensor.transpose(out=tb[:, h * 128:(h + 1) * 128],
                                                               in_=src[:, h * 128:(h + 1) * 128], identity=ident[:]),
                             reads=[Rs, R_misc], writes=[rtb])
                    T.op('act', lambda: nc.scalar.copy(out=dstc[:].rearrange("p h n -> p (h n)"), in_=tb[:, 0:512]),
                         reads=[rtb], writes=[Rdc])
                if c > 0:
                    T.op('dve', lambda: nc.vector.tensor_tensor(out=qfT[b][:], in0=qTc[b][:], in1=QFt[:], op=ALU.mult),
                         reads=[R_qTc[b], R_ct], writes=[R_qfT[b]])
                if c < NT - 1:
                    T.op('dve', lambda: nc.vector.tensor_tensor(out=qbT[b][:], in0=qTc[b][:], in1=QBt[:], op=ALU.mult),
                         reads=[R_qTc[b], R_ct], writes=[R_qbT[b]])
                pg, rpg = proj_tm(wg3_, rwg_, c)
                T.op('act', lambda: nc.scalar.activation(out=sge[b][:, :], in_=pg[:, :], func=AF.Silu), reads=[rpg],
                     writes=[R_sge[b]])
                T.op('dve', lambda: nc.vector.tensor_tensor(out=sge[b][:, :], in0=sge[b][:, :], in1=gnb[:, 0:512],
                                                             op=ALU.mult), reads=[R_sge[b], R_ct], writes=[R_sge[b]])
                pscore, rps = ps_next()
                for h in range(4):
                    T.op('pe', lambda: nc.tensor.matmul(pscore[:, h * 128:(h + 1) * 128], lhsT=kTc[b][:, h, :],
                                                        rhs=qTc[b][:, h, :], start=True, stop=True),
                         reads=[R_kTc[b], R_qTc[b]], writes=[rps])
                T.op('dve', lambda: nc.vector.tensor_tensor(out=pTr[b][:].rearrange("p h n -> p (h n)"),
                                                             in0=pscore[:, :],
                                                             in1=DTt[:].rearrange("p h n -> p (h n)"), op=ALU.mult),
                     reads=[rps, R_ct], writes=[R_pTr[b]])
            def ret_mid(c):
                b = c % 2
                po, rpo = ps_next()
                sfb_prev = Sfb[(c + 1) % 2]
                R_sfb_prev = R_Sfb[(c + 1) % 2]
                for h in range(4):
                    last_is = 0 if (c == 0 and c == NT - 1) else None
                    steps = [('intra', None)]
                    if c > 0:
                        steps.append(('f', None))
                    if c < NT - 1:
                        steps.append(('b', None))
                    for si_, (kind, _) in enumerate(steps):
                        st_, sp_ = (si_ == 0), (si_ == len(steps) - 1)
                        if kind == 'intra':
                            T.op('pe', lambda: nc.tensor.matmul(po[:, h * 128:(h + 1) * 128], lhsT=pTr[b][:, h, :],
                                                                rhs=vtm[:, c, h * 128:(h + 1) * 128], start=st_,
                                                                stop=sp_),
                                 reads=[R_pTr[b], R_v[c]], writes=[rpo])
                        elif kind == 'f':
                            T.op('pe', lambda: nc.tensor.matmul(po[:, h * 128:(h + 1) * 128], lhsT=qfT[b][:, h, :],
                                                                rhs=sfb_prev[:, h * 128:(h + 1) * 128], start=st_,
                                                                stop=sp_),
                                 reads=[R_qfT[b], R_sfb_prev], writes=[rpo])
                        else:
                            T.op('pe', lambda: nc.tensor.matmul(po[:, h * 128:(h + 1) * 128], lhsT=qbT[b][:, h, :],
                                                                rhs=Sb[:, c, h * 128:(h + 1) * 128], start=st_,
                                                                stop=sp_),
                                 reads=[R_qbT[b], R_Sb[c]], writes=[rpo])
                if c < NT - 1:
                    u_ = state_u(c, 0)
                    state_acc(u_[0], u_[1], Sf32, R_S32, _CDF, first=(c == 0))
                    T.op('act', lambda: nc.scalar.copy(out=Sfb[b][:, :], in_=Sf32[:, :]), reads=[R_S32],
                         writes=[R_Sfb[b]])
                T.op('act', lambda: nc.scalar.activation(out=o1[b][:, :], in_=po[:, :], func=AF.Square),
                     reads=[rpo], writes=[R_o1[b]])
                T.op('dve', lambda: nc.vector.tensor_reduce(out=gss[b][:, :],
                                                             in_=o1[b][:, :].rearrange("p (h d) -> p h d", h=4),
                                                             axis=mybir.AxisListType.X, op=ALU.add),
                     reads=[R_o1[b]], writes=[R_gss[b]])
                T.op('dve', lambda: nc.vector.tensor_scalar(out=gss[b][:, :], in0=gss[b][:, :], scalar1=1.0 / 128,
                                                             scalar2=EPS, op0=ALU.mult, op1=ALU.add),
                     reads=[R_gss[b]], writes=[R_gss[b]])
                T.op('pool', lambda: nc.gpsimd.tensor_tensor(out=gss[b][:, :], in0=gss[b][:, :], in1=mhalf[:, :],
                                                              op=ALU.pow), reads=[R_gss[b], R_mh],
                     writes=[R_gss[b]])
                T.op('dve', lambda: nc.vector.tensor_tensor(
                    out=o1[b][:, :].rearrange("p (h d) -> p h d", h=4),
                    in0=po[:, :].rearrange("p (h d) -> p h d", h=4),
                    in1=gss[b][:, :].unsqueeze(2).to_broadcast([128, 4, 128]), op=ALU.mult),
                    reads=[rpo, R_gss[b]], writes=[R_o1[b]])
                T.op('dve', lambda: nc.vector.tensor_tensor(out=yrt[b][:, :], in0=o1[b][:, :], in1=sge[b][:, :],
                                                             op=ALU.mult), reads=[R_o1[b], R_sge[b]],
                     writes=[R_yrt[b]])

            def ret_back(c):
                b = c % 2
                transpose_tm_to_fm(yrt[b], R_yrt[b], 512, ysT[1], 0, c, R_ys[1][c])

            ret_front(0)
            for c in range(NT):
                if c + 1 < NT:
                    ret_front(c + 1)
                ret_mid(c)
                if c >= 1:
                    ret_back(c - 1)
            ret_back(NT - 1)
            w_done(2)
            T.barrier(engines=('act', 'dve', 'pool', 'sp'))
            if dbg and l == 0:
                dump('yretT', ysT[1][:].rearrange("p k n -> p (k n)"), R_ys[1])
                dump('krot', krot[:].rearrange("p a b -> p (a b)"), [])
                dump('vtm', vtm[:].rearrange("p a b -> p (a b)"), [])
                dump('Sb', Sb[:].rearrange("p a b -> p (a b)"), [])
                for nm_, t_ in (('qTc', qTc[0]), ('kTc', kTc[0]), ('pTr', pTr[0]), ('qfT', qfT[0]), ('qbT', qbT[0])):
                    dump(nm_, t_[:].rearrange("p a b -> p (a b)"), [])
                dump('yrt', yrt[0][:, :], [])
                dump('sge', sge[0][:, :], [])
                dump('o1', o1[0][:, :], [])
            if stop == 'C':
                break

            CV.reset()
            xcp = CV.get("xcp", [128, S + 4], F32)
            xf = CV.get("xf", [128, S], F32)
            xfb = CV.get("xfb", [128, S], BF16)
            gg = CV.get("gg", [128, S], BF16)
            abuf = [CV.get("abuf%d" % i, [128, S], F32) for i in range(2)]
            bbuf = [CV.get("bbuf%d" % i, [128, S], F32) for i in range(2)]
            mbuf = [CV.get("mbuf%d" % i, [128, S], F32) for i in range(2)]
            R_xcp, R_xf, R_xfb, R_gg = (Res() for _ in range(4))
            R_a = [Res(), Res()]
            R_b = [Res(), Res()]
            R_m = [Res(), Res()]
            R_pad = Res()
            T.op('act', lambda: nc.scalar.activation(out=xcp[:, 0:2], in_=ident[:, 0:2], func=AF.Copy, scale=0.0),
                 reads=[R_misc], writes=[R_pad])
            T.op('act', lambda: nc.scalar.activation(out=xcp[:, S + 2:S + 4], in_=ident[:, 0:2], func=AF.Copy,
                                                     scale=0.0), reads=[R_misc], writes=[R_pad])
            for ch in range(4):
                wl, rwl = w_next(2560)
                wl3 = wl[:, 0:2048].rearrange("p (k n) -> p k n", k=8)
                wm = wl[:, 2048:2560].rearrange("p (m n) -> p m n", m=4)
                for which in range(2):
                    for tg in range(4):
                        pb, rpb = ps_next()
                        for kc in range(8):
                            T.op('pe', lambda: nc.tensor.matmul(pb[:, :], lhsT=wl3[:, kc, which * 128:(which + 1) * 128],
                                                                rhs=hT[:, kc, tg * 512:(tg + 1) * 512],
                                                                start=(kc == 0), stop=(kc == 7)),
                                 reads=[rwl] + R_hT[tg * 4:tg * 4 + 4], writes=[rpb])
                        if which == 0:
                            T.op('act', lambda: nc.scalar.copy(out=xcp[:, 2 + tg * 512:2 + (tg + 1) * 512], in_=pb[:, :]),
                                 reads=[rpb, R_pad], writes=[R_xcp])
                        else:
                            T.op('act', lambda: nc.scalar.activation(out=gg[:, tg * 512:(tg + 1) * 512], in_=pb[:, :],
                                                                     func=AF.Gelu_apprx_tanh),
                                 reads=[rpb], writes=[R_gg])
                cw = lambda k_: colp[:, cb + 24 + ch * 4 + k_:cb + 24 + ch * 4 + k_ + 1]
                T.op('dve', lambda: nc.vector.tensor_scalar(out=xf[:, :], in0=xcp[:, 0:S], scalar1=cw(0),
                                                             scalar2=colp[:, cb + 40 + ch:cb + 41 + ch], op0=ALU.mult,
                                                             op1=ALU.add), reads=[R_xcp, R_misc], writes=[R_xf])
                for k_ in range(1, 4):
                    T.op('dve', lambda: nc.vector.scalar_tensor_tensor(out=xf[:, :], in0=xcp[:, k_:k_ + S],
                                                                        scalar=cw(k_), in1=xf[:, :], op0=ALU.mult,
                                                                        op1=ALU.add), reads=[R_xcp, R_xf, R_misc],
                         writes=[R_xf])
                T.op('act', lambda: nc.scalar.copy(out=xfb[:, :], in_=xf[:, :]), reads=[R_xf], writes=[R_xfb])
                def lru_dir(dirn):
                    ci = l * 8 + dirn * 4 + ch
                    ba_ = colp[:, cb + 44 + dirn * 4 + ch:cb + 45 + dirn * 4 + ch]
                    bx_ = colp[:, cb + 52 + dirn * 4 + ch:cb + 53 + dirn * 4 + ch]
                    ab, bb, mb = abuf[dirn], bbuf[dirn], mbuf[dirn]
                    Ra, Rb, Rm = R_a[dirn], R_b[dirn], R_m[dirn]
                    for tg in range(4):
                        for mi, bias_, dstb, Rd in ((dirn * 2, ba_, ab, Ra), (dirn * 2 + 1, bx_, bb, Rb)):
                            pb, rpb = ps_next()
                            T.op('pe', lambda: nc.tensor.matmul(pb[:, :], lhsT=wm[:, mi, :],
                                                                rhs=xfb[:, tg * 512:(tg + 1) * 512], start=True,
                                                                stop=True), reads=[rwl, R_xfb], writes=[rpb])
                            T.op('act', lambda: nc.scalar.activation(out=dstb[:, tg * 512:(tg + 1) * 512], in_=pb[:, :],
                                                                     func=AF.Sigmoid, bias=bias_),
                                 reads=[rpb, R_misc], writes=[Rd])
                        yield
                    T.op('act', lambda: nc.scalar.activation(out=mb[:, :], in_=ab[:, :], func=AF.Exp,
                                                             scale=lru_2c[:, ci:ci + 1]), reads=[Ra, R_lc],
                         writes=[Rm])
                    T.op('act', lambda: nc.scalar.activation(out=ab[:, :], in_=ab[:, :], func=AF.Exp,
                                                             scale=lru_c[:, ci:ci + 1]), reads=[Ra, R_lc],
                         writes=[Ra])
                    T.op('pool', lambda: nc.gpsimd.tensor_tensor(out=bb[:, :], in0=bb[:, :], in1=xf[:, :],
                                                                  op=ALU.mult), reads=[Rb, R_xf], writes=[Rb])
                    yield
                    T.op('act', lambda: nc.scalar.activation(out=mb[:, :], in_=mb[:, :], func=AF.Sqrt, scale=-1.0,
                                                             bias=1.0), reads=[Rm], writes=[Rm])
                    first = S - 1 if dirn else 0
                    T.op('dve', lambda: nc.vector.tensor_scalar(out=mb[:, first:first + 1],
                                                                 in0=mb[:, first:first + 1], scalar1=0.0, scalar2=1.0,
                                                                 op0=ALU.mult, op1=ALU.add), reads=[Rm], writes=[Rm])
                    yield
                    T.op('dve', lambda: nc.vector.tensor_tensor(out=bb[:, :], in0=bb[:, :], in1=mb[:, :],
                                                                 op=ALU.mult), reads=[Rb, Rm], writes=[Rb])
                    yield
                    if dirn == 0:
                        T.op('dve', lambda: nc.vector.tensor_tensor_scan(out=bb[:, :], data0=ab[:, :],
                                                                          data1=bb[:, :], initial=0.0, op0=ALU.mult,
                                                                          op1=ALU.add), reads=[Ra, Rb], writes=[Rb])
                    else:
                        T.op('dve', lambda: nc.vector.tensor_tensor_scan(out=bb[:, ::-1], data0=ab[:, ::-1],
                                                                          data1=bb[:, ::-1], initial=0.0,
                                                                          op0=ALU.mult, op1=ALU.add),
                             reads=[Ra, Rb], writes=[Rb])
                    yield

                gens = [lru_dir(0), lru_dir(1)]
                alive = [True, True]
                while any(alive):
                    for gi_ in range(2):
                        if alive[gi_]:
                            try:
                                next(gens[gi_])
                            except StopIteration:
                                alive[gi_] = False
                T.op('pool', lambda: nc.gpsimd.tensor_tensor(out=bbuf[0][:, :], in0=bbuf[0][:, :], in1=bbuf[1][:, :],
                                                              op=ALU.add), reads=[R_b[0], R_b[1]], writes=[R_b[0]])
                T.op('pool', lambda: nc.gpsimd.tensor_tensor(out=ysT[2][:, ch, :], in0=bbuf[0][:, :], in1=gg[:, :],
                                                              op=ALU.mult), reads=[R_b[0], R_gg], writes=R_ys[2])
                w_done()
            T.barrier(engines=('act', 'dve', 'pool', 'sp'))
            if dbg and l == 0:
                dump('ylruT', ysT[2][:].rearrange("p k n -> p (k n)"), R_ys[2])
            if stop == 'D':
                break

            CV.reset()
            macc = CV.get("macc", [128, 8, S], F32)
            gtmp = [CV.get("gtmp%d" % i, [128, 512], F32) for i in range(2)]
            ptmp = [CV.get("ptmp%d" % i, [128, 512], F32) for i in range(2)]
            R_macc = [[Res() for _ in range(4)] for _ in range(8)]
            R_gt = [Res(), Res()]
            R_ptm = [Res(), Res()]
            mT = nc.alloc_sbuf_tensor_at("mT_%d" % l, [128, 8, S], BF16, offset=YS_ADDR)
            R_mT = [Res() for _ in range(4)]
            gctr = 0
            for n in range(3):
                wb, rwb = w_next(4096)
                wb3 = wb[:, :].rearrange("p (k n) -> p k n", k=4)
                for half in range(2):
                    wgt, rwgt = w_next(4096)
                    wgt3 = wgt[:, :].rearrange("p (k n) -> p k n", k=8)
                    for j in range(4):
                        cc = half * 4 + j
                        bcol = colp[:, cb + n * 8 + cc:cb + n * 8 + cc + 1]
                        for tg in range(4):
                            pl, rpl = ps_next()
                            for kc in range(8):
                                T.op('pe', lambda: nc.tensor.matmul(pl[:, :], lhsT=wgt3[:, kc, j * 128:(j + 1) * 128],
                                                                    rhs=hT[:, kc, tg * 512:(tg + 1) * 512],
                                                                    start=(kc == 0), stop=(kc == 7)),
                                     reads=[rwgt] + R_hT[tg * 4:tg * 4 + 4], writes=[rpl])
                            pbr, rpbr = ps_next()
                            for kc in range(4):
                                T.op('pe', lambda: nc.tensor.matmul(pbr[:, :], lhsT=wb3[:, kc, cc * 128:(cc + 1) * 128],
                                                                    rhs=ysT[n][:, kc, tg * 512:(tg + 1) * 512],
                                                                    start=(kc == 0), stop=(kc == 3)),
                                     reads=[rwb] + R_ys[n][tg * 4:tg * 4 + 4], writes=[rpbr])
                            gi = gctr % 2
                            gctr += 1
                            T.op('act', lambda: nc.scalar.activation(out=gtmp[gi][:, :], in_=pl[:, :], func=AF.Sigmoid,
                                                                     bias=bcol), reads=[rpl, R_misc],
                                 writes=[R_gt[gi]])
                            msl = macc[:, cc, tg * 512:(tg + 1) * 512]
                            if n == 0:
                                T.op('dve', lambda: nc.vector.tensor_tensor(out=msl, in0=pbr[:, :], in1=gtmp[gi][:, :],
                                                                             op=ALU.mult), reads=[rpbr, R_gt[gi]],
                                     writes=[R_macc[cc][tg]])
                            else:
                                T.op('dve', lambda: nc.vector.tensor_tensor(out=ptmp[gi][:, :], in0=pbr[:, :],
                                                                             in1=gtmp[gi][:, :], op=ALU.mult),
                                     reads=[rpbr, R_gt[gi]], writes=[R_ptm[gi]])
                                if n == 1:
                                    T.op('dve', lambda: nc.vector.tensor_tensor(out=msl, in0=msl, in1=ptmp[gi][:, :],
                                                                                 op=ALU.add),
                                         reads=[R_ptm[gi], R_macc[cc][tg]], writes=[R_macc[cc][tg]])
                                else:
                                    T.op('dve', lambda: nc.vector.tensor_tensor(
                                        out=mT[:, cc, tg * 512:(tg + 1) * 512], in0=msl, in1=ptmp[gi][:, :], op=ALU.add),
                                        reads=[R_ptm[gi], R_macc[cc][tg]] + R_ys[0][tg * 4:tg * 4 + 4] +
                                        R_ys[1][tg * 4:tg * 4 + 4], writes=[R_mT[tg]])
                    w_done_last()
                w_done()
            T.barrier(engines=('act', 'dve', 'pool', 'sp'))
            for t in range(NT):
                T.dma('sp', ('x', t % 4), X[:, t, :], xd[t * 128:(t + 1) * 128, :], reads=[R_xd[t]], writes=[R_X[t]])
            for half in range(2):
                wo, rwo = w_next(4096)
                wo3 = wo[:, :].rearrange("p (k n) -> p k n", k=8)
                for t in range(NT):
                    pb, rpb = ps_next()
                    for kc in range(8):
                        T.op('pe', lambda: nc.tensor.matmul(pb[:, :], lhsT=mT[:, kc, t * 128:(t + 1) * 128],
                                                            rhs=wo3[:, kc, :], start=(kc == 0), stop=(kc == 7)),
                             reads=[rwo, R_mT[t // 4]], writes=[rpb])
                    xs = X[:, t, half * 512:(half + 1) * 512]
                    T.op('dve', lambda: nc.vector.tensor_tensor(out=xs, in0=xs, in1=pb[:, :], op=ALU.add),
                         reads=[rpb, R_X[t]], writes=[R_X[t]])
                w_done()
            if dbg and l == 0:
                T.barrier()
                for t in range(NT):
                    T.dma('sp', 'dbg', dbg_out['xE'][t * 128:(t + 1) * 128, :], X[:, t, :], reads=[R_X[t]])
                T.barrier()
            if stop == 'E':
                break

            rmsnorm_to_hT(l * 3 + 1, spill=False)
            actT = nc.alloc_sbuf_tensor_at("actT_%d" % l, [128, 8, S], BF16,
                                           offset=YS_ADDR)
            R_sq = [Res(), Res()]
            R_act = [[Res() for _ in range(4)] for _ in range(8)]
            sctr2 = 0
            for s_ in range(4):
                for j2 in range(2):
                    wu, rwu = w_next(4096)
                    wu3 = wu[:, :].rearrange("p (k n) -> p k n", k=8)
                    for j in range(4):
                        fc = j2 * 4 + j
                        for tg in range(4):
                            pb, rpb = ps_next()
                            for kc in range(8):
                                T.op('pe', lambda: nc.tensor.matmul(pb[:, :], lhsT=wu3[:, kc, j * 128:(j + 1) * 128],
                                                                    rhs=hT[:, kc, tg * 512:(tg + 1) * 512],
                                                                    start=(kc == 0), stop=(kc == 7)),
                                     reads=[rwu] + R_hT[tg * 4:tg * 4 + 4], writes=[rpb])
                            si = sctr2 % 2
                            sctr2 += 1
                            T.op('act', lambda: nc.scalar.activation(out=sqf[si][:, :], in_=pb[:, :], func=AF.Square),
                                 reads=[rpb], writes=[R_sq[si]])
                            T.op('dve', lambda: nc.vector.scalar_tensor_tensor(
                                out=actT[:, fc, tg * 512:(tg + 1) * 512], in0=pb[:, :], scalar=0.0, in1=sqf[si][:, :],
                                op0=ALU.is_gt, op1=ALU.mult), reads=[rpb, R_sq[si]], writes=[R_act[fc][tg]] + R_mT)
                    w_done()
                wd = []
                for j2 in range(2):
                    w_, rw_ = w_next(4096)
                    wd.append((w_[:, :].rearrange("p (k n) -> p k n", k=4), rw_))
                for t in range(NT):
                    for half in range(2):
                        pb, rpb = ps_next()
                        for fc in range(8):
                            w3_, rw_ = wd[fc // 4]
                            T.op('pe', lambda: nc.tensor.matmul(pb[:, :], lhsT=actT[:, fc, t * 128:(t + 1) * 128],
                                                                rhs=w3_[:, fc % 4, half * 512:(half + 1) * 512],
                                                                start=(fc == 0), stop=(fc == 7)),
                                 reads=[rw_, R_act[fc][t // 4]], writes=[rpb])
                        xs = X[:, t, half * 512:(half + 1) * 512]
                        T.op('dve', lambda: nc.vector.tensor_tensor(out=xs, in0=xs, in1=pb[:, :], op=ALU.add),
                             reads=[rpb, R_X[t]], writes=[R_X[t]])
                w_done(2)
            if dbg and l == 0:
                T.barrier()
                for t in range(NT):
                    T.dma('sp', 'dbg', dbg_out['xF'][t * 128:(t + 1) * 128, :], X[:, t, :], reads=[R_X[t]])
                T.barrier()
            if stop == 'F':
                break

            rmsnorm_to_hT(l * 3 + 2, spill=False)
            ppT = nc.alloc_sbuf_tensor_at("ppT_%d" % l, [128, 2, S], BF16,
                                          offset=YS_ADDR)
            R_pp = Res()
            for kc in range(2):
                T.dma('pool', 'c_pp', ppT[:, kc, :], pT_in[l, kc * 128:(kc + 1) * 128, :],
                      writes=[R_pp] + R_act[0] + R_act[1])
            wp, rwp = w_next(2048)
            wp3 = wp[:, 0:2048].rearrange("p (k n) -> p k n", k=2)
            gctr = 0
            for half in range(2):
                wgp, rwgp = w_next(4096)
                wgp3 = wgp[:, :].rearrange("p (k n) -> p k n", k=8)
                for t in range(NT):
                    pl, rpl = ps_next()
                    for kc in range(8):
                        T.op('pe', lambda: nc.tensor.matmul(pl[:, :], lhsT=hT[:, kc, t * 128:(t + 1) * 128],
                                                            rhs=wgp3[:, kc, :], start=(kc == 0), stop=(kc == 7)),
                             reads=[rwgp, R_hT[t]], writes=[rpl])
                    pe_, rpe_ = ps_next()
                    for kc in range(2):
                        T.op('pe', lambda: nc.tensor.matmul(pe_[:, :], lhsT=ppT[:, kc, t * 128:(t + 1) * 128],
                                                            rhs=wp3[:, kc, half * 512:(half + 1) * 512],
                                                            start=(kc == 0), stop=(kc == 1)),
                             reads=[rwp, R_pp], writes=[rpe_])
                    gi = gctr % 2
                    gctr += 1
                    T.op('act', lambda: nc.scalar.activation(out=sqf[gi][:, :], in_=pl[:, :], func=AF.Sigmoid),
                         reads=[rpl], writes=[R_sq[gi]])
                    T.op('dve', lambda: nc.vector.tensor_tensor(out=sqf[gi][:, :], in0=pe_[:, :], in1=sqf[gi][:, :],
                                                                 op=ALU.mult), reads=[rpe_, R_sq[gi]],
                         writes=[R_sq[gi]])
                    xs = X[:, t, half * 512:(half + 1) * 512]
                    T.op('dve', lambda: nc.vector.tensor_tensor(out=xs, in0=xs, in1=sqf[gi][:, :], op=ALU.add),
                         reads=[R_sq[gi], R_X[t]], writes=[R_X[t]])
                w_done_last()
            w_done()
            if dbg and l == 0:
                T.barrier()
                for t in range(NT):
                    T.dma('sp', 'dbg', dbg_out['xG'][t * 128:(t + 1) * 128, :], X[:, t, :], reads=[R_X[t]])
                T.barrier()
            if stop == 'G':
                break

        if stop is None:
            R_g = R_gG
            T.dma('sp', 'c_g', gbc[:], gvec_in[DEPTH * 3:DEPTH * 3 + 1, :].partition_broadcast(128), writes=[R_g])
            R_ss = R_ssG
            for t in range(NT):
                T.op('act', lambda: nc.scalar.activation(out=junk[:], in_=X[:, t, :], func=AF.Square,
                                                         accum_out=ssq[:, t:t + 1]), reads=[R_X[t]], writes=[R_ss, R_junk])
            T.op('dve', lambda: nc.vector.tensor_scalar(out=rstd[:], in0=ssq[:], scalar1=1.0 / D, scalar2=EPS,
                                                         op0=ALU.mult, op1=ALU.add), reads=[R_ss], writes=[R_ss])
            T.op('act', lambda: nc.scalar.activation(out=rstd[:], in_=rstd[:], func=AF.Sqrt), reads=[R_ss],
                 writes=[R_ss])
            T.op('dve', lambda: nc.vector.reciprocal(out=rstd[:], in_=rstd[:]), reads=[R_ss], writes=[R_ss])
            for t in range(NT):
                T.op('dve', lambda: nc.vector.scalar_tensor_tensor(out=X[:, t, :], in0=X[:, t, :],
                                                                    scalar=rstd[:, t:t + 1], in1=gbc[:], op0=ALU.mult,
                                                                    op1=ALU.mult), reads=[R_X[t], R_ss, R_g],
                     writes=[R_X[t]])
                T.dma('sp', ('yo', t % 4), y_out[t * 128:(t + 1) * 128, :], X[:, t, :], reads=[R_X[t]])
        T.barrier()
    return nc


def _find_addr(nc, t):
    raise RuntimeError("cannot determine arena address; attrs=%s" % [a for a in dir(t) if not a.startswith('__')])


def _addr_of(nc, t, arena_addr, arena):
    for attr in ('addr', 'offset', 'address', 'base_addr', 'start_addr'):
        if hasattr(t, attr):
            v = getattr(t, attr)
            v = v() if callable(v) else v
            if isinstance(v, int):
                return v
    raise RuntimeError("no addr attr: %s" % [a for a in dir(t) if not a.startswith('__')])


def _prep_inputs(inp):
    inp = {k: np.asarray(v, dtype=np.float32) for k, v in inp.items()}
    ws = np.stack([np.concatenate(_layer_tiles(l, inp), axis=1) for l in range(DEPTH)], 0)
    cols = np.concatenate([_small_cols(l, inp) for l in range(DEPTH)], 1)
    gv = []
    for l in range(DEPTH):
        gv += [inp['g_mix'][l], inp['g_mlp'][l], inp['g_ple'][l]]
    gv.append(inp['g_final'])
    gvec = np.stack(gv, 0)
    bt = np.stack([_na_bias_table(inp['na_rpb'][l]).reshape(128, -1) for l in range(DEPTH)], 0)
    shared = {
        'ws': np.ascontiguousarray(ws), 'cols': np.ascontiguousarray(cols), 'gvec': np.ascontiguousarray(gvec),
        'gn': np.ascontiguousarray(inp['ret_gn']), 'bt': np.ascontiguousarray(bt),
        'ident': _CONSTS['ident'], 'cos': _CONSTS['cos'].reshape(128, -1), 'sin': _CONSTS['sin'].reshape(128, -1),
        'DT': _CONSTS['DT'].reshape(128, -1), 'QF': _CONSTS['QF'].reshape(128, -1),
        'QB': _CONSTS['QB'].reshape(128, -1), 'KD': _CONSTS['KD'].reshape(128, -1),
    }
    in_maps = []
    for b in range(8):
        m = dict(shared)
        m['x'] = np.ascontiguousarray(inp['x'][b])
        m['pT'] = np.ascontiguousarray(inp['p'][:, b].transpose(0, 2, 1))
        in_maps.append(m)
    return in_maps


def kernel(**inputs):
    in_maps = _prep_inputs(inputs)
    nc = build()
    res = run_bass_kernel_spmd(nc, in_maps, core_ids=list(range(8)))
    return np.stack([np.asarray(r['y'], dtype=np.float32) for r in res.results], 0)
```

```python
import contextlib
import numpy as np
import concourse.bass as bass
import concourse.mybir as mybir
from concourse.bass_utils import run_bass_kernel_spmd

F32 = mybir.dt.float32
BF16 = mybir.dt.bfloat16
AF = mybir.ActivationFunctionType
ALU = mybir.AluOpType

D = 1024
S = 2048
NT = 16
DEPTH = 2
EPS = 1e-6
NEG = -30000.0
WTILE = 4096
NSLOT = 3

OFF_RET = 1536
OFF_LRU = 3584
OFF_GATE = 4608
NCOLS_SMALL = 68


def _kmajor(W):
    K, N = W.shape
    return np.ascontiguousarray(W.reshape(K // 128, 128, N).transpose(1, 0, 2).reshape(128, -1))


def _ret_perm():
    idx = []
    for h in range(4):
        idx += [h * 128 + 2 * m for m in range(64)]
        idx += [h * 128 + 2 * m + 1 for m in range(64)]
    return np.array(idx)


def _layer_tiles(l, inp):
    w_in = inp['w_in'][l]
    tiles = []
    for ps in range(2):
        q = w_in[:, ps * 256:(ps + 1) * 256]
        k = w_in[:, 512 + ps * 256:512 + (ps + 1) * 256]
        tiles.append(_kmajor(np.concatenate([q, k], 1)))
        tiles.append(_kmajor(w_in[:, 1024 + ps * 256:1024 + (ps + 1) * 256]))
    perm = _ret_perm()
    qr = w_in[:, OFF_RET:OFF_RET + 512]
    kr = w_in[:, OFF_RET + 512:OFF_RET + 1024]
    vr = w_in[:, OFF_RET + 1024:OFF_RET + 1536]
    gr = w_in[:, OFF_RET + 1536:OFF_RET + 2048]
    tiles += [_kmajor(kr[:, perm]), _kmajor(vr), _kmajor(qr[:, perm]), _kmajor(gr)]
    xc = w_in[:, OFF_LRU:OFF_LRU + 512]
    gc = w_in[:, OFF_LRU + 512:OFF_LRU + 1024]
    for ch in range(4):
        a = _kmajor(np.concatenate([xc[:, ch * 128:(ch + 1) * 128], gc[:, ch * 128:(ch + 1) * 128]], 1))
        mats = np.zeros((128, 4, 128), np.float32)
        for mi, (nm, d) in enumerate([('lru_wa', 0), ('lru_wx', 0), ('lru_wa', 1), ('lru_wx', 1)]):
            w = inp[nm][l][d]
            mats[0:64, mi, 0:64] = w[2 * ch]
            mats[64:128, mi, 64:128] = w[2 * ch + 1]
        tiles.append(np.concatenate([a, mats.reshape(128, 512)], 1))
    for n in range(3):
        tiles.append(_kmajor(inp['w_branch'][l][n]))
        for half in range(2):
            c0 = OFF_GATE + n * 1024 + half * 512
            tiles.append(_kmajor(w_in[:, c0:c0 + 512]))
    for half in range(2):
        tiles.append(_kmajor(inp['w_out'][l][:, half * 512:(half + 1) * 512]))
    for s in range(4):
        for j in range(2):
            tiles.append(_kmajor(inp['w_up'][l][:, (2 * s + j) * 512:(2 * s + j + 1) * 512]))
        for j in range(2):
            tiles.append(_kmajor(inp['w_down'][l][(2 * s + j) * 512:(2 * s + j + 1) * 512, :]))
    tiles.append(_kmajor(inp['w_ple'][l]))
    for half in range(2):
        tiles.append(_kmajor(inp['w_ple_gate'][l][:, half * 512:(half + 1) * 512]))
    return tiles


def _tile_sizes():
    sz = []
    for ps in range(2):
        sz += [4096, 2048]
    sz += [4096] * 4
    sz += [2560] * 4
    for n in range(3):
        sz += [4096, 4096, 4096]
    sz += [4096, 4096]
    for s in range(4):
        sz += [4096] * 4
    sz += [2048, 4096, 4096]
    return sz


def _na_bias_table(rpb):
    ext = np.concatenate([rpb.reshape(8, 15 * 31), np.full((8, 1), NEG, np.float32)], 1)
    p = np.arange(128)
    b = p // 64
    kc = p % 64
    d = np.arange(14)
    c = np.arange(64)
    cs = np.clip(c - 8, 0, 48)
    row = d[None, :, None] + b[:, None, None]
    col = kc[:, None, None] - c[None, None, :] + 15
    valid = (kc[:, None, None] >= cs[None, None, :]) & (kc[:, None, None] < cs[None, None, :] + 16)
    flat = np.where(valid, row * 31 + np.clip(col, 0, 30), 15 * 31)
    out = ext[:, flat]
    return np.ascontiguousarray(out.transpose(1, 0, 2, 3)).astype(np.float32)


def _consts():
    f8 = np.float64
    c = {}
    c['ident'] = np.eye(128, dtype=np.float32)
    pos = np.arange(S, dtype=np.float32)
    theta = (1.0 / (np.float32(10000.0) ** np.linspace(0.0, 1.0, 64, dtype=np.float32))).astype(np.float32)
    ang = (pos[:, None] * theta[None, :]).astype(np.float32)
    cos = np.cos(ang.astype(f8)).astype(np.float32).reshape(16, 128, 64).transpose(1, 0, 2)
    sin = np.sin(ang.astype(f8)).astype(np.float32).reshape(16, 128, 64).transpose(1, 0, 2)
    c['cos'] = np.ascontiguousarray(cos)
    c['sin'] = np.ascontiguousarray(sin)
    hidx = np.arange(4, dtype=f8)
    lgf = np.log1p(-np.exp2(-5.0 - hidx))
    lgb = np.log1p(-np.exp2(-5.5 - hidx))
    i = np.arange(128, dtype=f8)
    scale = 128.0 ** -0.5
    diff = i[None, :] - i[:, None]
    DT = np.zeros((128, 4, 128), f8)
    for h in range(4):
        DT[:, h, :] = np.where(diff >= 0, np.exp(lgf[h] * np.maximum(diff, 0)),
                               np.exp(lgb[h] * np.maximum(-diff, 0))) * scale
    c['DT'] = DT.astype(np.float32)
    QF = np.zeros((128, 4, 128), f8)
    QB = np.zeros((128, 4, 128), f8)
    for h in range(4):
        QF[:, h, :] = (np.exp(lgf[h] * (i + 1.0)) * scale)[None, :]
        QB[:, h, :] = (np.exp(lgb[h] * (128.0 - i)) * scale)[None, :]
    c['QF'] = QF.astype(np.float32)
    c['QB'] = QB.astype(np.float32)
    KD = np.zeros((128, 2, 4), f8)
    for h in range(4):
        KD[:, 0, h] = np.exp(lgf[h] * (127.0 - i))
        KD[:, 1, h] = np.exp(lgb[h] * i)
    c['KD'] = KD.astype(np.float32)
    cdf = [float(np.exp(lgf[h] * 128.0)) for h in range(4)]
    cdb = [float(np.exp(lgb[h] * 128.0)) for h in range(4)]
    return c, cdf, cdb


_CONSTS, _CDF, _CDB = _consts()


def _small_cols(l, inp):
    out = np.zeros((128, NCOLS_SMALL), np.float32)
    out[:, 0:24] = inp['b_gate'][l].reshape(3, 8, 128).transpose(2, 0, 1).reshape(128, 24)
    out[:, 24:40] = inp['conv_w'][l].reshape(4, 4, 128).transpose(2, 1, 0).reshape(128, 16)
    out[:, 40:44] = inp['conv_b'][l].reshape(4, 128).T
    out[:, 44:52] = inp['lru_ba'][l].reshape(2, 4, 128).transpose(2, 0, 1).reshape(128, 8)
    out[:, 52:60] = inp['lru_bx'][l].reshape(2, 4, 128).transpose(2, 0, 1).reshape(128, 8)
    out[:, 60:68] = inp['lru_lambda'][l].reshape(2, 4, 128).transpose(2, 0, 1).reshape(128, 8)
    return out


class Res:
    __slots__ = ('w', 'r', 'name')

    def __init__(self, name=''):
        self.w = None
        self.r = {}
        self.name = name


class Trk:
    def __init__(self, nc, stack):
        self.nc = nc
        self.stack = stack
        self.eng = {'pe': nc.tensor, 'act': nc.scalar, 'dve': nc.vector, 'pool': nc.gpsimd, 'sp': nc.sync}
        self.sem = {}
        self.cnt = {}
        for e in ['pe', 'act', 'dve', 'pool']:
            self.sem[e] = stack.enter_context(nc.semaphore('s_' + e))
            self.cnt[e] = 0
        self.dsem = {}
        self.dcnt = {}
        self.waited = {e: {} for e in self.eng}

    def _dsem(self, key):
        if key not in self.dsem:
            self.dsem[key] = self.stack.enter_context(self.nc.semaphore('d_%d' % len(self.dsem)))
            self.dcnt[key] = 0
        return self.dsem[key]

    def _wait(self, e, deps, raw=()):
        need = {}
        for d, is_raw in [(d, False) for d in deps] + [(d, True) for d in raw]:
            if d is None:
                continue
            kind, key, v = d
            if kind == 'e' and key == e and e == 'pe':
                continue
            k = (kind, key)
            if self.waited[e].get(k, 0) >= v:
                continue
            if need.get(k, 0) < v:
                need[k] = v
        for (kind, key), v in need.items():
            sem = self.sem[key] if kind == 'e' else self.dsem[key]
            self.eng[e].wait_ge(sem, v)
            self.waited[e][(kind, key)] = v

    def _deps(self, reads, writes):
        deps = []
        for w in writes:
            deps.append(w.w)
            for (kind, key), v in w.r.items():
                deps.append((kind, key, v))
        return deps

    def _commit(self, tok, reads, writes):
        k = (tok[0], tok[1])
        for r in reads:
            r.r[k] = tok[2]
        for w in writes:
            w.w = tok
            w.r = {}

    def op(self, e, fn, reads=(), writes=()):
        self._wait(e, self._deps(reads, writes), raw=[r.w for r in reads])
        inst = fn()
        self.cnt[e] += 1
        inst.then_inc(self.sem[e], 1)
        self._commit(('e', e, self.cnt[e]), reads, writes)

    def dma(self, q, key, out, in_, reads=(), writes=()):
        self._wait(q, self._deps(reads, writes), raw=[r.w for r in reads])
        sem = self._dsem(key)
        inst = self.eng[q].dma_start(out=out, in_=in_)
        self.dcnt[key] += 16
        inst.then_inc(sem, 16)
        self._commit(('d', key, self.dcnt[key]), reads, writes)

    def barrier(self, engines=('pe', 'act', 'dve', 'pool', 'sp')):
        for e in engines:
            deps = [('e', f, self.cnt[f]) for f in self.cnt if f != e and self.cnt[f] > 0]
            deps += [('d', k, v) for k, v in self.dcnt.items() if v > 0]
            self._wait(e, deps)

    def wait_all(self, e):
        deps = [('e', f, self.cnt[f]) for f in self.cnt if f != e and self.cnt[f] > 0]
        deps += [('d', k, v) for k, v in self.dcnt.items() if v > 0]
        self._wait(e, deps)


def build(depth=DEPTH, stop=None, dbg=False):
    nc = bass.Bass("TRN2", target_bir_lowering=False)
    sizes = _tile_sizes()
    TOT = sum(sizes)
    offs = np.concatenate([[0], np.cumsum(sizes)]).astype(int)
    NTILE = len(sizes)

    x_in = nc.dram_tensor("x", [S, D], F32, kind="ExternalInput").ap()
    pT_in = nc.dram_tensor("pT", [DEPTH, 256, S], F32, kind="ExternalInput").ap()
    ws_in = nc.dram_tensor("ws", [DEPTH, 128, TOT], F32, kind="ExternalInput").ap()
    cols_in = nc.dram_tensor("cols", [128, DEPTH * NCOLS_SMALL], F32, kind="ExternalInput").ap()
    gvec_in = nc.dram_tensor("gvec", [DEPTH * 3 + 1, D], F32, kind="ExternalInput").ap()
    gn_in = nc.dram_tensor("gn", [DEPTH, 512], F32, kind="ExternalInput").ap()
    bt_in = nc.dram_tensor("bt", [DEPTH, 128, 8 * 14 * 64], F32, kind="ExternalInput").ap()
    ident_in = nc.dram_tensor("ident", [128, 128], F32, kind="ExternalInput").ap()
    cos_in = nc.dram_tensor("cos", [128, 16 * 64], F32, kind="ExternalInput").ap()
    sin_in = nc.dram_tensor("sin", [128, 16 * 64], F32, kind="ExternalInput").ap()
    dt_in = nc.dram_tensor("DT", [128, 512], F32, kind="ExternalInput").ap()
    qf_in = nc.dram_tensor("QF", [128, 512], F32, kind="ExternalInput").ap()
    qb_in = nc.dram_tensor("QB", [128, 512], F32, kind="ExternalInput").ap()
    kd_in = nc.dram_tensor("KD", [128, 8], F32, kind="ExternalInput").ap()
    y_out = nc.dram_tensor("y", [S, D], F32, kind="ExternalOutput").ap()
    xd = nc.dram_tensor("xd", [S, D], F32).ap()
    dbg_out = {}
    if dbg:
        for nm in ['hT', 'ynaT', 'yretT', 'ylruT']:
            dbg_out[nm] = nc.dram_tensor("dbg_" + nm, [128, 8 * S if nm == 'hT' else 4 * S], BF16,
                                         kind="ExternalOutput").ap()
        for nm in ['xE', 'xF', 'xG']:
            dbg_out[nm] = nc.dram_tensor("dbg_" + nm, [S, D], F32, kind="ExternalOutput").ap()
        dbg_out['qT'] = nc.dram_tensor("dbg_qT", [128, 2 * S], BF16, kind="ExternalOutput").ap()
        dbg_out['kT'] = nc.dram_tensor("dbg_kT", [128, 2 * S], BF16, kind="ExternalOutput").ap()
        dbg_out['Ve'] = nc.dram_tensor("dbg_Ve", [128, 16 * 4 * 65], BF16, kind="ExternalOutput").ap()
        dbg_out['Vo'] = nc.dram_tensor("dbg_Vo", [128, 16 * 4 * 65], BF16, kind="ExternalOutput").ap()
        dbg_out['bt'] = nc.dram_tensor("dbg_bt", [128, 8 * 14 * 64], BF16, kind="ExternalOutput").ap()
        for nm in ['krot', 'vtm', 'Sb']:
            dbg_out[nm] = nc.dram_tensor("dbg_" + nm, [128, 16 * 512], BF16, kind="ExternalOutput").ap()
        for nm in ['qTc', 'kTc', 'pTr', 'qfT', 'qbT', 'yrt']:
            dbg_out[nm] = nc.dram_tensor("dbg_" + nm, [128, 512], BF16, kind="ExternalOutput").ap()
        for nm in ['sge', 'o1']:
            dbg_out[nm] = nc.dram_tensor("dbg_" + nm, [128, 512], F32, kind="ExternalOutput").ap()

    stack = contextlib.ExitStack()
    with stack:
        T = Trk(nc, stack)

        sb_state = {'off': (int(nc.sbuf_base) + 63) // 64 * 64, 'n': 0}
        sb_addr = {}

        def sb(name, shape, dt):
            nbytes = int(np.prod(shape[1:])) * (2 if dt == BF16 else 4)
            nbytes = (nbytes + 63) // 64 * 64
            addr = sb_state['off']
            sb_state['off'] += nbytes
            assert sb_state['off'] <= int(nc.sbuf_top), (name, sb_state['off'], int(nc.sbuf_top))
            h = nc.alloc_sbuf_tensor_at(name, shape, dt, offset=addr)
            sb_addr[name] = addr
            return h

        ARENA_BYTES = 86 * 1024
        arena = sb("arena", [128, ARENA_BYTES // 4], F32)
        ARENA_ADDR = sb_addr["arena"]
        hT = sb("hT", [128, 8, S], BF16)
        ysT = [sb("ysT%d" % i, [128, 4, S], BF16) for i in range(3)]
        YS_ADDR = sb_addr["ysT0"]
        wbuf = [sb("wbuf%d" % i, [128, WTILE], BF16) for i in range(NSLOT)]
        ident = sb("ident", [128, 128], BF16)
        colp = sb("colp", [128, DEPTH * NCOLS_SMALL], F32)
        gbc = sb("gbc", [128, D], F32)
        ssq = sb("ssq", [128, 16], F32)
        rstd = sb("rstd", [128, 16], F32)
        hn = [sb("hn%d" % i, [128, D], BF16) for i in range(2)]
        junk = sb("junk", [128, D], BF16)
        kdt = sb("kdt", [128, 8], F32)
        sqf = [sb("sqf%d" % i, [128, 512], F32) for i in range(2)]
        lru_c = sb("lru_c", [128, DEPTH * 8], F32)
        mhalf = sb("mhalf", [128, 16], F32)
        lru_2c = sb("lru_2c", [128, DEPTH * 8], F32)

        class Carver:
            def __init__(self):
                self.off = 0
                self.n = 0

            def reset(self):
                self.off = 0

            def get(self, name, shape, dt):
                nbytes = int(np.prod(shape[1:])) * (2 if dt == BF16 else 4)
                nbytes = (nbytes + 63) // 64 * 64
                assert self.off + nbytes <= ARENA_BYTES, (name, self.off, nbytes)
                self.n += 1
                h = nc.alloc_sbuf_tensor_at("%s_%d" % (name, self.n), shape, dt, offset=ARENA_ADDR + self.off)
                self.off += nbytes
                return h

        CV = Carver()

        X = nc.alloc_sbuf_tensor_at("Xres", [128, NT, D], F32, offset=ARENA_ADDR)
        R_X = [Res('X%d' % t) for t in range(NT)]
        R_hT = [Res('hT%d' % t) for t in range(NT)]
        R_ys = [[Res('ys%d_%d' % (i, t)) for t in range(NT)] for i in range(3)]
        R_xd = [Res('xd%d' % t) for t in range(NT)]
        R_misc = Res('misc')
        R_junk = Res('junk')
        R_gG = Res('gbc')
        R_ssG = Res('ssq')
        R_hnG = [Res('hn0'), Res('hn1')]

        pbank = [stack.enter_context(nc.psum_tensor("pb%d" % i, [128, 512], F32)) for i in range(6)]
        tbank = [stack.enter_context(nc.psum_tensor("tb%d" % i, [128, 1024], BF16)) for i in range(2)]
        R_pb = [Res('pb%d' % i) for i in range(6)]
        R_tb = [Res('tb%d' % i) for i in range(2)]
        pctr = [0, 0]

        def ps_next():
            i = pctr[0] % 6
            pctr[0] += 1
            return pbank[i], R_pb[i]

        def tb_next():
            i = pctr[1] % 2
            pctr[1] += 1
            return tbank[i], R_tb[i]

        R_w = [Res('w%d' % i) for i in range(NSLOT)]
        wstate = {'issued': 0, 'taken': 0}
        total_tiles = depth * NTILE

        wdone = set()

        def w_issue_ready():
            while wstate['issued'] < total_tiles and wstate['issued'] < wstate['released'] + NSLOT:
                g = wstate['issued']
                l, i = divmod(g, NTILE)
                slot = g % NSLOT
                n = sizes[i]
                bsz = 2048 if n % 2048 == 0 else 512
                T.dma('pool', ('w', slot), wbuf[slot][:, 0:n].rearrange("p (a b) -> p a b", b=bsz),
                      ws_in[l, :, int(offs[i]):int(offs[i]) + n].rearrange("p (a b) -> p a b", b=bsz),
                      writes=[R_w[slot]])
                wstate['issued'] += 1

        def w_next(expect_size):
            g = wstate['taken']
            l, i = divmod(g, NTILE)
            assert sizes[i] == expect_size, (g, i, sizes[i], expect_size)
            assert g < wstate['issued'], "weight tile not issued (too many held)"
            slot = g % NSLOT
            wstate['taken'] += 1
            wheld.append(g)
            return wbuf[slot], R_w[slot]

        def w_done(k=1):
            for _ in range(k):
                g = wheld.pop(0)
                wdone.add(g)
            while wstate['released'] in wdone:
                wdone.discard(wstate['released'])
                wstate['released'] += 1
            w_issue_ready()

        def w_done_last():
            g = wheld.pop()
            wdone.add(g)
            while wstate['released'] in wdone:
                wdone.discard(wstate['released'])
                wstate['released'] += 1
            w_issue_ready()

        wheld = []
        wstate['released'] = 0

        T.dma('pool', 'c_setup', ident[:], ident_in, writes=[R_misc])
        T.dma('sp', 'c_setup', colp[:], cols_in, writes=[R_misc])
        T.dma('sp', 'c_setup', kdt[:], kd_in, writes=[R_misc])
        w_issue_ready()
        for t in range(NT):
            T.dma('sp', ('x', t % 4), X[:, t, :], x_in[t * 128:(t + 1) * 128, :], writes=[R_X[t]])

        R_lc = Res('lruc')
        R_mh = Res('mhalf')
        T.op('act', lambda: nc.scalar.activation(out=mhalf[:, :], in_=ident[:, 0:16], func=AF.Copy, scale=0.0, bias=-0.5),
             reads=[R_misc], writes=[R_mh])
        for l in range(depth):
            src = colp[:, l * NCOLS_SMALL + 60:l * NCOLS_SMALL + 68]
            dst = lru_c[:, l * 8:(l + 1) * 8]
            T.op('act', lambda: nc.scalar.activation(out=dst, in_=src, func=AF.Exp, scale=-1.0),
                 reads=[R_misc], writes=[R_lc])
            T.op('act', lambda: nc.scalar.activation(out=dst, in_=dst, func=AF.Ln, bias=1.0),
                 reads=[R_lc], writes=[R_lc])
            T.op('dve', lambda: nc.vector.tensor_scalar(out=lru_2c[:, l * 8:(l + 1) * 8], in0=dst, scalar1=-16.0,
                                                         scalar2=None, op0=ALU.mult), reads=[R_lc], writes=[R_lc])
            T.op('dve', lambda: nc.vector.tensor_scalar(out=dst, in0=dst, scalar1=-8.0, scalar2=None, op0=ALU.mult),
                 reads=[R_lc], writes=[R_lc])

        def rmsnorm_to_hT(grow, spill):
            R_g = R_gG
            T.dma('sp', 'c_g', gbc[:], gvec_in[grow:grow + 1, :].partition_broadcast(128), writes=[R_g],
                  reads=[])
            R_ss = R_ssG
            for t in range(NT):
                T.op('act', lambda: nc.scalar.activation(out=junk[:], in_=X[:, t, :], func=AF.Square,
                                                         accum_out=ssq[:, t:t + 1]),
                     reads=[R_X[t]], writes=[R_ss, R_junk])
                if spill:
                    T.dma('sp', ('xs', t % 4), xd[t * 128:(t + 1) * 128, :], X[:, t, :], reads=[R_X[t]],
                          writes=[R_xd[t]])
            T.op('dve', lambda: nc.vector.tensor_scalar(out=rstd[:], in0=ssq[:], scalar1=1.0 / D, scalar2=EPS,
                                                         op0=ALU.mult, op1=ALU.add), reads=[R_ss], writes=[R_ss])
            T.op('pool', lambda: nc.gpsimd.tensor_tensor(out=rstd[:], in0=rstd[:], in1=mhalf[:, :], op=ALU.pow),
                 reads=[R_ss, R_mh], writes=[R_ss])
            R_hn = R_hnG
            for t in range(NT):
                b = t % 2
                T.op('dve', lambda: nc.vector.scalar_tensor_tensor(out=hn[b][:], in0=X[:, t, :],
                                                                    scalar=rstd[:, t:t + 1], in1=gbc[:],
                                                                    op0=ALU.mult, op1=ALU.mult),
                     reads=[R_X[t], R_ss, R_g], writes=[R_hn[b]])
                tb, rtb = tb_next()
                for kc in range(8):
                    T.op('pe', lambda: nc.tensor.transpose(out=tb[:, kc * 128:(kc + 1) * 128],
                                                           in_=hn[b][:, kc * 128:(kc + 1) * 128], identity=ident[:]),
                         reads=[R_hn[b], R_misc], writes=[rtb])
                if t % 2 == 0:
                    T.op('act', lambda: nc.scalar.copy(out=hT[:, :, t * 128:(t + 1) * 128],
                                                       in_=tb[:, :].rearrange("p (k n) -> p k n", k=8)),
                         reads=[rtb], writes=[R_hT[t]])
                else:
                    T.op('dve', lambda: nc.vector.tensor_copy(out=hT[:, :, t * 128:(t + 1) * 128],
                                                               in_=tb[:, :].rearrange("p (k n) -> p k n", k=8)),
                         reads=[rtb], writes=[R_hT[t]])

        def transpose_tm_to_fm(src_tile, R_src, ncols, dstT, kc0, t, R_dst):
            nk = ncols // 128
            tb, rtb = tb_next()
            for j in range(nk):
                T.op('pe', lambda: nc.tensor.transpose(out=tb[:, j * 128:(j + 1) * 128],
                                                       in_=src_tile[:, j * 128:(j + 1) * 128], identity=ident[:]),
                     reads=[R_src, R_misc], writes=[rtb])
            T.op('act', lambda: nc.scalar.copy(out=dstT[:, kc0:kc0 + nk, t * 128:(t + 1) * 128],
                                               in_=tb[:, 0:nk * 128].rearrange("p (k n) -> p k n", k=nk)),
                 reads=[rtb], writes=[R_dst])

        def dump(name, src_ap, reads):
            if dbg and name in dbg_out:
                T.barrier()
                T.dma('sp', 'dbg', dbg_out[name], src_ap, reads=reads)
                T.barrier()

        for l in range(depth):
            cb = l * NCOLS_SMALL

            rmsnorm_to_hT(l * 3 + 0, spill=True)
            T.barrier(engines=('act', 'dve', 'pool', 'sp'))
            if dbg and l == 0:
                dump('hT', hT[:].rearrange("p k n -> p (k n)"), R_hT)
            if stop == 'A':
                break

            CV.reset()
            btab = CV.get("btab", [128, 8, 14, 64], BF16)
            R_bt = Res('bt')
            T.dma('pool', 'c_bt', btab[:].rearrange("p a b c -> p a (b c)"),
                  bt_in[l].rearrange("p (a b) -> p a b", a=8), writes=[R_bt])
            qT = CV.get("qT", [128, 2, S], BF16)
            kT = CV.get("kT", [128, 2, S], BF16)
            Ve = CV.get("Ve", [128, 16, 4, 65], BF16)
            Vo = CV.get("Vo", [128, 16, 4, 65], BF16)
            NSC = 4
            scb = [CV.get("sc%d" % i, [128, 512], F32) for i in range(NSC)]
            ptb = [CV.get("pt%d" % i, [128, 512], BF16) for i in range(NSC)]
            ytile = [CV.get("yt%d" % i, [128, 256], BF16) for i in range(2)]
            rden = [CV.get("rd%d" % i, [128, 4], F32) for i in range(2)]
            R_sc = [Res() for _ in range(NSC)]
            R_pt = [Res() for _ in range(NSC)]
            R_yt = [Res(), Res()]
            R_rd = [Res(), Res()]
            R_q = [Res() for _ in range(4)]
            R_k = [Res() for _ in range(4)]
            R_ve = [Res() for _ in range(NT)]
            R_vo = [Res() for _ in range(NT)]
            R_ones = Res('ones')
            for Vt_ in (Ve, Vo):
                T.op('act', lambda: nc.scalar.activation(out=Vt_[:, :, :, 64],
                                                         in_=ident[:, 0:64].rearrange("p (a b) -> p a b", a=16),
                                                         func=AF.Copy, scale=0.0, bias=1.0),
                     reads=[R_misc], writes=[R_ones])
            sctr = 0
            for ps in range(2):
                wqk, rwqk = w_next(4096)
                wqk3 = wqk[:, :].rearrange("p (k n) -> p k n", k=8)
                for which, dst, Rd in ((0, qT, R_q), (1, kT, R_k)):
                    for hp in range(2):
                        for tg in range(4):
                            pb, rpb = ps_next()
                            for kc in range(8):
                                c0 = which * 256 + hp * 128
                                T.op('pe', lambda: nc.tensor.matmul(pb[:, :], lhsT=wqk3[:, kc, c0:c0 + 128],
                                                                    rhs=hT[:, kc, tg * 512:(tg + 1) * 512],
                                                                    start=(kc == 0), stop=(kc == 7)),
                                     reads=[rwqk] + R_hT[tg * 4:tg * 4 + 4], writes=[rpb])
                            sc_ = 0.125 if which == 0 else 1.0
                            T.op('act', lambda: nc.scalar.activation(out=dst[:, hp, tg * 512:(tg + 1) * 512],
                                                                     in_=pb[:, :], func=AF.Copy, scale=sc_),
                                 reads=[rpb], writes=[Rd[tg]])
                w_done()
                wv, rwv = w_next(2048)
                wv3 = wv[:, 0:2048].rearrange("p (k n) -> p k n", k=8)
                for par, Vt, Rv, ntl in ((0, Ve, R_ve, 16), (1, Vo, R_vo, 15)):
                    for t in range(ntl):
                        tok0 = t * 128 + par * 64
                        pb, rpb = ps_next()
                        rr = R_hT[t:t + 2] if par else R_hT[t:t + 1]
                        for kc in range(8):
                            T.op('pe', lambda: nc.tensor.matmul(pb[:, 0:256], lhsT=hT[:, kc, tok0:tok0 + 128],
                                                                rhs=wv3[:, kc, :], start=(kc == 0), stop=(kc == 7)),
                                 reads=[rwv] + rr, writes=[rpb])
                        T.op('act', lambda: nc.scalar.copy(out=Vt[:, t, :, 0:64],
                                                           in_=pb[:, 0:256].rearrange("p (h d) -> p h d", h=4)),
                             reads=[rpb, R_ones], writes=[Rv[t]])
                w_done()
                steps = [(t, a, hh) for t in range(NT) for a in range(2) for hh in range(2)]
                front = {}

                def na_front(sidx):
                    t, a, hh = steps[sidx]
                    r = 2 * t + a
                    rs = min(max(r - 4, 0), 24)
                    d0 = rs - r + 7
                    pb, rpb = ps_next()
                    for hp in range(2):
                        for i in range(4):
                            k0 = (rs + 2 * i) * 64
                            T.op('pe', lambda: nc.tensor.matmul(
                                pb[:, (hp * 4 + i) * 64:(hp * 4 + i + 1) * 64],
                                lhsT=kT[hh * 64:(hh + 1) * 64, hp, k0:k0 + 128],
                                rhs=qT[hh * 64:(hh + 1) * 64, hp, r * 64:(r + 1) * 64],
                                start=True, stop=True),
                                reads=[R_k[k0 // 512], R_k[(k0 + 127) // 512], R_q[r // 8]], writes=[rpb])
                    si = sidx % NSC
                    hg = ps * 4 + hh
                    T.op('dve', lambda: nc.vector.tensor_tensor(
                        out=scb[si][:, :].rearrange("p (h i c) -> p h i c", h=2, i=4),
                        in0=pb[:, :].rearrange("p (h i c) -> p h i c", h=2, i=4),
                        in1=btab[:, hg:hg + 3:2, d0:d0 + 7:2, :], op=ALU.add),
                        reads=[rpb, R_bt], writes=[R_sc[si]])
                    T.op('act', lambda: nc.scalar.activation(out=ptb[si][:, :], in_=scb[si][:, :], func=AF.Exp),
                         reads=[R_sc[si]], writes=[R_pt[si]])

                obs = {}

                def na_back(sidx):
                    t, a, hh = steps[sidx]
                    r = 2 * t + a
                    rs = min(max(r - 4, 0), 24)
                    si = sidx % NSC
                    if (a, hh) == (0, 0):
                        obs[t] = ps_next()
                    ob, rob = obs[t]
                    for hp in range(2):
                        hl = hp * 2 + hh
                        for i in range(4):
                            kr = rs + 2 * i
                            if rs % 2 == 0:
                                Vt, Rv, tl = Ve, R_ve, kr // 2
                            else:
                                Vt, Rv, tl = Vo, R_vo, (kr - 1) // 2
                            T.op('pe', lambda: nc.tensor.matmul(
                                ob[a * 64:(a + 1) * 64, hl * 65:(hl + 1) * 65],
                                lhsT=ptb[si][:, (hp * 4 + i) * 64:(hp * 4 + i + 1) * 64],
                                rhs=Vt[:, tl, hl, :], start=(i == 0), stop=(i == 3)),
                                reads=[R_pt[si], Rv[tl]], writes=[rob])
                    if (a, hh) == (1, 1):
                        yb = t % 2
                        ob3 = ob[:, 0:260].rearrange("p (h d) -> p h d", h=4)
                        T.op('dve', lambda: nc.vector.reciprocal(out=rden[yb][:, :], in_=ob3[:, :, 64]),
                             reads=[rob], writes=[R_rd[yb]])
                        T.op('dve', lambda: nc.vector.tensor_tensor(
                            out=ytile[yb][:, :].rearrange("p (h d) -> p h d", h=4), in0=ob3[:, :, 0:64],
                            in1=rden[yb][:, :].unsqueeze(2).to_broadcast([128, 4, 64]), op=ALU.mult),
                            reads=[rob, R_rd[yb]], writes=[R_yt[yb]])
                        pend_tr.append(t)

                pend_tr = []

                def na_flush():
                    while pend_tr:
                        t_ = pend_tr.pop(0)
                        transpose_tm_to_fm(ytile[t_ % 2], R_yt[t_ % 2], 256, ysT[0], ps * 2, t_, R_ys[0][t_])

                GRP = 2
                ngrp = len(steps) // GRP
                for k_ in range(GRP):
                    na_front(k_)
                for g_ in range(ngrp):
                    if g_ + 1 < ngrp:
                        for k_ in range(GRP):
                            na_front((g_ + 1) * GRP + k_)
                    if steps[g_ * GRP][1:] == (1, 0):
                        na_flush()
                    for k_ in range(GRP):
                        na_back(g_ * GRP + k_)
                na_flush()
            T.barrier(engines=('act', 'dve', 'pool', 'sp'))
            if dbg and l == 0:
                dump('ynaT', ysT[0][:].rearrange("p k n -> p (k n)"), R_ys[0])
                dump('qT', qT[:].rearrange("p k n -> p (k n)"), [])
                dump('kT', kT[:].rearrange("p k n -> p (k n)"), [])
                dump('Ve', Ve[:].rearrange("p a b c -> p (a b c)"), [])
                dump('Vo', Vo[:].rearrange("p a b c -> p (a b c)"), [])
                dump('bt', btab[:].rearrange("p a b c -> p (a b c)"), [])
            if stop == 'B':
                break

            CV.reset()
            cosT = CV.get("cos", [128, 16, 64], F32)
            sinT = CV.get("sin", [128, 16, 64], F32)
            DTt = CV.get("DT", [128, 4, 128], F32)
            QFt = CV.get("QF", [128, 4, 128], BF16)
            QBt = CV.get("QB", [128, 4, 128], BF16)
            gnb = gbc
            R_ct = Res('ctab')
            T.dma('sp', 'c_cos', cosT[:].rearrange("p a b -> p (a b)"), cos_in, writes=[R_ct])
            T.dma('sp', 'c_sin', sinT[:].rearrange("p a b -> p (a b)"), sin_in, writes=[R_ct])
            T.dma('sp', 'c_dt', DTt[:].rearrange("p a b -> p (a b)"), dt_in, writes=[R_ct])
            T.dma('pool', 'c_qf', QFt[:].rearrange("p a b -> p (a b)"), qf_in, writes=[R_ct])
            T.dma('pool', 'c_qb', QBt[:].rearrange("p a b -> p (a b)"), qb_in, writes=[R_ct])
            T.dma('sp', 'c_gn', gnb[:, 0:512], gn_in[l:l + 1, :].partition_broadcast(128), writes=[R_ct])
            krot = CV.get("krot", [128, NT, 512], BF16)
            vtm = CV.get("vtm", [128, NT, 512], BF16)
            Sb = CV.get("Sb", [128, NT, 512], BF16)
            Sf32 = CV.get("Sf32", [128, 512], F32)
            Sfb = [CV.get("Sfb%d" % i, [128, 512], BF16) for i in range(2)]
            rt1 = sqf[0]
            rt2 = sqf[1]
            kd = [CV.get("kd%d" % i, [128, 512], BF16) for i in range(2)]
            qrot = [CV.get("qrot%d" % i, [128, 512], BF16) for i in range(2)]
            qTc = [CV.get("qTc%d" % i, [128, 4, 128], BF16) for i in range(2)]
            kTc = [CV.get("kTc%d" % i, [128, 4, 128], BF16) for i in range(2)]
            qfT = [CV.get("qfT%d" % i, [128, 4, 128], BF16) for i in range(2)]
            qbT = [CV.get("qbT%d" % i, [128, 4, 128], BF16) for i in range(2)]
            pTr = [CV.get("pTr%d" % i, [128, 4, 128], BF16) for i in range(2)]
            sge = [CV.get("sge%d" % i, [128, 512], F32) for i in range(2)]
            o1 = [nc.alloc_sbuf_tensor_at("o1_%d_%d" % (l, i), [128, 512], F32, offset=sb_addr["hn%d" % i])
                  for i in range(2)]
            yrt = [CV.get("yrt%d" % i, [128, 512], BF16) for i in range(2)]
            gss = [ssq[:, 0:4], ssq[:, 4:8]]
            R_kr = [Res() for _ in range(NT)]
            R_v = [Res() for _ in range(NT)]
            R_Sb = [Res() for _ in range(NT)]
            R_S32 = Res()
            R_Sfb = [Res(), Res()]
            R_rt = Res()
            R_kd = [Res(), Res()]
            R_qrot = [Res(), Res()]
            R_qTc = [Res(), Res()]
            R_kTc = [Res(), Res()]
            R_qfT = [Res(), Res()]
            R_qbT = [Res(), Res()]
            R_pTr = [Res(), Res()]
            R_sge = [Res(), Res()]
            R_o1 = [Res(), Res()]
            R_yrt = [Res(), Res()]
            R_gss = [Res(), Res()]

            def rotary(pb, rpb, t, dst, R_dst):
                p4 = pb[:, :].rearrange("p (h two m) -> p h two m", h=4, two=2)
                d4 = dst.rearrange("p (h two m) -> p h two m", h=4, two=2)
                a4 = rt1[:, :].rearrange("p (h two m) -> p h two m", h=4, two=2)
                b4 = rt2[:, :].rearrange("p (h two m) -> p h two m", h=4, two=2)
                cb_ = cosT[:, t, :].unsqueeze(1).unsqueeze(1).to_broadcast([128, 4, 2, 64])
                sb_ = sinT[:, t, :].unsqueeze(1).unsqueeze(1).to_broadcast([128, 4, 2, 64])
                T.op('dve', lambda: nc.vector.tensor_tensor(out=a4, in0=p4, in1=cb_, op=ALU.mult),
                     reads=[rpb, R_ct], writes=[R_rt])
                T.op('dve', lambda: nc.vector.tensor_tensor(out=b4, in0=p4[:, :, ::-1, :], in1=sb_, op=ALU.mult),
                     reads=[rpb, R_ct], writes=[R_rt])
                T.op('dve', lambda: nc.vector.tensor_tensor(out=d4[:, :, 0, :], in0=a4[:, :, 0, :], in1=b4[:, :, 0, :],
                                                             op=ALU.subtract), reads=[R_rt], writes=[R_dst])
                T.op('dve', lambda: nc.vector.tensor_tensor(out=d4[:, :, 1, :], in0=a4[:, :, 1, :], in1=b4[:, :, 1, :],
                                                             op=ALU.add), reads=[R_rt], writes=[R_dst])

            def proj_tm(w3, rw, t):
                pb, rpb = ps_next()
                for kc in range(8):
                    T.op('pe', lambda: nc.tensor.matmul(pb[:, :], lhsT=hT[:, kc, t * 128:(t + 1) * 128],
                                                        rhs=w3[:, kc, :], start=(kc == 0), stop=(kc == 7)),
                         reads=[rw, R_hT[t]], writes=[rpb])
                return pb, rpb

            wk, rwk = w_next(4096)
            wk3 = wk[:, :].rearrange("p (k n) -> p k n", k=8)
            wv_, rwv_ = w_next(4096)
            wv3_ = wv_[:, :].rearrange("p (k n) -> p k n", k=8)
            for t in range(NT):
                pb, rpb = proj_tm(wk3, rwk, t)
                rotary(pb, rpb, t, krot[:, t, :], R_kr[t])
                pb, rpb = proj_tm(wv3_, rwv_, t)
                T.op('act', lambda: nc.scalar.copy(out=vtm[:, t, :], in_=pb[:, :]), reads=[rpb], writes=[R_v[t]])
            w_done(2)

            def state_u(c, dirn):
                b = c % 2
                T.op('dve', lambda: nc.vector.tensor_tensor(
                    out=kd[b][:, :].rearrange("p (h d) -> p h d", h=4),
                    in0=krot[:, c, :].rearrange("p (h d) -> p h d", h=4),
                    in1=kdt[:, dirn * 4:dirn * 4 + 4].unsqueeze(2).to_broadcast([128, 4, 128]), op=ALU.mult),
                    reads=[R_kr[c], R_misc], writes=[R_kd[b]])
                pb, rpb = ps_next()
                for h in range(4):
                    T.op('pe', lambda: nc.tensor.matmul(pb[:, h * 128:(h + 1) * 128], lhsT=kd[b][:, h * 128:(h + 1) * 128],
                                                        rhs=vtm[:, c, h * 128:(h + 1) * 128], start=True, stop=True),
                         reads=[R_kd[b], R_v[c]], writes=[rpb])
                return pb, rpb

            def state_acc(pb, rpb, S32, R_S, cdec, first):
                if first:
                    T.op('dve', lambda: nc.vector.tensor_copy(out=S32[:, :], in_=pb[:, :]), reads=[rpb], writes=[R_S])
                else:
                    for h in range(4):
                        T.op('dve', lambda: nc.vector.scalar_tensor_tensor(
                            out=S32[:, h * 128:(h + 1) * 128], in0=S32[:, h * 128:(h + 1) * 128], scalar=cdec[h],
                            in1=pb[:, h * 128:(h + 1) * 128], op0=ALU.mult, op1=ALU.add),
                            reads=[rpb, R_S], writes=[R_S])

            u_next = state_u(NT - 1, 1)
            for c in range(NT - 1, 0, -1):
                u_cur = u_next
                if c - 1 >= 1:
                    u_next = state_u(c - 1, 1)
                state_acc(u_cur[0], u_cur[1], Sf32, R_S32, _CDB, first=(c == NT - 1))
                T.op('act', lambda: nc.scalar.copy(out=Sb[:, c - 1, :], in_=Sf32[:, :]), reads=[R_S32],
                     writes=[R_Sb[c - 1]])
            wq, rwq = w_next(4096)
            wq3 = wq[:, :].rearrange("p (k n) -> p k n", k=8)
            wg_, rwg_ = w_next(4096)
            wg3_ = wg_[:, :].rearrange("p (k n) -> p k n", k=8)
            def ret_front(c):
                b = c % 2
                pb, rpb = proj_tm(wq3, rwq, c)
                rotary(pb, rpb, c, qrot[b][:, :], R_qrot[b])
                for src, Rs, dstc, Rdc in ((qrot[b][:, :], R_qrot[b], qTc[b], R_qTc[b]),
                                           (krot[:, c, :], R_kr[c], kTc[b], R_kTc[b])):
                    tb, rtb = tb_next()
                    for h in range(4):
                        T.op('pe', lambda: nc.tensor.transpose(out=tb[:, h * 128:(h + 1) * 128],
                                                               in_=src[:, h * 128:(h + 1) * 128], identity=ident[:]),
                             reads=[Rs, R_misc], writes=[rtb])
                    T.op('act', lambda: nc.scalar.copy(out=dstc[:].rearrange("p h n -> p (h n)"), in_=tb[:, 0:512]),
                         reads=[rtb], writes=[Rdc])
                if c > 0:
                    T.op('dve', lambda: nc.vector.tensor_tensor(out=qfT[b][:], in0=qTc[b][:], in1=QFt[:], op=ALU.mult),
                         reads=[R_qTc[b], R_ct], writes=[R_qfT[b]])
                if c < NT - 1:
                    T.op('dve', lambda: nc.vector.tensor_tensor(out=qbT[b][:], in0=qTc[b][:], in1=QBt[:], op=ALU.mult),
                         reads=[R_qTc[b], R_ct], writes=[R_qbT[b]])
                pg, rpg = proj_tm(wg3_, rwg_, c)
                T.op('act', lambda: nc.scalar.activation(out=sge[b][:, :], in_=pg[:, :], func=AF.Silu), reads=[rpg],
                     writes=[R_sge[b]])
                T.op('dve', lambda: nc.vector.tensor_tensor(out=sge[b][:, :], in0=sge[b][:, :], in1=gnb[:, 0:512],
                                                             op=ALU.mult), reads=[R_sge[b], R_ct], writes=[R_sge[b]])
                pscore, rps = ps_next()
                for h in range(4):
                    T.op('pe', lambda: nc.tensor.matmul(pscore[:, h * 128:(h + 1) * 128], lhsT=kTc[b][:, h, :],
                                                        rhs=qTc[b][:, h, :], start=True, stop=True),
                         reads=[R_kTc[b], R_qTc[b]], writes=[rps])
                T.op('dve', lambda: nc.vector.tensor_tensor(out=pTr[b][:].rearrange("p h n -> p (h n)"),
                                                             in0=pscore[:, :],
                                                             in1=DTt[:].rearrange("p h n -> p (h n)"), op=ALU.mult),
                     reads=[rps, R_ct], writes=[R_pTr[b]])
            def ret_mid(c):
                b = c % 2
                po, rpo = ps_next()
                sfb_prev = Sfb[(c + 1) % 2]
                R_sfb_prev = R_Sfb[(c + 1) % 2]
                for h in range(4):
                    last_is = 0 if (c == 0 and c == NT - 1) else None
                    steps = [('intra', None)]
                    if c > 0:
                        steps.append(('f', None))
                    if c < NT - 1:
                        steps.append(('b', None))
                    for si_, (kind, _) in enumerate(steps):
                        st_, sp_ = (si_ == 0), (si_ == len(steps) - 1)
                        if kind == 'intra':
                            T.op('pe', lambda: nc.tensor.matmul(po[:, h * 128:(h + 1) * 128], lhsT=pTr[b][:, h, :],
                                                                rhs=vtm[:, c, h * 128:(h + 1) * 128], start=st_,
                                                                stop=sp_),
                                 reads=[R_pTr[b], R_v[c]], writes=[rpo])
                        elif kind == 'f':
                            T.op('pe', lambda: nc.tensor.matmul(po[:, h * 128:(h + 1) * 128], lhsT=qfT[b][:, h, :],
                                                                rhs=sfb_prev[:, h * 128:(h + 1) * 128], start=st_,
                                                                stop=sp_),
                                 reads=[R_qfT[b], R_sfb_prev], writes=[rpo])
                        else:
                            T.op('pe', lambda: nc.tensor.matmul(po[:, h * 128:(h + 1) * 128], lhsT=qbT[b][:, h, :],
                                                                rhs=Sb[:, c, h * 128:(h + 1) * 128], start=st_,
                                                                stop=sp_),
                                 reads=[R_qbT[b], R_Sb[c]], writes=[rpo])
                if c < NT - 1:
                    u_ = state_u(c, 0)
                    state_acc(u_[0], u_[1], Sf32, R_S32, _CDF, first=(c == 0))
                    T.op('act', lambda: nc.scalar.copy(out=Sfb[b][:, :], in_=Sf32[:, :]), reads=[R_S32],
                         writes=[R_Sfb[b]])
                T.op('act', lambda: nc.scalar.activation(out=o1[b][:, :], in_=po[:, :], func=AF.Square),
                     reads=[rpo], writes=[R_o1[b]])
                T.op('dve', lambda: nc.vector.tensor_reduce(out=gss[b][:, :],
                                                             in_=o1[b][:, :].rearrange("p (h d) -> p h d", h=4),
                                                             axis=mybir.AxisListType.X, op=ALU.add),
                     reads=[R_o1[b]], writes=[R_gss[b]])
                T.op('dve', lambda: nc.vector.tensor_scalar(out=gss[b][:, :], in0=gss[b][:, :], scalar1=1.0 / 128,
                                                             scalar2=EPS, op0=ALU.mult, op1=ALU.add),
                     reads=[R_gss[b]], writes=[R_gss[b]])
                T.op('pool', lambda: nc.gpsimd.tensor_tensor(out=gss[b][:, :], in0=gss[b][:, :], in1=mhalf[:, 0:4],
                                                              op=ALU.pow), reads=[R_gss[b], R_mh],
                     writes=[R_gss[b]])
                T.op('dve', lambda: nc.vector.tensor_tensor(
                    out=o1[b][:, :].rearrange("p (h d) -> p h d", h=4),
                    in0=po[:, :].rearrange("p (h d) -> p h d", h=4),
                    in1=gss[b][:, :].unsqueeze(2).to_broadcast([128, 4, 128]), op=ALU.mult),
                    reads=[rpo, R_gss[b]], writes=[R_o1[b]])
                T.op('dve', lambda: nc.vector.tensor_tensor(out=yrt[b][:, :], in0=o1[b][:, :], in1=sge[b][:, :],
                                                             op=ALU.mult), reads=[R_o1[b], R_sge[b]],
                     writes=[R_yrt[b]])

            def ret_back(c):
                b = c % 2
                transpose_tm_to_fm(yrt[b], R_yrt[b], 512, ysT[1], 0, c, R_ys[1][c])

            ret_front(0)
            for c in range(NT):
                if c + 1 < NT:
                    ret_front(c + 1)
                ret_mid(c)
                if c >= 1:
                    ret_back(c - 1)
            ret_back(NT - 1)
            w_done(2)
            T.barrier(engines=('act', 'dve', 'pool', 'sp'))
            if dbg and l == 0:
                dump('yretT', ysT[1][:].rearrange("p k n -> p (k n)"), R_ys[1])
                dump('krot', krot[:].rearrange("p a b -> p (a b)"), [])
                dump('vtm', vtm[:].rearrange("p a b -> p (a b)"), [])
                dump('Sb', Sb[:].rearrange("p a b -> p (a b)"), [])
                for nm_, t_ in (('qTc', qTc[0]), ('kTc', kTc[0]), ('pTr', pTr[0]), ('qfT', qfT[0]), ('qbT', qbT[0])):
                    dump(nm_, t_[:].rearrange("p a b -> p (a b)"), [])
                dump('yrt', yrt[0][:, :], [])
                dump('sge', sge[0][:, :], [])
                dump('o1', o1[0][:, :], [])
            if stop == 'C':
                break

            CV.reset()
            xcp = CV.get("xcp", [128, S + 4], F32)
            xf = CV.get("xf", [128, S], F32)
            xfb = CV.get("xfb", [128, S], BF16)
            gg = CV.get("gg", [128, S], BF16)
            abuf = [CV.get("abuf%d" % i, [128, S], F32) for i in range(2)]
            bbuf = [CV.get("bbuf%d" % i, [128, S], F32) for i in range(2)]
            mbuf = [CV.get("mbuf%d" % i, [128, S], F32) for i in range(2)]
            R_xcp, R_xf, R_xfb, R_gg = (Res() for _ in range(4))
            R_a = [Res(), Res()]
            R_b = [Res(), Res()]
            R_m = [Res(), Res()]
            R_pad = Res()
            T.op('act', lambda: nc.scalar.activation(out=xcp[:, 0:2], in_=ident[:, 0:2], func=AF.Copy, scale=0.0),
                 reads=[R_misc], writes=[R_pad])
            T.op('act', lambda: nc.scalar.activation(out=xcp[:, S + 2:S + 4], in_=ident[:, 0:2], func=AF.Copy,
                                                     scale=0.0), reads=[R_misc], writes=[R_pad])
            for ch in range(4):
                wl, rwl = w_next(2560)
                wl3 = wl[:, 0:2048].rearrange("p (k n) -> p k n", k=8)
                wm = wl[:, 2048:2560].rearrange("p (m n) -> p m n", m=4)
                for which in range(2):
                    for tg in range(4):
                        pb, rpb = ps_next()
                        for kc in range(8):
                            T.op('pe', lambda: nc.tensor.matmul(pb[:, :], lhsT=wl3[:, kc, which * 128:(which + 1) * 128],
                                                                rhs=hT[:, kc, tg * 512:(tg + 1) * 512],
                                                                start=(kc == 0), stop=(kc == 7)),
                                 reads=[rwl] + R_hT[tg * 4:tg * 4 + 4], writes=[rpb])
                        if which == 0:
                            T.op('act', lambda: nc.scalar.copy(out=xcp[:, 2 + tg * 512:2 + (tg + 1) * 512], in_=pb[:, :]),
                                 reads=[rpb, R_pad], writes=[R_xcp])
                        else:
                            T.op('act', lambda: nc.scalar.activation(out=gg[:, tg * 512:(tg + 1) * 512], in_=pb[:, :],
                                                                     func=AF.Gelu_apprx_tanh),
                                 reads=[rpb], writes=[R_gg])
                cw = lambda k_: colp[:, cb + 24 + ch * 4 + k_:cb + 24 + ch * 4 + k_ + 1]
                T.op('dve', lambda: nc.vector.tensor_scalar(out=xf[:, :], in0=xcp[:, 0:S], scalar1=cw(0),
                                                             scalar2=colp[:, cb + 40 + ch:cb + 41 + ch], op0=ALU.mult,
                                                             op1=ALU.add), reads=[R_xcp, R_misc], writes=[R_xf])
                for k_ in range(1, 4):
                    T.op('dve', lambda: nc.vector.scalar_tensor_tensor(out=xf[:, :], in0=xcp[:, k_:k_ + S],
                                                                        scalar=cw(k_), in1=xf[:, :], op0=ALU.mult,
                                                                        op1=ALU.add), reads=[R_xcp, R_xf, R_misc],
                         writes=[R_xf])
                T.op('act', lambda: nc.scalar.copy(out=xfb[:, :], in_=xf[:, :]), reads=[R_xf], writes=[R_xfb])
                def lru_dir(dirn):
                    ci = l * 8 + dirn * 4 + ch
                    ba_ = colp[:, cb + 44 + dirn * 4 + ch:cb + 45 + dirn * 4 + ch]
                    bx_ = colp[:, cb + 52 + dirn * 4 + ch:cb + 53 + dirn * 4 + ch]
                    ab, bb, mb = abuf[dirn], bbuf[dirn], mbuf[dirn]
                    Ra, Rb, Rm = R_a[dirn], R_b[dirn], R_m[dirn]
                    for tg in range(4):
                        for mi, bias_, dstb, Rd in ((dirn * 2, ba_, ab, Ra), (dirn * 2 + 1, bx_, bb, Rb)):
                            pb, rpb = ps_next()
                            T.op('pe', lambda: nc.tensor.matmul(pb[:, :], lhsT=wm[:, mi, :],
                                                                rhs=xfb[:, tg * 512:(tg + 1) * 512], start=True,
                                                                stop=True), reads=[rwl, R_xfb], writes=[rpb])
                            T.op('act', lambda: nc.scalar.activation(out=dstb[:, tg * 512:(tg + 1) * 512], in_=pb[:, :],
                                                                     func=AF.Sigmoid, bias=bias_),
                                 reads=[rpb, R_misc], writes=[Rd])
                        yield
                    T.op('act', lambda: nc.scalar.activation(out=mb[:, :], in_=ab[:, :], func=AF.Exp,
                                                             scale=lru_2c[:, ci:ci + 1]), reads=[Ra, R_lc],
                         writes=[Rm])
                    T.op('act', lambda: nc.scalar.activation(out=ab[:, :], in_=ab[:, :], func=AF.Exp,
                                                             scale=lru_c[:, ci:ci + 1]), reads=[Ra, R_lc],
                         writes=[Ra])
                    T.op('pool', lambda: nc.gpsimd.tensor_tensor(out=bb[:, :], in0=bb[:, :], in1=xf[:, :],
                                                                  op=ALU.mult), reads=[Rb, R_xf], writes=[Rb])
                    yield
                    T.op('act', lambda: nc.scalar.activation(out=mb[:, :], in_=mb[:, :], func=AF.Sqrt, scale=-1.0,
                                                             bias=1.0), reads=[Rm], writes=[Rm])
                    first = S - 1 if dirn else 0
                    T.op('dve', lambda: nc.vector.tensor_scalar(out=mb[:, first:first + 1],
                                                                 in0=mb[:, first:first + 1], scalar1=0.0, scalar2=1.0,
                                                                 op0=ALU.mult, op1=ALU.add), reads=[Rm], writes=[Rm])
                    yield
                    T.op('dve', lambda: nc.vector.tensor_tensor(out=bb[:, :], in0=bb[:, :], in1=mb[:, :],
                                                                 op=ALU.mult), reads=[Rb, Rm], writes=[Rb])
                    yield
                    if dirn == 0:
                        T.op('dve', lambda: nc.vector.tensor_tensor_scan(out=bb[:, :], data0=ab[:, :],
                                                                          data1=bb[:, :], initial=0.0, op0=ALU.mult,
                                                                          op1=ALU.add), reads=[Ra, Rb], writes=[Rb])
                    else:
                        T.op('dve', lambda: nc.vector.tensor_tensor_scan(out=bb[:, ::-1], data0=ab[:, ::-1],
                                                                          data1=bb[:, ::-1], initial=0.0,
                                                                          op0=ALU.mult, op1=ALU.add),
                             reads=[Ra, Rb], writes=[Rb])
                    yield

                gens = [lru_dir(0), lru_dir(1)]
                alive = [True, True]
                while any(alive):
                    for gi_ in range(2):
                        if alive[gi_]:
                            try:
                                next(gens[gi_])
                            except StopIteration:
                                alive[gi_] = False
                T.op('pool', lambda: nc.gpsimd.tensor_tensor(out=bbuf[0][:, :], in0=bbuf[0][:, :], in1=bbuf[1][:, :],
                                                              op=ALU.add), reads=[R_b[0], R_b[1]], writes=[R_b[0]])
                T.op('pool', lambda: nc.gpsimd.tensor_tensor(out=ysT[2][:, ch, :], in0=bbuf[0][:, :], in1=gg[:, :],
                                                              op=ALU.mult), reads=[R_b[0], R_gg], writes=R_ys[2])
                w_done()
            T.barrier(engines=('act', 'dve', 'pool', 'sp'))
            if dbg and l == 0:
                dump('ylruT', ysT[2][:].rearrange("p k n -> p (k n)"), R_ys[2])
            if stop == 'D':
                break

            CV.reset()
            macc = CV.get("macc", [128, 8, S], F32)
            gtmp = [CV.get("gtmp%d" % i, [128, 512], F32) for i in range(2)]
            ptmp = [CV.get("ptmp%d" % i, [128, 512], F32) for i in range(2)]
            R_macc = [[Res() for _ in range(4)] for _ in range(8)]
            R_gt = [Res(), Res()]
            R_ptm = [Res(), Res()]
            mT = nc.alloc_sbuf_tensor_at("mT_%d" % l, [128, 8, S], BF16, offset=YS_ADDR)
            R_mT = [Res() for _ in range(4)]
            gctr = 0
            for n in range(3):
                wb, rwb = w_next(4096)
                wb3 = wb[:, :].rearrange("p (k n) -> p k n", k=4)
                for half in range(2):
                    wgt, rwgt = w_next(4096)
                    wgt3 = wgt[:, :].rearrange("p (k n) -> p k n", k=8)
                    for j in range(4):
                        cc = half * 4 + j
                        bcol = colp[:, cb + n * 8 + cc:cb + n * 8 + cc + 1]
                        for tg in range(4):
                            pl, rpl = ps_next()
                            for kc in range(8):
                                T.op('pe', lambda: nc.tensor.matmul(pl[:, :], lhsT=wgt3[:, kc, j * 128:(j + 1) * 128],
                                                                    rhs=hT[:, kc, tg * 512:(tg + 1) * 512],
                                                                    start=(kc == 0), stop=(kc == 7)),
                                     reads=[rwgt] + R_hT[tg * 4:tg * 4 + 4], writes=[rpl])
                            pbr, rpbr = ps_next()
                            for kc in range(4):
                                T.op('pe', lambda: nc.tensor.matmul(pbr[:, :], lhsT=wb3[:, kc, cc * 128:(cc + 1) * 128],
                                                                    rhs=ysT[n][:, kc, tg * 512:(tg + 1) * 512],
                                                                    start=(kc == 0), stop=(kc == 3)),
                                     reads=[rwb] + R_ys[n][tg * 4:tg * 4 + 4], writes=[rpbr])
                            gi = gctr % 2
                            gctr += 1
                            T.op('act', lambda: nc.scalar.activation(out=gtmp[gi][:, :], in_=pl[:, :], func=AF.Sigmoid,
                                                                     bias=bcol), reads=[rpl, R_misc],
                                 writes=[R_gt[gi]])
                            msl = macc[:, cc, tg * 512:(tg + 1) * 512]
                            if n == 0:
                                T.op('dve', lambda: nc.vector.tensor_tensor(out=msl, in0=pbr[:, :], in1=gtmp[gi][:, :],
                                                                             op=ALU.mult), reads=[rpbr, R_gt[gi]],
                                     writes=[R_macc[cc][tg]])
                            else:
                                T.op('dve', lambda: nc.vector.tensor_tensor(out=ptmp[gi][:, :], in0=pbr[:, :],
                                                                             in1=gtmp[gi][:, :], op=ALU.mult),
                                     reads=[rpbr, R_gt[gi]], writes=[R_ptm[gi]])
                                if n == 1:
                                    T.op('dve', lambda: nc.vector.tensor_tensor(out=msl, in0=msl, in1=ptmp[gi][:, :],
                                                                                 op=ALU.add),
                                         reads=[R_ptm[gi], R_macc[cc][tg]], writes=[R_macc[cc][tg]])
                                else:
                                    T.op('dve', lambda: nc.vector.tensor_tensor(
                                        out=mT[:, cc, tg * 512:(tg + 1) * 512], in0=msl, in1=ptmp[gi][:, :], op=ALU.add),
                                        reads=[R_ptm[gi], R_macc[cc][tg]] + R_ys[0][tg * 4:tg * 4 + 4] +
                                        R_ys[1][tg * 4:tg * 4 + 4], writes=[R_mT[tg]])
                    w_done_last()
                w_done()
            T.barrier(engines=('act', 'dve', 'pool', 'sp'))
            for t in range(NT):
                T.dma('sp', ('x', t % 4), X[:, t, :], xd[t * 128:(t + 1) * 128, :], reads=[R_xd[t]], writes=[R_X[t]])
            for half in range(2):
                wo, rwo = w_next(4096)
                wo3 = wo[:, :].rearrange("p (k n) -> p k n", k=8)
                for t in range(NT):
                    pb, rpb = ps_next()
                    for kc in range(8):
                        T.op('pe', lambda: nc.tensor.matmul(pb[:, :], lhsT=mT[:, kc, t * 128:(t + 1) * 128],
                                                            rhs=wo3[:, kc, :], start=(kc == 0), stop=(kc == 7)),
                             reads=[rwo, R_mT[t // 4]], writes=[rpb])
                    xs = X[:, t, half * 512:(half + 1) * 512]
                    T.op('dve', lambda: nc.vector.tensor_tensor(out=xs, in0=xs, in1=pb[:, :], op=ALU.add),
                         reads=[rpb, R_X[t]], writes=[R_X[t]])
                w_done()
            if dbg and l == 0:
                T.barrier()
                for t in range(NT):
                    T.dma('sp', 'dbg', dbg_out['xE'][t * 128:(t + 1) * 128, :], X[:, t, :], reads=[R_X[t]])
                T.barrier()
            if stop == 'E':
                break

            rmsnorm_to_hT(l * 3 + 1, spill=False)
            actT = nc.alloc_sbuf_tensor_at("actT_%d" % l, [128, 8, S], BF16,
                                           offset=YS_ADDR)
            R_sq = [Res(), Res()]
            R_act = [[Res() for _ in range(4)] for _ in range(8)]
            sctr2 = 0
            for s_ in range(4):
                for j2 in range(2):
                    wu, rwu = w_next(4096)
                    wu3 = wu[:, :].rearrange("p (k n) -> p k n", k=8)
                    for j in range(4):
                        fc = j2 * 4 + j
                        for tg in range(4):
                            pb, rpb = ps_next()
                            for kc in range(8):
                                T.op('pe', lambda: nc.tensor.matmul(pb[:, :], lhsT=wu3[:, kc, j * 128:(j + 1) * 128],
                                                                    rhs=hT[:, kc, tg * 512:(tg + 1) * 512],
                                                                    start=(kc == 0), stop=(kc == 7)),
                                     reads=[rwu] + R_hT[tg * 4:tg * 4 + 4], writes=[rpb])
                            si = sctr2 % 2
                            sctr2 += 1
                            T.op('act', lambda: nc.scalar.activation(out=sqf[si][:, :], in_=pb[:, :], func=AF.Square),
                                 reads=[rpb], writes=[R_sq[si]])
                            T.op('dve', lambda: nc.vector.scalar_tensor_tensor(
                                out=actT[:, fc, tg * 512:(tg + 1) * 512], in0=pb[:, :], scalar=0.0, in1=sqf[si][:, :],
                                op0=ALU.is_gt, op1=ALU.mult), reads=[rpb, R_sq[si]], writes=[R_act[fc][tg]] + R_mT)
                    w_done()
                wd = []
                for j2 in range(2):
                    w_, rw_ = w_next(4096)
                    wd.append((w_[:, :].rearrange("p (k n) -> p k n", k=4), rw_))
                for t in range(NT):
                    for half in range(2):
                        pb, rpb = ps_next()
                        for fc in range(8):
                            w3_, rw_ = wd[fc // 4]
                            T.op('pe', lambda: nc.tensor.matmul(pb[:, :], lhsT=actT[:, fc, t * 128:(t + 1) * 128],
                                                                rhs=w3_[:, fc % 4, half * 512:(half + 1) * 512],
                                                                start=(fc == 0), stop=(fc == 7)),
                                 reads=[rw_, R_act[fc][t // 4]], writes=[rpb])
                        xs = X[:, t, half * 512:(half + 1) * 512]
                        T.op('dve', lambda: nc.vector.tensor_tensor(out=xs, in0=xs, in1=pb[:, :], op=ALU.add),
                             reads=[rpb, R_X[t]], writes=[R_X[t]])
                w_done(2)
            if dbg and l == 0:
                T.barrier()
                for t in range(NT):
                    T.dma('sp', 'dbg', dbg_out['xF'][t * 128:(t + 1) * 128, :], X[:, t, :], reads=[R_X[t]])
                T.barrier()
            if stop == 'F':
                break

            rmsnorm_to_hT(l * 3 + 2, spill=False)
            ppT = nc.alloc_sbuf_tensor_at("ppT_%d" % l, [128, 2, S], BF16,
                                          offset=YS_ADDR)
            R_pp = Res()
            for kc in range(2):
                T.dma('pool', 'c_pp', ppT[:, kc, :], pT_in[l, kc * 128:(kc + 1) * 128, :],
                      writes=[R_pp] + R_act[0] + R_act[1])
            wp, rwp = w_next(2048)
            wp3 = wp[:, 0:2048].rearrange("p (k n) -> p k n", k=2)
            gctr = 0
            for half in range(2):
                wgp, rwgp = w_next(4096)
                wgp3 = wgp[:, :].rearrange("p (k n) -> p k n", k=8)
                for t in range(NT):
                    pl, rpl = ps_next()
                    for kc in range(8):
                        T.op('pe', lambda: nc.tensor.matmul(pl[:, :], lhsT=hT[:, kc, t * 128:(t + 1) * 128],
                                                            rhs=wgp3[:, kc, :], start=(kc == 0), stop=(kc == 7)),
                             reads=[rwgp, R_hT[t]], writes=[rpl])
                    pe_, rpe_ = ps_next()
                    for kc in range(2):
                        T.op('pe', lambda: nc.tensor.matmul(pe_[:, :], lhsT=ppT[:, kc, t * 128:(t + 1) * 128],
                                                            rhs=wp3[:, kc, half * 512:(half + 1) * 512],
                                                            start=(kc == 0), stop=(kc == 1)),
                             reads=[rwp, R_pp], writes=[rpe_])
                    gi = gctr % 2
                    gctr += 1
                    T.op('act', lambda: nc.scalar.activation(out=sqf[gi][:, :], in_=pl[:, :], func=AF.Sigmoid),
                         reads=[rpl], writes=[R_sq[gi]])
                    T.op('dve', lambda: nc.vector.tensor_tensor(out=sqf[gi][:, :], in0=pe_[:, :], in1=sqf[gi][:, :],
                                                                 op=ALU.mult), reads=[rpe_, R_sq[gi]],
                         writes=[R_sq[gi]])
                    xs = X[:, t, half * 512:(half + 1) * 512]
                    T.op('dve', lambda: nc.vector.tensor_tensor(out=xs, in0=xs, in1=sqf[gi][:, :], op=ALU.add),
                         reads=[R_sq[gi], R_X[t]], writes=[R_X[t]])
                w_done_last()
            w_done()
            if dbg and l == 0:
                T.barrier()
                for t in range(NT):
                    T.dma('sp', 'dbg', dbg_out['xG'][t * 128:(t + 1) * 128, :], X[:, t, :], reads=[R_X[t]])
                T.barrier()
            if stop == 'G':
                break

        if stop is None:
            R_g = R_gG
            T.dma('sp', 'c_g', gbc[:], gvec_in[DEPTH * 3:DEPTH * 3 + 1, :].partition_broadcast(128), writes=[R_g])
            R_ss = R_ssG
            for t in range(NT):
                T.op('act', lambda: nc.scalar.activation(out=junk[:], in_=X[:, t, :], func=AF.Square,
                                                         accum_out=ssq[:, t:t + 1]), reads=[R_X[t]], writes=[R_ss, R_junk])
            T.op('dve', lambda: nc.vector.tensor_scalar(out=rstd[:], in0=ssq[:], scalar1=1.0 / D, scalar2=EPS,
                                                         op0=ALU.mult, op1=ALU.add), reads=[R_ss], writes=[R_ss])
            T.op('pool', lambda: nc.gpsimd.tensor_tensor(out=rstd[:], in0=rstd[:], in1=mhalf[:, :], op=ALU.pow),
                 reads=[R_ss, R_mh], writes=[R_ss])
            for t in range(NT):
                T.op('dve', lambda: nc.vector.scalar_tensor_tensor(out=X[:, t, :], in0=X[:, t, :],
                                                                    scalar=rstd[:, t:t + 1], in1=gbc[:], op0=ALU.mult,
                                                                    op1=ALU.mult), reads=[R_X[t], R_ss, R_g],
                     writes=[R_X[t]])
                T.dma('sp', ('yo', t % 4), y_out[t * 128:(t + 1) * 128, :], X[:, t, :], reads=[R_X[t]])
        T.barrier()
    return nc


def _find_addr(nc, t):
    raise RuntimeError("cannot determine arena address; attrs=%s" % [a for a in dir(t) if not a.startswith('__')])


def _addr_of(nc, t, arena_addr, arena):
    for attr in ('addr', 'offset', 'address', 'base_addr', 'start_addr'):
        if hasattr(t, attr):
            v = getattr(t, attr)
            v = v() if callable(v) else v
            if isinstance(v, int):
                return v
    raise RuntimeError("no addr attr: %s" % [a for a in dir(t) if not a.startswith('__')])


def _prep_inputs(inp):
    inp = {k: np.asarray(v, dtype=np.float32) for k, v in inp.items()}
    ws = np.stack([np.concatenate(_layer_tiles(l, inp), axis=1) for l in range(DEPTH)], 0)
    cols = np.concatenate([_small_cols(l, inp) for l in range(DEPTH)], 1)
    gv = []
    for l in range(DEPTH):
        gv += [inp['g_mix'][l], inp['g_mlp'][l], inp['g_ple'][l]]
    gv.append(inp['g_final'])
    gvec = np.stack(gv, 0)
    bt = np.stack([_na_bias_table(inp['na_rpb'][l]).reshape(128, -1) for l in range(DEPTH)], 0)
    shared = {
        'ws': np.ascontiguousarray(ws), 'cols': np.ascontiguousarray(cols), 'gvec': np.ascontiguousarray(gvec),
        'gn': np.ascontiguousarray(inp['ret_gn']), 'bt': np.ascontiguousarray(bt),
        'ident': _CONSTS['ident'], 'cos': _CONSTS['cos'].reshape(128, -1), 'sin': _CONSTS['sin'].reshape(128, -1),
        'DT': _CONSTS['DT'].reshape(128, -1), 'QF': _CONSTS['QF'].reshape(128, -1),
        'QB': _CONSTS['QB'].reshape(128, -1), 'KD': _CONSTS['KD'].reshape(128, -1),
    }
    in_maps = []
    for b in range(8):
        m = dict(shared)
        m['x'] = np.ascontiguousarray(inp['x'][b])
        m['pT'] = np.ascontiguousarray(inp['p'][:, b].transpose(0, 2, 1))
        in_maps.append(m)
    return in_maps


def kernel(**inputs):
    in_maps = _prep_inputs(inputs)
    nc = build()
    res = run_bass_kernel_spmd(nc, in_maps, core_ids=list(range(8)))
    return np.stack([np.asarray(r['y'], dtype=np.float32) for r in res.results], 0)
```
